# Optimizing a Trainium2 kernel written in Bass

```python
import jax
import jax.numpy as jnp
from jax import lax
import numpy as np

D_MODEL = 1024
BATCH = 8
SEQ = 4096
DEPTH = 1

EPS = 1e-6
ATTN_GROUPS = ((128, 1), (512, 4), (2048, 16))
N_GROUPS = 3
HEADS_PER_GROUP = 8
HEAD_DIM = 64
ATTN_WIDTH = HEADS_PER_GROUP * HEAD_DIM
N_ATTN_HEADS = N_GROUPS * HEADS_PER_GROUP
ATTN_QKV_COLS = N_GROUPS * 3 * ATTN_WIDTH
NUM_BUCKETS = 32
REL_MAX_DISTANCE = 1024
NEG_INF = -1e30
HG_HEADS = 8
HG_DK = D_MODEL // HG_HEADS
HG_DV = D_MODEL // HG_HEADS
HG_WIDTH = HG_HEADS * HG_DV
HG_CHUNK = 32
SPLIT_SIZES = (ATTN_QKV_COLS, ATTN_WIDTH, HG_WIDTH, HG_WIDTH, HG_WIDTH, HG_WIDTH, HG_WIDTH, D_MODEL, D_MODEL)
IN_COLS = ATTN_QKV_COLS + ATTN_WIDTH + 5 * HG_WIDTH + 2 * D_MODEL

kernel_name = "hybrid_dilated_attn_hgrn2_block"


def rms_norm(t, w):
    t32 = t.astype(jnp.float32)
    return t32 * lax.rsqrt(jnp.mean(t32 * t32, axis=-1, keepdims=True) + EPS) * w.astype(jnp.float32)


def t5_bucket(rel):
    half = NUM_BUCKETS // 2
    max_exact = half // 2
    n = np.abs(rel)
    large = max_exact + (np.log(np.maximum(n, 1) / max_exact)
                         / np.log(REL_MAX_DISTANCE / max_exact) * (half - max_exact)).astype(np.int32)
    large = np.minimum(large, half - 1)
    return np.where(rel > 0, half, 0) + np.where(n < max_exact, n, large)


def dilated_window_attention(q, k, v, bias_table, window, dilation):
    B, S, H, Dh = q.shape
    side = (window // 2) // dilation
    blk = side
    L = S // dilation
    nb = -(-L // blk)
    Lp = nb * blk

    def to_blocks(t):
        t = t.reshape(B, L, dilation, H, Dh)
        t = jnp.pad(t, ((0, 0), (0, Lp - L), (0, 0), (0, 0), (0, 0)))
        return t.reshape(B, nb, blk, dilation, H, Dh)

    def with_halo(t):
        tp = jnp.pad(t, ((0, 0), (1, 1), (0, 0), (0, 0), (0, 0), (0, 0)))
        return jnp.concatenate([tp[:, :-2], tp[:, 1:-1], tp[:, 2:]], axis=2)

    def from_blocks(t):
        t = t.reshape((B, Lp) + t.shape[3:])[:, :L]
        return t.reshape((B, S) + t.shape[3:])

    qb = to_blocks(q) * (HEAD_DIM ** -0.5)
    kh = with_halo(to_blocks(k))
    vh = with_halo(to_blocks(v))

    i_idx = np.arange(blk)[:, None]
    j_idx = np.arange(3 * blk)[None, :]
    rel = j_idx - blk - i_idx
    bias = jnp.transpose(bias_table[t5_bucket(rel * dilation)], (2, 0, 1))
    t_key = np.arange(nb)[:, None, None] * blk - blk + j_idx[None]
    valid = (np.abs(rel)[None] <= side) & (t_key >= 0) & (t_key < L)

    s = jnp.einsum('bnqrhd,bnkrhd->bnrhqk', qb, kh) + bias
    s = jnp.where(valid[None, :, None, None], s, NEG_INF)
    m = jnp.max(s, axis=-1, keepdims=True)
    p = jnp.exp(s - m)
    den = jnp.transpose(jnp.sum(p, axis=-1), (0, 1, 4, 2, 3))
    o = jnp.einsum('bnrhqk,bnkrhd->bnqrhd', p, vh) / den[..., None]
    m = jnp.transpose(m[..., 0], (0, 1, 4, 2, 3))
    return from_blocks(o), from_blocks(m), from_blocks(den)


def gla_chunk_scan(q, k, v, log_f):
    B, T, H, DK = q.shape
    DV = v.shape[-1]
    N = T // HG_CHUNK

    def chunks(t):
        return t.reshape(B, N, HG_CHUNK, H, t.shape[-1]).transpose(1, 0, 3, 2, 4)

    qc, kc, vc, gc = chunks(q), chunks(k), chunks(v), chunks(log_f)
    b = jnp.cumsum(gc, axis=3)
    b_last = b[:, :, :, -1, :]
    q_t = qc * jnp.exp(b)
    k_t = kc * jnp.exp(-b)
    k_s = kc * jnp.exp(b_last[:, :, :, None, :] - b)
    lower = jnp.tril(jnp.ones((HG_CHUNK, HG_CHUNK), dtype=bool))
    a = jnp.where(lower, jnp.einsum('nbhik,nbhjk->nbhij', q_t, k_t), 0.0)
    o_intra = jnp.einsum('nbhij,nbhjv->nbhiv', a, vc)

    def step(state, xs):
        q_n, k_n, v_n, dec_n = xs
        o_n = jnp.einsum('bhik,bhkv->bhiv', q_n, state)
        state = dec_n[..., None] * state + jnp.einsum('bhik,bhiv->bhkv', k_n, v_n)
        return state, o_n

    s0 = jnp.zeros((B, H, DK, DV), jnp.float32)
    _, o_inter = lax.scan(step, s0, (q_t, k_s, vc, jnp.exp(b_last)))
    o = o_intra + o_inter
    return o.transpose(1, 0, 3, 2, 4).reshape(B, T, H, DV)


def hgrn2_scan(q, i, z_f, lb):
    f = lb + (1.0 - lb) * jax.nn.sigmoid(z_f)
    return gla_chunk_scan(q, 1.0 - f, i, jnp.log(f))


def setup_inputs(seed: int = 0) -> dict:
    key = jax.random.key(seed)
    ks = jax.random.split(key, 12)
    f32 = jnp.float32
    nrm = jax.random.normal
    return {
        "x": nrm(ks[0], (BATCH, SEQ, D_MODEL), f32),
        "norm_w": 1.0 + 0.02 * nrm(ks[1], (DEPTH, D_MODEL), f32),
        "w_in": nrm(ks[2], (DEPTH, D_MODEL, IN_COLS), f32) * D_MODEL ** -0.5,
        "q_norm_w": 1.0 + 0.02 * nrm(ks[3], (DEPTH, N_GROUPS, HEAD_DIM), f32),
        "k_norm_w": 1.0 + 0.02 * nrm(ks[4], (DEPTH, N_GROUPS, HEAD_DIM), f32),
        "rel_bias": 0.5 * nrm(ks[5], (NUM_BUCKETS, N_ATTN_HEADS), f32),
        "lb_fwd": 0.1 * nrm(ks[6], (DEPTH + 1, HG_HEADS * HG_DK), f32),
        "lb_bwd": 0.1 * nrm(ks[7], (DEPTH + 1, HG_HEADS * HG_DK), f32),
        "hg_norm_w": 1.0 + 0.02 * nrm(ks[8], (DEPTH, HG_DV), f32),
        "w_proj_a": nrm(ks[9], (DEPTH, ATTN_WIDTH, D_MODEL), f32) * ATTN_WIDTH ** -0.5,
        "w_proj_b": nrm(ks[10], (DEPTH, HG_WIDTH, D_MODEL), f32) * HG_WIDTH ** -0.5,
        "w_out": nrm(ks[11], (DEPTH, D_MODEL, D_MODEL), f32) * D_MODEL ** -0.5,
    }


def reference(x, norm_w, w_in, q_norm_w, k_norm_w, rel_bias, lb_fwd, lb_bwd, hg_norm_w,
              w_proj_a, w_proj_b, w_out):
    f32 = jnp.float32
    B, S, _ = x.shape
    split_points = np.cumsum(SPLIT_SIZES)[:-1].tolist()
    lb_f_all = jnp.cumsum(jax.nn.softmax(lb_fwd.astype(f32), axis=0), axis=0)
    lb_b_all = jnp.cumsum(jax.nn.softmax(lb_bwd.astype(f32), axis=0), axis=0)
    bias_tab = rel_bias.astype(f32)
    h_res = x.astype(f32)
    for layer in range(DEPTH):
        h = rms_norm(h_res, norm_w[layer])
        z = jnp.einsum('bsd,dc->bsc', h, w_in[layer].astype(f32))
        qkv_a, g_a, q_b, zf_fwd, zf_bwd, i_b, g_b, zgate_a, zgate_b = jnp.split(z, split_points, axis=-1)

        qkv_a = qkv_a.reshape(B, S, N_GROUPS, 3, HEADS_PER_GROUP, HEAD_DIM)
        outs, maxes, dens = [], [], []
        for g, (window, dilation) in enumerate(ATTN_GROUPS):
            q = rms_norm(qkv_a[:, :, g, 0], q_norm_w[layer, g])
            k = rms_norm(qkv_a[:, :, g, 1], k_norm_w[layer, g])
            v = qkv_a[:, :, g, 2]
            o_g, m_g, d_g = dilated_window_attention(
                q, k, v, bias_tab[:, g * HEADS_PER_GROUP:(g + 1) * HEADS_PER_GROUP], window, dilation)
            outs.append(o_g)
            maxes.append(m_g)
            dens.append(d_g)
        m_all = jnp.stack(maxes)
        wts = jnp.stack(dens) * jnp.exp(m_all - jnp.max(m_all, axis=0, keepdims=True))
        o_a = jnp.sum(wts[..., None] * jnp.stack(outs), axis=0) / jnp.sum(wts, axis=0)[..., None]
        y_a = jnp.einsum('bsc,cd->bsd', o_a.reshape(B, S, ATTN_WIDTH) * jax.nn.silu(g_a),
                         w_proj_a[layer].astype(f32))

        q_b = q_b.reshape(B, S, HG_HEADS, HG_DK)
        i_b = i_b.reshape(B, S, HG_HEADS, HG_DV)
        zf_fwd = zf_fwd.reshape(B, S, HG_HEADS, HG_DK)
        zf_bwd = zf_bwd.reshape(B, S, HG_HEADS, HG_DK)
        lb_f = lb_f_all[layer].reshape(HG_HEADS, HG_DK)
        lb_b = lb_b_all[layer].reshape(HG_HEADS, HG_DK)
        o_fwd = hgrn2_scan(q_b, i_b, zf_fwd, lb_f)
        o_bwd = jnp.flip(hgrn2_scan(jnp.flip(q_b, 1), jnp.flip(i_b, 1), jnp.flip(zf_bwd, 1), lb_b), 1)
        o_b = rms_norm(o_fwd + o_bwd, hg_norm_w[layer]) * jax.nn.silu(g_b.reshape(B, S, HG_HEADS, HG_DV))
        y_b = jnp.einsum('bsc,cd->bsd', o_b.reshape(B, S, HG_WIDTH), w_proj_b[layer].astype(f32))

        merged = jax.nn.sigmoid(zgate_a) * y_a + jax.nn.sigmoid(zgate_b) * y_b
        h_res = h_res + jnp.einsum('bsd,de->bse', merged, w_out[layer].astype(f32))
    return h_res.astype(x.dtype)
```

```python
import numpy as np
from contextlib import ExitStack
import concourse.bass as bass
import concourse.mybir as mybir
from concourse.bass_utils import run_bass_kernel_spmd

F32 = mybir.dt.float32
BF16 = mybir.dt.bfloat16
U8 = mybir.dt.uint8
AF = mybir.ActivationFunctionType
ALU = mybir.AluOpType

S = 4096
D = 1024
PAD = 1024
EPS = 1e-6
GROUPS = ((128, 1), (512, 4), (2048, 16))
NUM_BUCKETS = 32
REL_MAX_DISTANCE = 1024
IN_COLS = 12288
ENGS = ("pe", "act", "dve", "pool", "sp")
DBG = {}


class KB:
    def __init__(self, nc):
        self.nc = nc
        self.prog = {e: [] for e in ENGS}
        self.count = {}
        self.sems = {}
        self.seen = {e: {} for e in ENGS}
        self.snap = {}
        self.last_w = {}
        self.readers = {}
        self.pending = {e: ([], []) for e in ENGS}
        self._ctx = []
        self.ninstr = 0
        for e in ENGS:
            self._mksem(e)

    def _mksem(self, name):
        cm = self.nc.semaphore("s_" + name)
        h = cm.__enter__()
        self._ctx.append(cm)
        self.sems[name] = h
        self.count[name] = 0
        return h

    def close(self):
        for cm in reversed(self._ctx):
            cm.__exit__(None, None, None)

    def _deps(self, e, reads, writes):
        deps = {}

        def add(cv):
            if cv is None:
                return
            c, v = cv
            if deps.get(c, 0) < v:
                deps[c] = v
        for r in reads:
            add(self.last_w.get(r))
        for w in writes:
            add(self.last_w.get(w))
            for rd in self.readers.get(w, ()):
                add(rd)
        waits = []
        seen = self.seen[e]
        for c, v in deps.items():
            if c == e and e == "pe":
                continue
            if seen.get(c, 0) < v:
                waits.append((c, v))
                seen[c] = v
                sn = self.snap.get((c, v))
                if sn:
                    for c2, v2 in sn.items():
                        if c2 != e and seen.get(c2, 0) < v2:
                            seen[c2] = v2
        return waits

    def _commit(self, token, reads, writes):
        for r in reads:
            self.readers.setdefault(r, []).append(token)
        for w in writes:
            self.last_w[w] = token
            self.readers[w] = []

    def op(self, e, fn, reads=(), writes=(), track=True):
        reads = tuple(reads)
        writes = tuple(writes)
        waits = self._deps(e, reads, writes)
        self.ninstr += 1
        if track:
            self.count[e] += 1
            token = (e, self.count[e])
            pr, pw = self.pending[e]
            self._commit(token, list(reads) + pr, list(writes) + pw)
            self.pending[e] = ([], [])
            self.snap[token] = dict(self.seen[e])
            self.prog[e].append((waits, fn, (e, 1)))
        else:
            pr, pw = self.pending[e]
            pr.extend(reads)
            pw.extend(writes)
            self.prog[e].append((waits, fn, None))

    def dma(self, q, out, in_, reads=(), writes=(), sem=None):
        reads = tuple(reads)
        writes = tuple(writes)
        if sem is None:
            sem = "d_" + str(writes[0] if writes else reads[0])
        if sem not in self.sems:
            self._mksem(sem)
        waits = self._deps(q, reads, writes)
        self.count[sem] += 16
        token = (sem, self.count[sem])
        self._commit(token, reads, writes)
        self.snap[token] = dict(self.seen[q])
        self.ninstr += 1

        def fn(eng, out=out, in_=in_):
            return eng.dma_start(out=out, in_=in_)
        self.prog[q].append((waits, fn, (sem, 16)))
        return token

    def barrier(self):
        for e in ENGS:
            waits = []
            for c, v in self.count.items():
                if v > 0 and self.seen[e].get(c, 0) < v:
                    if c == e and e in ("pe", "sp"):
                        continue
                    waits.append((c, v))
                    self.seen[e][c] = v
            if waits:
                self.prog[e].append((waits, None, None))
        self.last_w = {}
        self.readers = {}

    def finish(self, e="sp"):
        waits = []
        for c, v in self.count.items():
            if v > 0 and self.seen[e].get(c, 0) < v and c != e:
                waits.append((c, v))
        self.prog[e].append((waits, None, None))

    def emit(self):
        nc = self.nc
        with nc.Block() as block:
            for e in ENGS:
                items = self.prog[e]
                if not items:
                    continue

                def body(eng, items=items):
                    for waits, fn, inc in items:
                        for c, v in waits:
                            eng.wait_ge(self.sems[c], v)
                        if fn is None:
                            continue
                        ins = fn(eng)
                        if inc is not None:
                            ins.then_inc(self.sems[inc[0]], inc[1])
                reg = {"pe": block.tensor, "act": block.scalar, "dve": block.vector,
                       "pool": block.gpsimd, "sp": block.sync}[e]
                reg(body)


class Arena:
    def __init__(self, ap, nbytes):
        self.ap = ap
        self.nbytes = nbytes
        self.off = 0
        self.peak = 0

    def alloc(self, shape, dt):
        n = int(np.prod(shape[1:]))
        nb = n * (4 if dt == F32 else 2)
        nb = (nb + 63) // 64 * 64
        assert self.off + nb <= self.nbytes, ("SBUF arena overflow", self.off, nb)
        a = self.ap[:, self.off:self.off + nb].bitcast(dt)[:, 0:n]
        self.off += nb
        self.peak = max(self.peak, self.off)
        if len(shape) == 3:
            a = a.rearrange("p (a b) -> p a b", b=shape[2])
        return a

    def mark(self):
        return self.off

    def reset(self, m):
        self.off = m


def t5_bucket(rel):
    half = NUM_BUCKETS // 2
    max_exact = half // 2
    n = np.abs(rel)
    large = max_exact + (np.log(np.maximum(n, 1) / max_exact)
                         / np.log(REL_MAX_DISTANCE / max_exact) * (half - max_exact)).astype(np.int32)
    large = np.minimum(large, half - 1)
    return np.where(rel > 0, half, 0) + np.where(n < max_exact, n, large)


def build_program(debug=False, phases=(1, 2, 3, 4)):
    nc = bass.Bass("TRN2", target_bir_lowering=False)
    x_d = nc.dram_tensor("x", [S, D], F32, kind="ExternalInput").ap()
    win_d = nc.dram_tensor("w_in", [D, IN_COLS], F32, kind="ExternalInput").ap()
    normw_d = nc.dram_tensor("norm_w", [1, D], F32, kind="ExternalInput").ap()
    qkw_d = nc.dram_tensor("qkw", [128, 6], F32, kind="ExternalInput").ap()
    bias_d = nc.dram_tensor("biasT", [128, 3 * 8 * 2 * 128], F32, kind="ExternalInput").ap()
    lbf_d = nc.dram_tensor("lbf", [128, 16], F32, kind="ExternalInput").ap()
    lbb_d = nc.dram_tensor("lbb", [128, 16], F32, kind="ExternalInput").ap()
    hgw_d = nc.dram_tensor("hgw", [128, 1], F32, kind="ExternalInput").ap()
    wpa_d = nc.dram_tensor("w_proj_a", [512, D], F32, kind="ExternalInput").ap()
    wpb_d = nc.dram_tensor("w_proj_b", [D, D], F32, kind="ExternalInput").ap()
    wo_d = nc.dram_tensor("w_out", [D, D], F32, kind="ExternalInput").ap()
    out_d = nc.dram_tensor("out", [S, D], F32, kind="ExternalOutput").ap()
    skind = "ExternalOutput" if debug else "Internal"
    oa_d = nc.dram_tensor("oa_scr", [512, S], BF16, kind=skind).ap()
    ob_d = nc.dram_tensor("ob_scr", [D, S], BF16, kind=skind).ap()
    if debug:
        hT_dbg = nc.dram_tensor("hT_dbg", [128, 8 * S], BF16, kind="ExternalOutput").ap()

    win_v = win_d.rearrange("(c p) n -> p c n", p=128)

    es = ExitStack()
    ARENA_BYTES = 209920
    arena_t = es.enter_context(nc.sbuf_tensor("arena", [128, ARENA_BYTES], U8))
    A = Arena(arena_t, ARENA_BYTES)
    P = [es.enter_context(nc.psum_tensor("ps%d" % i, [128, 512], F32)) for i in range(7)]
    PT = es.enter_context(nc.psum_tensor("pst", [128, 1024], BF16))
    k = KB(nc)

    def mm(out, lhsT, rhs, start, stop, reads, writes, track=True, tp=None):
        def fn(e):
            if tp is None:
                return e.matmul(out, lhsT=lhsT, rhs=rhs, start=start, stop=stop)
            return e.matmul(out, lhsT=lhsT, rhs=rhs, start=start, stop=stop, tile_position=tp)
        k.op("pe", fn, reads, writes, track)

    def tr(out, in_, reads, writes, track=True):
        k.op("pe", lambda e: e.transpose(out, in_, ident), tuple(reads) + ("const",), writes, track)

    def act(out, in_, func, reads, writes, scale=None, bias=None, accum=None):
        kw = {}
        if scale is not None:
            kw["scale"] = scale
        if bias is not None:
            kw["bias"] = bias
        if accum is not None:
            kw["accum_out"] = accum
        k.op("act", lambda e: e.activation(out=out, in_=in_, func=func, **kw), reads, writes)

    def tt(eng, out, in0, in1, op, reads, writes):
        k.op(eng, lambda e: e.tensor_tensor(out=out, in0=in0, in1=in1, op=op), reads, writes)

    def ts(eng, out, in0, s1, op0, reads, writes, s2=None, op1=None):
        if op1 is None:
            k.op(eng, lambda e: e.tensor_scalar(out=out, in0=in0, scalar1=s1, scalar2=None, op0=op0), reads, writes)
        else:
            k.op(eng, lambda e: e.tensor_scalar(out=out, in0=in0, scalar1=s1, scalar2=s2, op0=op0, op1=op1), reads, writes)

    def stt(out, in0, scalar, in1, op0, op1, reads, writes):
        k.op("dve", lambda e: e.scalar_tensor_tensor(out=out, in0=in0, scalar=scalar, in1=in1, op0=op0, op1=op1),
             reads, writes)

    def cp(eng, out, in_, reads, writes):
        if eng == "act":
            k.op("act", lambda e: e.copy(out=out, in_=in_), reads, writes)
        else:
            k.op(eng, lambda e: e.tensor_copy(out=out, in_=in_), reads, writes)

    def recip(out, in_, reads, writes):
        k.op("dve", lambda e: e.reciprocal(out=out, in_=in_), reads, writes)

    def memset(eng, ap, val, writes):
        k.op(eng, lambda e: e.memset(ap, val), (), writes)

    hTf = A.alloc([128, PAD + 8 * S + PAD], BF16)
    ident = A.alloc([128, 128], BF16)
    ones_bd = A.alloc([128, 128], BF16)
    ones128 = A.alloc([128, 128], BF16)
    tri4 = A.alloc([128, 3, 128], F32)
    scanmask = A.alloc([128, 512], F32)
    qkw = A.alloc([128, 6], F32)
    lbt = A.alloc([128, 32], F32)
    lbd = A.alloc([128, 16], F32)
    oml = A.alloc([128, 16], F32)
    hgw = A.alloc([128, 1], F32)
    persist_mark = A.mark()

    def hT_nat(c, t0, n):
        o = PAD + c * S + t0
        return hTf[:, o:o + n]

    def hT_perm(c, d, n0, cnt):
        L = S // d
        r0 = n0 // L
        l0 = n0 % L
        base = PAD + c * S
        if d == 1:
            return hTf[:, base + n0: base + n0 + cnt], None
        if cnt <= L - l0:
            st = base + l0 * d + r0
            return hTf[:, st: st + (cnt - 1) * d + 1: d], None
        assert l0 == 0 and cnt % L == 0
        nres = cnt // L
        st = base + r0
        v = hTf[:, st: st + L * d].rearrange("p (l r) -> p r l", r=d)[:, 0:nres, :]
        return v, L

    def hT_vtile(c, d, r, i):
        st = PAD + c * S + (128 * i - 64) * d + r
        if d == 1:
            return hTf[:, st: st + 128]
        return hTf[:, st: st + 127 * d + 1: d]

    def acc_perm(acc, d, n0, cnt):
        L = S // d
        r0 = n0 // L
        l0 = n0 % L
        if d == 1:
            return acc[:, n0:n0 + cnt], None
        if cnt <= L - l0:
            st = l0 * d + r0
            return acc[:, st: st + (cnt - 1) * d + 1: d], None
        assert l0 == 0 and cnt % L == 0
        nres = cnt // L
        v = acc.rearrange("p (l r) -> p r l", r=d)[:, r0:r0 + nres, :]
        return v, L

    memset("pool", hTf[:, 0:PAD], 0.0, ["hTpad"])
    memset("pool", hTf[:, PAD + 8 * S: PAD + 8 * S + PAD], 0.0, ["hTpad"])
    memset("pool", ident, 0.0, ["const"])
    k.op("pool", lambda e: e.affine_select(out=ident, in_=ident, pattern=[[-1, 128]], compare_op=ALU.not_equal,
                                           fill=1.0, base=0, channel_multiplier=1), ["const"], ["const"])
    memset("pool", ones_bd, 0.0, ["const"])
    memset("pool", ones_bd[0:64, 0:64], 1.0, ["const"])
    memset("pool", ones_bd[64:128, 64:128], 1.0, ["const"])
    memset("pool", ones128, 1.0, ["const"])
    memset("pool", tri4, 1.0, ["const"])
    for idx in (0, 2):
        k.op("pool", lambda e, idx=idx: e.affine_select(out=tri4[:, idx, :], in_=tri4[:, idx, :], pattern=[[-1, 128]],
                                                        compare_op=ALU.is_ge, fill=0.0, base=0, channel_multiplier=1),
             ["const"], ["const"])
    k.op("pool", lambda e: e.affine_select(out=tri4[:, 1, :], in_=tri4[:, 1, :], pattern=[[1, 128]],
                                           compare_op=ALU.is_ge, fill=0.0, base=0, channel_multiplier=-1),
         ["const"], ["const"])
    memset("pool", scanmask, 1.0, ["const"])
    memset("pool", scanmask.rearrange("p (c t) -> p c t", t=128)[:, :, 0:1], 0.0, ["const"])
    k.dma("sp", qkw, qkw_d, writes=["qkw"])
    k.dma("sp", lbt[:, 0:16], lbf_d, writes=["lbt"], sem="d_lbt")
    k.dma("sp", lbt[:, 16:32], lbb_d, writes=["lbt"], sem="d_lbt")
    k.dma("sp", hgw, hgw_d, writes=["hgw"])
    for g in range(3):
        ts("dve", qkw[:, 2 * g + 1:2 * g + 2], qkw[:, 2 * g + 1:2 * g + 2], 8.0, ALU.mult, ["qkw"], ["qkw"])
    lbv = lbt.rearrange("p (d l h) -> p d l h", d=2, l=2)
    for dd in range(2):
        tt("dve", lbd[:, dd * 8:(dd + 1) * 8], lbv[:, dd, 0, :], lbv[:, dd, 1, :], ALU.subtract, ["lbt"], ["lbd"])
    act(oml, lbd, AF.Sigmoid, ["lbd"], ["oml"], scale=-1.0)

    if 1 in phases:
        m0 = A.mark()
        normw_rep = A.alloc([128, D], F32)
        xts = [A.alloc([128, D], F32) for _ in range(2)]
        xn = [A.alloc([128, D], BF16) for _ in range(2)]
        junk = A.alloc([128, D], BF16)
        ssq = A.alloc([128, 32], F32)
        rstd = A.alloc([128, 32], F32)
        k.dma("sp", normw_rep, normw_d.partition_broadcast(128), writes=["normw"])
        memset("pool", ssq, 0.0, [("ssq", t) for t in range(32)])
        hT_ct = hTf[:, PAD:PAD + 8 * S].rearrange("p (c t) -> p c t", t=S)
        for t in range(32):
            sl = t % 2
            k.dma("sp" if sl == 0 else "act", xts[sl], x_d[t * 128:(t + 1) * 128, :], writes=[("xt", sl)])
            act(junk, xts[sl], AF.Square, [("xt", sl)], ["junk", ("ssq", t)], accum=ssq[:, t:t + 1])
            ts("dve", rstd[:, t:t + 1], ssq[:, t:t + 1], 1.0 / D, ALU.mult, [("ssq", t)], [("rstd", t)], s2=EPS, op1=ALU.add)
            act(rstd[:, t:t + 1], rstd[:, t:t + 1], AF.Sqrt, [("rstd", t)], [("rstd", t)])
            recip(rstd[:, t:t + 1], rstd[:, t:t + 1], [("rstd", t)], [("rstd", t)])
            stt(xn[sl], xts[sl], rstd[:, t:t + 1], normw_rep, ALU.mult, ALU.mult,
                [("xt", sl), ("rstd", t), "normw"], [("xn", sl)])
            for c in range(8):
                tr(PT[:, c * 128:(c + 1) * 128], xn[sl][:, c * 128:(c + 1) * 128], [("xn", sl)], ["PT"], track=(c == 7))
            cp("act" if t % 2 == 0 else "dve", hT_ct[:, :, t * 128:(t + 1) * 128],
               PT[:, :].rearrange("p (c t) -> p c t", t=128), ["PT"], [("hT", t)])
        k.barrier()
        A.reset(m0)
        if debug:
            k.dma("sp", hT_dbg, hTf[:, PAD:PAD + 8 * S], reads=["x"], sem="dbg")

    if 2 in phases:
        m0 = A.mark()
        wA = [A.alloc([128, 8, 384], BF16) for _ in range(2)]
        wG = A.alloc([128, 8, 512], BF16)
        qn = A.alloc([128, S], BF16)
        knp = A.alloc([128, 64 + S + 64], BF16)
        vTp = A.alloc([128, 64 + S + 64], BF16)
        vP = A.alloc([128, 33 * 256], BF16)
        acc = [A.alloc([128, S], F32) for _ in range(2)]
        bM = A.alloc([128, 512], F32)
        Mrep = A.alloc([128, 1024], F32)
        sq = [[A.alloc([128, 512], BF16) for _ in range(2)] for _ in range(2)]
        lnb = [A.alloc([128, 512], F32) for _ in range(2)]
        ee = [A.alloc([128, 512], F32) for _ in range(3)]
        pT = [A.alloc([128, 512], BF16) for _ in range(3)]
        sgh = [A.alloc([128, 512], F32) for _ in range(2)]
        rD = [A.alloc([128, 512], F32) for _ in range(2)]
        t1 = [A.alloc([128, 512], F32) for _ in range(2)]
        oaS = [A.alloc([128, 512], BF16) for _ in range(4)]
        vP4 = vP.rearrange("p (t h d) -> p t h d", h=2, d=128)
        memset("pool", knp[:, 0:64], 0.0, [("kn", -1)])
        memset("pool", knp[:, 64 + S:64 + S + 64], 0.0, [("kn", 8)])
        memset("pool", vTp[:, 0:64], 0.0, [("vT", -1)])
        memset("pool", vTp[:, 64 + S:64 + S + 64], 0.0, [("vT", 8)])
        memset("pool", vP4[:, :, :, 64:128], 1.0, ["vPones"])
        for seg in range(4):
            k.dma("pool", wG[:, :, seg * 128:(seg + 1) * 128], win_v[:, :, 4608 + seg * 128: 4608 + (seg + 1) * 128],
                  writes=["wG"], sem="d_wG")
        maskAB = tri4[:, 0:2, :]
        units = [(hp, g) for hp in range(DBG.get("nhp", 4)) for g in range(DBG.get("ng", 3))]

        def load_unit_w(u):
            hp_, g_ = units[u]
            slot_ = u % 2
            for seg in range(3):
                col0 = g_ * 1536 + seg * 512 + hp_ * 128
                k.dma("pool", wA[slot_][:, :, seg * 128:(seg + 1) * 128], win_v[:, :, col0:col0 + 128],
                      writes=[("wA", slot_)], sem="d_wA%d" % slot_)
        load_unit_w(0)
        sccnt = 0
        oacnt = 0
        for u, (hp, g) in enumerate(units):
            win_, d = GROUPS[g]
            L = S // d
            TPS = L // 128
            slot = u % 2
            w = wA[slot]
            if u + 1 < len(units):
                load_unit_w(u + 1)
            bcol = (g * 8 + 2 * hp) * 256
            k.dma("sp", bM, bias_d[:, bcol:bcol + 512], writes=["bM"])
            act(bM, bM, AF.Exp, ["bM"], ["bM"])
            bMv = bM.rearrange("p (h a c) -> p h a c", h=2, a=2)
            Mv = Mrep.rearrange("p (h m a c) -> p h m a c", h=2, m=2, a=2)
            for hl in range(2):
                for mi in range(2):
                    tt("pool", Mv[:, hl, mi], bMv[:, hl], maskAB, ALU.mult, ["bM", "const"], ["Mrep"])

            nl = 512 // d

            def pview(ap_nat):
                if d == 1:
                    return ap_nat
                return ap_nat.rearrange("p (l r) -> p r l", r=d)

            def dview(buf, off, j):
                if d == 1:
                    return buf[:, off + 512 * j: off + 512 * (j + 1)]
                return buf[:, off:off + S].rearrange("p (r l) -> p r l", l=L)[:, :, nl * j: nl * (j + 1)]

            def wkeys(nm, j):
                return [(nm, j)] if d == 1 else [(nm, jj) for jj in range(8)]

            def proj_front(j):
                pqb, pkb, pvb = j % 2, 2 + j % 2, 4
                for (pb, woff) in ((pqb, 0), (pkb, 128), (pvb, 256)):
                    for c in range(8):
                        mm(P[pb][:, :], w[:, c, woff:woff + 128], hT_nat(c, 512 * j, 512), c == 0, c == 7,
                           [("wA", slot), "hT"], ["ps%d" % pb], track=(c == 7))
                act(sq[0][j % 2], P[pqb][:, :], AF.Square, ["ps%d" % pqb], [("sq", 0, j % 2)])
                act(sq[1][j % 2], P[pkb][:, :], AF.Square, ["ps%d" % pkb], [("sq", 1, j % 2)])
                cp("dve", dview(vTp, 64, j), pview(P[pvb][:, :]), ["ps%d" % pvb], wkeys("vT", j))

            def proj_back(j):
                pqb, pkb = j % 2, 2 + j % 2
                for qi, (pb, wcol, dst, dkeys) in enumerate((
                        (pqb, 2 * g, dview(qn, 0, j), wkeys("qn", j)),
                        (pkb, 2 * g + 1, dview(knp, 64, j), wkeys("kn", j)))):
                    pss = P[5 + qi]
                    pssk = "ps%d" % (5 + qi)
                    mm(pss[:, :], ones_bd, sq[qi][j % 2], True, True, [("sq", qi, j % 2), "const"], [pssk])
                    act(lnb[qi], pss[:, :], AF.Ln, [pssk], [("lnb", qi)], bias=64.0 * EPS)
                    act(lnb[qi], lnb[qi], AF.Exp, [("lnb", qi)], [("lnb", qi)], scale=-0.5)
                    stt(dst, pview(P[pb][:, :]), qkw[:, wcol:wcol + 1], pview(lnb[qi]), ALU.mult, ALU.mult,
                        ["ps%d" % pb, ("lnb", qi), "qkw"], dkeys)
            if "q" in DBG.get("st", "qvaf"):
                proj_front(0)
                for j in range(8):
                    if j + 1 < 8:
                        proj_front(j + 1)
                    proj_back(j)
                for t0_ in range(0, 33, 8):
                    nb = min(8, 33 - t0_)
                    for s_ in range(nb):
                        vt = t0_ + s_
                        lo = (128 * vt - 64) // 512 if vt > 0 else -1
                        hi = min((128 * vt + 63) // 512, 8)
                        tr(PT[:, s_ * 128:(s_ + 1) * 128], vTp[:, 128 * vt:128 * vt + 128],
                           [("vT", lo), ("vT", hi)], ["PT"], track=(s_ == nb - 1))
                    cp("act", vP4[:, t0_:t0_ + nb, :, 0:64],
                       PT[:, 0:nb * 128].rearrange("p (t h d) -> p t h d", h=2, d=64), ["PT"], [("vP", t0_ // 8)])

            its = [(mg, mp, hl) for mg in range(8) for mp in range(2) for hl in range(2)]
            state = {}

            def att_front(ii):
                nonlocal sccnt
                mg, mp, hl = its[ii]
                rows = slice(hl * 64, hl * 64 + 64)
                sb_ = sccnt % 3
                sl2 = sccnt % 3
                sccnt += 1
                state[ii] = (sb_, sl2)
                psc = P[sb_]
                psck = "ps%d" % sb_
                for mi in range(2):
                    m = mg * 4 + mp * 2 + mi
                    jq = m // 4
                    for ab in range(2):
                        kst = 128 * m + 128 * ab
                        kj = [("kn", (kst - 64) // 512 if kst >= 64 else -1), ("kn", min((kst + 127 - 64) // 512, 8))]
                        mm(psc[:, (mi * 2 + ab) * 128:(mi * 2 + ab + 1) * 128],
                           knp[rows, kst:kst + 128], qn[rows, 128 * m:128 * m + 128], True, True,
                           [("qn", jq)] + kj, [psck], track=(mi == 1 and ab == 1), tp=(hl * 64, 0))
                act(ee[sl2], psc[:, :], AF.Exp, [psck], [("ee", sl2)])
                tt("dve" if ii % 2 == 0 else "pool", pT[sl2], ee[sl2], Mrep[:, hl * 512:(hl + 1) * 512], ALU.mult,
                   [("ee", sl2), "Mrep"], [("pT", sl2)])

            def att_back(ii):
                mg, mp, hl = its[ii]
                sb_, sl2 = state[ii]
                pb = 3 + (mg % 2) * 2 + hl
                pnd = P[pb]
                pndk = "ps%d" % pb
                for mi in range(2):
                    m = mg * 4 + mp * 2 + mi
                    n0 = 128 * m
                    i = (n0 % L) // 128
                    first = (i == 0)
                    last = (i == TPS - 1)
                    vA = m
                    vB = m + 1
                    col = (mp * 2 + mi) * 128
                    KA = slice(64, 128) if first else slice(0, 128)
                    KBs = slice(0, 64) if last else slice(0, 128)
                    vkeys = [("vP", vA // 8), ("vP", vB // 8), "vPones"]
                    mm(pnd[:, col:col + 128], vP4[KA, vA, hl, :],
                       pT[sl2][KA, (mi * 2) * 128:(mi * 2 + 1) * 128], True, False,
                       [("pT", sl2)] + vkeys, [pndk], track=False, tp=(KA.start, 0))
                    mm(pnd[:, col:col + 128], vP4[KBs, vB, hl, :],
                       pT[sl2][KBs, (mi * 2 + 1) * 128:(mi * 2 + 2) * 128], False, True,
                       [("pT", sl2)] + vkeys, [pndk], track=(mi == 1), tp=(0, 0))
                if mp == 1:
                    av, inner = acc_perm(acc[hl], d, 512 * mg, 512)
                    pv_ = pnd[:, :] if inner is None else pnd[:, :].rearrange("p (a b) -> p a b", b=inner)
                    akey = ("acc", hl)
                    if g == 0:
                        cp("act", av, pv_, [pndk], [akey])
                    else:
                        tt("dve", av, av, pv_, ALU.add, [pndk, akey], [akey])
            if "a" in DBG.get("st", "qvaf"):
                LA = 2
                for ii in range(min(LA, len(its))):
                    att_front(ii)
                for ii in range(len(its)):
                    if ii + LA < len(its):
                        att_front(ii + LA)
                    att_back(ii)

            if g == len([1 for (hp_, g_) in units if hp_ == hp]) - 1 and "f" in DBG.get("st", "qvaf"):
                for j in range(8):
                    pg = P[j % 3]
                    pgk = "ps%d" % (j % 3)
                    for c in range(8):
                        mm(pg[:, :], wG[:, c, hp * 128:(hp + 1) * 128], hT_nat(c, 512 * j, 512), c == 0, c == 7,
                           ["wG", "hT"], [pgk], track=(c == 7))
                    for hl in range(2):
                        act(sgh[hl][0:64, :], pg[hl * 64:(hl + 1) * 64, :], AF.Silu, [pgk], [("sgh", hl)])
                        act(rD[hl][0:64, :], acc[hl][64:128, 512 * j:512 * (j + 1)], AF.Ln, [("acc", hl)], [("rD", hl)])
                        act(rD[hl][0:64, :], rD[hl][0:64, :], AF.Exp, [("rD", hl)], [("rD", hl)], scale=-1.0)
                        tt("pool", t1[hl][0:64, :], acc[hl][0:64, 512 * j:512 * (j + 1)], rD[hl][0:64, :], ALU.mult,
                           [("acc", hl), ("rD", hl)], [("t1", hl)])
                        os_ = oacnt % 4
                        oacnt += 1
                        tt("dve", oaS[os_][0:64, :], t1[hl][0:64, :], sgh[hl][0:64, :], ALU.mult,
                           [("t1", hl), ("sgh", hl)], [("oaS", os_)])
                        k.dma("sp", oa_d[(2 * hp + hl) * 64:(2 * hp + hl + 1) * 64, 512 * j:512 * (j + 1)], oaS[os_][0:64, :],
                              reads=[("oaS", os_)], sem="d_oaS%d" % os_)
        k.barrier()
        A.reset(m0)

    if 3 in phases:
        m0 = A.mark()
        wH = A.alloc([128, 8, 640], BF16)
        vH = A.alloc([128, S], BF16)
        gbT = A.alloc([128, S], BF16)
        qtT = [A.alloc([128, S], BF16) for _ in range(2)]
        ktT = [A.alloc([128, S], BF16) for _ in range(2)]
        ktok = [A.alloc([128, S], BF16) for _ in range(2)]
        dec = [A.alloc([128, 32], F32) for _ in range(2)]
        Ubf = [A.alloc([128, S], BF16) for _ in range(2)]
        iT = Ubf[0]
        Wst = [[A.alloc([128, 128], F32) for _ in range(2)] for _ in range(2)]
        su = [A.alloc([128, 512], F32) for _ in range(2)]
        sL1 = [A.alloc([128, 512], F32) for _ in range(2)]
        sg_ = [A.alloc([128, 512], F32) for _ in range(2)]
        sb_ = [A.alloc([128, 512], F32) for _ in range(2)]
        sr = [A.alloc([128, 512], F32) for _ in range(2)]
        se = [A.alloc([128, 512], F32) for _ in range(2)]
        AT = [A.alloc([128, 256], BF16) for _ in range(2)]
        sqh = A.alloc([128, 512], BF16)
        lnh = A.alloc([128, 512], F32)
        on = A.alloc([128, 512], F32)
        obS = [A.alloc([128, 512], BF16) for _ in range(2)]
        lbv2 = A.alloc([128, 16], F32)
        act(lbv2, lbd, AF.Sigmoid, ["lbd"], ["lbv2"])
        triH = tri4[:, 1:3, :]
        for h in range(DBG.get("nh", 8)):
            w = wH
            for seg in range(5):
                col0 = 5120 + seg * 1024 + h * 128
                k.dma("pool", w[:, :, seg * 128:(seg + 1) * 128], win_v[:, :, col0:col0 + 128],
                      writes=["wH"], sem="d_wH")
            for j in range(8):
                cols = slice(512 * j, 512 * (j + 1))
                pb = j % 2
                for c in range(8):
                    mm(P[pb][:, :], w[:, c, 512:640], hT_nat(c, 512 * j, 512), c == 0, c == 7,
                       ["wH", "hT"], ["ps%d" % pb], track=(c == 7))
                act(gbT[:, cols], P[pb][:, :], AF.Silu, ["ps%d" % pb], [("gbT", j)])
            for j in range(8):
                cols = slice(512 * j, 512 * (j + 1))
                pb = 2 + j % 2
                for c in range(8):
                    mm(P[pb][:, :], w[:, c, 384:512], hT_nat(c, 512 * j, 512), c == 0, c == 7,
                       ["wH", "hT"], ["ps%d" % pb], track=(c == 7))
                cp("dve", iT[:, cols], P[pb][:, :], ["ps%d" % pb], [("iT", j)])
            for t0_ in range(0, 32, 8):
                for s_ in range(8):
                    t = t0_ + s_
                    tr(PT[:, s_ * 128:(s_ + 1) * 128], iT[:, t * 128:(t + 1) * 128], [("iT", t // 4)], ["PT"], track=(s_ == 7))
                cp("act" if (t0_ // 8) % 2 == 0 else "dve", vH[:, t0_ * 128:(t0_ + 8) * 128], PT[:, :], ["PT"], [("vH", t0_ // 8)])

            def gate_front(j):
                for (pi_, woff) in ((0, 0), (1, 128), (2, 256)):
                    pb = pi_ * 2 + j % 2
                    for c in range(8):
                        mm(P[pb][:, :], w[:, c, woff:woff + 128], hT_nat(c, 512 * j, 512), c == 0, c == 7,
                           ["wH", "hT"], ["ps%d" % pb], track=(c == 7))
                cols = slice(512 * j, 512 * (j + 1))
                pqk = "ps%d" % (j % 2)
                pq = P[j % 2]
                for dr in range(2):
                    pzb_ = (1 + dr) * 2 + j % 2
                    pz = P[pzb_]
                    pzk = "ps%d" % pzb_
                    K_ = lambda nm: (nm, dr)
                    act(su[dr], pz[:, :], AF.Exp, [pzk], [K_("u")])
                    act(sL1[dr], su[dr], AF.Ln, [K_("u")], [K_("L1")], bias=1.0)
                    act(sg_[dr], su[dr], AF.Ln, [K_("u")], [K_("g")], bias=lbv2[:, dr * 8 + h: dr * 8 + h + 1])
                    tt("pool", sg_[dr], sg_[dr], sL1[dr], ALU.subtract, [K_("g"), K_("L1")], [K_("g")])
                    k.op("dve", lambda e, o=sb_[dr], g_=sg_[dr]: e.tensor_tensor_scan(out=o, data0=scanmask, data1=g_, initial=0.0,
                                                                                op0=ALU.mult, op1=ALU.add),
                         [K_("g"), "const"], [K_("b")])
                    if dr == 0:
                        src = sb_[dr]
                        srck = K_("b")
                        dcol = slice(127, 512, 128)
                    else:
                        tt("pool", sr[dr], sg_[dr], sb_[dr], ALU.subtract, [K_("g"), K_("b")], [K_("r")])
                        bbv = sb_[dr].rearrange("p (c t) -> p c t", t=128)
                        rrv = sr[dr].rearrange("p (c t) -> p c t", t=128)
                        tt("dve", rrv, rrv, bbv[:, :, 127:128].to_broadcast([128, 4, 128]), ALU.add, [K_("r"), K_("b")], [K_("r")])
                        src = sr[dr]
                        srck = K_("r")
                        dcol = slice(0, 512, 128)
                    act(se[dr], src, AF.Exp, [srck], [K_("e")])
                    tt("dve", qtT[dr][:, cols], pq[:, :], se[dr], ALU.mult, [pqk, K_("e")], [("qtT", dr, j)])
                    cp("pool", dec[dr][:, 4 * j:4 * j + 4], se[dr][:, dcol], [K_("e")], [("dec", dr)])
                    tt("pool", sL1[dr], sL1[dr], src, ALU.add, [K_("L1"), srck], [K_("L1")])
                    act(su[dr], sL1[dr], AF.Exp, [K_("L1")], [K_("u")], scale=-1.0)
                    ts("dve", ktT[dr][:, cols], su[dr], oml[:, dr * 8 + h: dr * 8 + h + 1], ALU.mult,
                       [K_("u"), "oml"], [("ktT", dr, j)])

            def gate_back(j):
                cols = slice(512 * j, 512 * (j + 1))
                for dr in range(2):
                    for cc in range(4):
                        ch = 4 * j + cc
                        tr(PT[:, (dr * 4 + cc) * 128:(dr * 4 + cc + 1) * 128], ktT[dr][:, ch * 128:(ch + 1) * 128],
                           [("ktT", dr, j)], ["PT"], track=(cc == 3 and dr == 1))
                cp("act", ktok[0][:, cols], PT[:, 0:512], ["PT"], [("ktok", 0, j)])
                cp("act", ktok[1][:, cols], PT[:, 512:1024], ["PT"], [("ktok", 1, j)])
            gate_front(0)
            for j in range(8):
                if j + 1 < 8:
                    gate_front(j + 1)
                gate_back(j)
            k.op("pool", lambda e: e.memset(Ubf[0][:, 0:128], 0.0), [("vH", 0)], [("U", 0, 0)])
            memset("pool", Ubf[1][:, 31 * 128:32 * 128], 0.0, [("U", 1, 31)])
            pslot = 0
            for step in range(32):
                for dr in range(2):
                    c = step if dr == 0 else 31 - step
                    pb = pslot % 4
                    pslot += 1
                    pP = P[pb][:, 0:128]
                    ppk = "ps%d" % pb
                    mm(pP, ktok[dr][:, c * 128:(c + 1) * 128], vH[:, c * 128:(c + 1) * 128], True, True,
                       [("ktok", dr, c // 4), ("vH", c // 8)], [ppk])
                    Wn = Wst[dr][step % 2]
                    Wo = Wst[dr][(step + 1) % 2]
                    if step == 0:
                        cp("dve", Wn, pP, [ppk], [("W", dr, step % 2)])
                    else:
                        cprev = c - 1 if dr == 0 else c + 1
                        dcl = dec[dr][:, cprev:cprev + 1]
                        k.op("act", lambda e, o=Ubf[dr][:, c * 128:(c + 1) * 128], i_=Wo, s_=dcl:
                             e.activation(out=o, in_=i_, func=AF.Copy, scale=s_),
                             [("W", dr, (step + 1) % 2), ("dec", dr)], [("U", dr, c)])
                        if step < 31:
                            stt(Wn, Wo, dcl, pP, ALU.mult, ALU.add,
                                [("W", dr, (step + 1) % 2), ("dec", dr), ppk], [("W", dr, step % 2)])
            def out_front(c):
                j = c // 4
                ch = slice(c * 128, (c + 1) * 128)
                pA = P[2 + c % 2][:, 0:256]
                pAk = "ps%d" % (2 + c % 2)
                for dr in range(2):
                    mm(pA[:, dr * 128:(dr + 1) * 128], ktT[dr][:, ch], qtT[dr][:, ch], True, True,
                       [("ktT", dr, j), ("qtT", dr, j)], [pAk], track=(dr == 1))
                tt("dve", AT[c % 2].rearrange("p (a b) -> p a b", b=128), pA.rearrange("p (a b) -> p a b", b=128),
                   triH, ALU.mult, [pAk, "const"], [("AT", c % 2)])

            def out_back(c):
                j = c // 4
                cc = c % 4
                ch = slice(c * 128, (c + 1) * 128)
                pob = 4 + j % 2
                po = P[pob]
                pok = "ps%d" % pob
                oc = po[:, cc * 128:(cc + 1) * 128]
                mm(oc, vH[:, ch], AT[c % 2][:, 0:128], True, False, [("vH", c // 8), ("AT", c % 2)], [pok], track=False)
                mm(oc, vH[:, ch], AT[c % 2][:, 128:256], False, False, [("vH", c // 8), ("AT", c % 2)], [pok], track=False)
                mm(oc, Ubf[0][:, ch], qtT[0][:, ch], False, False, [("U", 0, c), ("qtT", 0, j)], [pok], track=False)
                mm(oc, Ubf[1][:, ch], qtT[1][:, ch], False, True, [("U", 1, c), ("qtT", 1, j)], [pok], track=True)
                if cc == 3:
                    cols = slice(512 * j, 512 * (j + 1))
                    act(sqh, po[:, :], AF.Square, [pok], ["sqh"])
                    mm(P[6][:, :], ones128, sqh, True, True, ["sqh", "const"], ["ps6"])
                    act(lnh, P[6][:, :], AF.Ln, ["ps6"], ["lnh"], scale=1.0 / 128, bias=EPS)
                    act(lnh, lnh, AF.Exp, ["lnh"], ["lnh"], scale=-0.5)
                    stt(on, po[:, :], hgw[:, 0:1], lnh, ALU.mult, ALU.mult, [pok, "lnh", "hgw"], ["on"])
                    tt("pool", obS[j % 2], on, gbT[:, cols], ALU.mult, ["on", ("gbT", j)], [("obS", j % 2)])
                    k.dma("sp", ob_d[h * 128:(h + 1) * 128, cols], obS[j % 2], reads=[("obS", j % 2)], sem="d_obS%d" % (j % 2))
            out_front(0)
            for c in range(32):
                if c + 1 < 32:
                    out_front(c + 1)
                out_back(c)
            k.barrier()
        k.barrier()
        A.reset(m0)

    if 4 in phases:
        m0 = A.mark()
        wZ = A.alloc([128, 8, 2048], BF16)
        wpa = A.alloc([128, 4, 1024], BF16)
        wpb = A.alloc([128, 8, 1024], BF16)
        wo = A.alloc([128, 8, 1024], BF16)
        oaj = [A.alloc([128, 4, 512], BF16) for _ in range(1)]
        obj = [A.alloc([128, 8, 512], BF16) for _ in range(1)]
        mg_ = [A.alloc([128, 8, 512], BF16) for _ in range(1)]
        sa = A.alloc([128, 512], F32)
        sb2 = A.alloc([128, 512], F32)
        ma = A.alloc([128, 512], F32)
        mb = A.alloc([128, 512], F32)
        xr = [A.alloc([128, D], F32) for _ in range(2)]
        ot = [A.alloc([128, D], F32) for _ in range(2)]
        for seg in range(4):
            k.dma("pool", wZ[:, :, seg * 512:(seg + 1) * 512], win_v[:, :, 10240 + seg * 512:10240 + (seg + 1) * 512],
                  writes=["wZ"], sem="d_wZ")
        k.dma("pool", wpa, wpa_d.rearrange("(c p) n -> p c n", p=128), writes=["wpa"])
        wpb_v = wpb_d.rearrange("(c p) n -> p c n", p=128)
        wo_v = wo_d.rearrange("(c p) n -> p c n", p=128)
        for hh in range(2):
            k.dma("pool", wpb[:, hh * 4:(hh + 1) * 4, :], wpb_v[:, hh * 4:(hh + 1) * 4, :], writes=["wpb"], sem="d_wpb")
            k.dma("pool", wo[:, hh * 4:(hh + 1) * 4, :], wo_v[:, hh * 4:(hh + 1) * 4, :], writes=["wo"], sem="d_wo")
        oa_v = oa_d.rearrange("(c p) t -> p c t", p=128)
        ob_v = ob_d.rearrange("(c p) t -> p c t", p=128)
        xcnt = 0
        for j in range(8):
            cols = slice(512 * j, 512 * (j + 1))
            sl = 0
            k.dma("sp", oaj[sl], oa_v[:, :, cols], writes=[("oaj", sl)])
            k.dma("sp", obj[sl], ob_v[:, :, cols], writes=[("obj", sl)])
            for dc in range(8):
                dcs = slice(dc * 128, (dc + 1) * 128)
                for c in range(4):
                    mm(P[0][:, :], wpa[:, c, dcs], oaj[sl][:, c, :], c == 0, c == 3, ["wpa", ("oaj", sl)], ["ps0"], track=(c == 3))
                for c in range(8):
                    mm(P[1][:, :], wpb[:, c, dcs], obj[sl][:, c, :], c == 0, c == 7, ["wpb", ("obj", sl)], ["ps1"], track=(c == 7))
                for c in range(8):
                    mm(P[2][:, :], wZ[:, c, dcs], hT_nat(c, 512 * j, 512), c == 0, c == 7, ["wZ", "hT"], ["ps2"], track=(c == 7))
                for c in range(8):
                    mm(P[3][:, :], wZ[:, c, 1024 + dc * 128:1024 + (dc + 1) * 128], hT_nat(c, 512 * j, 512), c == 0, c == 7,
                       ["wZ", "hT"], ["ps3"], track=(c == 7))
                act(sa, P[2][:, :], AF.Sigmoid, ["ps2"], ["sa"])
                act(sb2, P[3][:, :], AF.Sigmoid, ["ps3"], ["sb2"])
                tt("dve", ma, P[0][:, :], sa, ALU.mult, ["ps0", "sa"], ["ma"])
                tt("dve", mb, P[1][:, :], sb2, ALU.mult, ["ps1", "sb2"], ["mb"])
                tt("pool", mg_[sl][:, dc, :], ma, mb, ALU.add, ["ma", "mb"], [("mg", sl)])
            for tq in range(4):
                t = 4 * j + tq
                xs = xcnt % 2
                xcnt += 1
                k.dma("act", xr[xs], x_d[t * 128:(t + 1) * 128, :], writes=[("xr", xs)])
                for half in range(2):
                    pb = 4 + half
                    for c in range(8):
                        mm(P[pb][:, :], mg_[sl][:, c, tq * 128:(tq + 1) * 128], wo[:, c, half * 512:(half + 1) * 512],
                           c == 0, c == 7, [("mg", sl), "wo"], ["ps%d" % pb], track=(c == 7))
                    tt("dve", ot[xs][:, half * 512:(half + 1) * 512], P[pb][:, :], xr[xs][:, half * 512:(half + 1) * 512],
                       ALU.add, ["ps%d" % pb, ("xr", xs)], [("ot", xs)])
                k.dma("sp", out_d[t * 128:(t + 1) * 128, :], ot[xs], reads=[("ot", xs)], sem="d_ot%d" % xs)
    k.finish("sp")
    print("instructions:", k.ninstr, "arena peak:", A.peak, {e: k.count[e] for e in ENGS})
    k.emit()
    k.close()
    es.close()
    return nc


def _bias_tiles(rel_bias):
    a = np.arange(128)[:, None]
    c = np.arange(128)[None, :]
    out = np.zeros((128, 3, 8, 2, 128), np.float32)
    for g, (_, d) in enumerate(GROUPS):
        for ab in range(2):
            rel = (a - 64 - c) if ab == 0 else (a + 64 - c)
            rel = np.clip(rel, -64, 64)
            idx = t5_bucket(rel * d)
            for h in range(8):
                out[:, g, h, ab, :] = rel_bias[idx, g * 8 + h]
    return out.reshape(128, -1)


_NC_CACHE = {}


def kernel(x, norm_w, w_in, q_norm_w, k_norm_w, rel_bias, lb_fwd, lb_bwd, hg_norm_w,
           w_proj_a, w_proj_b, w_out, _debug=False, _phases=(1, 2, 3, 4)):
    x = np.asarray(x, np.float32)
    B = x.shape[0]
    key = (_debug, tuple(_phases))
    if key not in _NC_CACHE:
        _NC_CACHE[key] = build_program(_debug, _phases)
    nc = _NC_CACHE[key]
    qkw = np.zeros((128, 6), np.float32)
    for g in range(3):
        qkw[:, 2 * g] = np.tile(np.asarray(q_norm_w, np.float32)[0, g], 2)
        qkw[:, 2 * g + 1] = np.tile(np.asarray(k_norm_w, np.float32)[0, g], 2)
    lbf = np.ascontiguousarray(np.asarray(lb_fwd, np.float32).reshape(2, 8, 128).transpose(2, 0, 1).reshape(128, 16))
    lbb = np.ascontiguousarray(np.asarray(lb_bwd, np.float32).reshape(2, 8, 128).transpose(2, 0, 1).reshape(128, 16))
    common = {
        "w_in": np.ascontiguousarray(np.asarray(w_in, np.float32)[0]),
        "norm_w": np.ascontiguousarray(np.asarray(norm_w, np.float32)[0:1]),
        "qkw": qkw,
        "biasT": _bias_tiles(np.asarray(rel_bias, np.float32)),
        "lbf": lbf, "lbb": lbb,
        "hgw": np.ascontiguousarray(np.asarray(hg_norm_w, np.float32)[0].reshape(128, 1)),
        "w_proj_a": np.ascontiguousarray(np.asarray(w_proj_a, np.float32)[0]),
        "w_proj_b": np.ascontiguousarray(np.asarray(w_proj_b, np.float32)[0]),
        "w_out": np.ascontiguousarray(np.asarray(w_out, np.float32)[0]),
    }
    in_maps = []
    for b in range(B):
        m = dict(common)
        m["x"] = np.ascontiguousarray(x[b])
        in_maps.append(m)
    res = run_bass_kernel_spmd(nc, in_maps, core_ids=list(range(B)))
    if _debug:
        return res.results
    return np.stack([np.asarray(r["out"], np.float32) for r in res.results], axis=0)
```

```python
import numpy as np
from contextlib import ExitStack
import concourse.bass as bass
import concourse.mybir as mybir
from concourse.bass_utils import run_bass_kernel_spmd

F32 = mybir.dt.float32
BF16 = mybir.dt.bfloat16
U8 = mybir.dt.uint8
AF = mybir.ActivationFunctionType
ALU = mybir.AluOpType

S = 4096
D = 1024
PAD = 1024
EPS = 1e-6
GROUPS = ((128, 1), (512, 4), (2048, 16))
NUM_BUCKETS = 32
REL_MAX_DISTANCE = 1024
IN_COLS = 12288
ENGS = ("pe", "act", "dve", "pool", "sp")
DBG = {}


class KB:
    def __init__(self, nc):
        self.nc = nc
        self.prog = {e: [] for e in ENGS}
        self.count = {}
        self.sems = {}
        self.seen = {e: {} for e in ENGS}
        self.snap = {}
        self.last_w = {}
        self.readers = {}
        self.pending = {e: ([], []) for e in ENGS}
        self._ctx = []
        self.ninstr = 0
        for e in ENGS:
            self._mksem(e)

    def _mksem(self, name):
        cm = self.nc.semaphore("s_" + name)
        h = cm.__enter__()
        self._ctx.append(cm)
        self.sems[name] = h
        self.count[name] = 0
        return h

    def close(self):
        for cm in reversed(self._ctx):
            cm.__exit__(None, None, None)

    def _deps(self, e, reads, writes):
        deps = {}

        def add(cv):
            if cv is None:
                return
            c, v = cv
            if deps.get(c, 0) < v:
                deps[c] = v
        for r in reads:
            add(self.last_w.get(r))
        for w in writes:
            add(self.last_w.get(w))
            for rd in self.readers.get(w, ()):
                add(rd)
        waits = []
        seen = self.seen[e]
        for c, v in deps.items():
            if c == e and e == "pe":
                continue
            if seen.get(c, 0) < v:
                waits.append((c, v))
                seen[c] = v
                sn = self.snap.get((c, v))
                if sn:
                    for c2, v2 in sn.items():
                        if c2 != e and seen.get(c2, 0) < v2:
                            seen[c2] = v2
        return waits

    def _commit(self, token, reads, writes):
        for r in reads:
            self.readers.setdefault(r, []).append(token)
        for w in writes:
            self.last_w[w] = token
            self.readers[w] = []

    def op(self, e, fn, reads=(), writes=(), track=True):
        reads = tuple(reads)
        writes = tuple(writes)
        waits = self._deps(e, reads, writes)
        self.ninstr += 1
        if track:
            self.count[e] += 1
            token = (e, self.count[e])
            pr, pw = self.pending[e]
            self._commit(token, list(reads) + pr, list(writes) + pw)
            self.pending[e] = ([], [])
            self.snap[token] = dict(self.seen[e])
            self.prog[e].append((waits, fn, (e, 1)))
        else:
            pr, pw = self.pending[e]
            pr.extend(reads)
            pw.extend(writes)
            self.prog[e].append((waits, fn, None))

    def dma(self, q, out, in_, reads=(), writes=(), sem=None):
        reads = tuple(reads)
        writes = tuple(writes)
        if sem is None:
            sem = "d_" + str(writes[0] if writes else reads[0])
        if sem not in self.sems:
            self._mksem(sem)
        waits = self._deps(q, reads, writes)
        self.count[sem] += 16
        token = (sem, self.count[sem])
        self._commit(token, reads, writes)
        self.snap[token] = dict(self.seen[q])
        self.ninstr += 1

        def fn(eng, out=out, in_=in_):
            return eng.dma_start(out=out, in_=in_)
        self.prog[q].append((waits, fn, (sem, 16)))
        return token

    def barrier(self):
        for e in ENGS:
            waits = []
            for c, v in self.count.items():
                if v > 0 and self.seen[e].get(c, 0) < v:
                    if c == e and e in ("pe", "sp"):
                        continue
                    waits.append((c, v))
                    self.seen[e][c] = v
            if waits:
                self.prog[e].append((waits, None, None))
        self.last_w = {}
        self.readers = {}

    def finish(self, e="sp"):
        waits = []
        for c, v in self.count.items():
            if v > 0 and self.seen[e].get(c, 0) < v and c != e:
                waits.append((c, v))
        self.prog[e].append((waits, None, None))

    def emit(self):
        nc = self.nc
        with nc.Block() as block:
            for e in ENGS:
                items = self.prog[e]
                if not items:
                    continue

                def body(eng, items=items):
                    for waits, fn, inc in items:
                        for c, v in waits:
                            eng.wait_ge(self.sems[c], v)
                        if fn is None:
                            continue
                        ins = fn(eng)
                        if inc is not None:
                            ins.then_inc(self.sems[inc[0]], inc[1])
                reg = {"pe": block.tensor, "act": block.scalar, "dve": block.vector,
                       "pool": block.gpsimd, "sp": block.sync}[e]
                reg(body)


class Arena:
    def __init__(self, ap, nbytes):
        self.ap = ap
        self.nbytes = nbytes
        self.off = 0
        self.peak = 0

    def alloc(self, shape, dt):
        n = int(np.prod(shape[1:]))
        nb = n * (4 if dt == F32 else 2)
        nb = (nb + 63) // 64 * 64
        assert self.off + nb <= self.nbytes, ("SBUF arena overflow", self.off, nb)
        a = self.ap[:, self.off:self.off + nb].bitcast(dt)[:, 0:n]
        self.off += nb
        self.peak = max(self.peak, self.off)
        if len(shape) == 3:
            a = a.rearrange("p (a b) -> p a b", b=shape[2])
        return a

    def mark(self):
        return self.off

    def reset(self, m):
        self.off = m


def t5_bucket(rel):
    half = NUM_BUCKETS // 2
    max_exact = half // 2
    n = np.abs(rel)
    large = max_exact + (np.log(np.maximum(n, 1) / max_exact)
                         / np.log(REL_MAX_DISTANCE / max_exact) * (half - max_exact)).astype(np.int32)
    large = np.minimum(large, half - 1)
    return np.where(rel > 0, half, 0) + np.where(n < max_exact, n, large)


def build_program(debug=False, phases=(1, 2, 3, 4)):
    nc = bass.Bass("TRN2", target_bir_lowering=False)
    x_d = nc.dram_tensor("x", [S, D], F32, kind="ExternalInput").ap()
    win_d = nc.dram_tensor("w_in", [D, IN_COLS], F32, kind="ExternalInput").ap()
    normw_d = nc.dram_tensor("norm_w", [1, D], F32, kind="ExternalInput").ap()
    qkw_d = nc.dram_tensor("qkw", [128, 6], F32, kind="ExternalInput").ap()
    bias_d = nc.dram_tensor("biasT", [128, 3 * 8 * 2 * 128], F32, kind="ExternalInput").ap()
    lbf_d = nc.dram_tensor("lbf", [128, 16], F32, kind="ExternalInput").ap()
    lbb_d = nc.dram_tensor("lbb", [128, 16], F32, kind="ExternalInput").ap()
    hgw_d = nc.dram_tensor("hgw", [128, 1], F32, kind="ExternalInput").ap()
    wpa_d = nc.dram_tensor("w_proj_a", [512, D], F32, kind="ExternalInput").ap()
    wpb_d = nc.dram_tensor("w_proj_b", [D, D], F32, kind="ExternalInput").ap()
    wo_d = nc.dram_tensor("w_out", [D, D], F32, kind="ExternalInput").ap()
    out_d = nc.dram_tensor("out", [S, D], F32, kind="ExternalOutput").ap()
    skind = "ExternalOutput" if debug else "Internal"
    oa_d = nc.dram_tensor("oa_scr", [512, S], BF16, kind=skind).ap()
    ob_d = nc.dram_tensor("ob_scr", [D, S], BF16, kind=skind).ap()
    if debug:
        hT_dbg = nc.dram_tensor("hT_dbg", [128, 8 * S], BF16, kind="ExternalOutput").ap()

    win_v = win_d.rearrange("(c p) n -> p c n", p=128)

    es = ExitStack()
    ARENA_BYTES = 209920
    arena_t = es.enter_context(nc.sbuf_tensor("arena", [128, ARENA_BYTES], U8))
    A = Arena(arena_t, ARENA_BYTES)
    P = [es.enter_context(nc.psum_tensor("ps%d" % i, [128, 512], F32)) for i in range(7)]
    PT = es.enter_context(nc.psum_tensor("pst", [128, 1024], BF16))
    k = KB(nc)

    def mm(out, lhsT, rhs, start, stop, reads, writes, track=True, tp=None):
        def fn(e):
            if tp is None:
                return e.matmul(out, lhsT=lhsT, rhs=rhs, start=start, stop=stop)
            return e.matmul(out, lhsT=lhsT, rhs=rhs, start=start, stop=stop, tile_position=tp)
        k.op("pe", fn, reads, writes, track)

    def tr(out, in_, reads, writes, track=True):
        k.op("pe", lambda e: e.transpose(out, in_, ident), tuple(reads) + ("const",), writes, track)

    def act(out, in_, func, reads, writes, scale=None, bias=None, accum=None):
        kw = {}
        if scale is not None:
            kw["scale"] = scale
        if bias is not None:
            kw["bias"] = bias
        if accum is not None:
            kw["accum_out"] = accum
        k.op("act", lambda e: e.activation(out=out, in_=in_, func=func, **kw), reads, writes)

    def tt(eng, out, in0, in1, op, reads, writes):
        k.op(eng, lambda e: e.tensor_tensor(out=out, in0=in0, in1=in1, op=op), reads, writes)

    def ts(eng, out, in0, s1, op0, reads, writes, s2=None, op1=None):
        if op1 is None:
            k.op(eng, lambda e: e.tensor_scalar(out=out, in0=in0, scalar1=s1, scalar2=None, op0=op0), reads, writes)
        else:
            k.op(eng, lambda e: e.tensor_scalar(out=out, in0=in0, scalar1=s1, scalar2=s2, op0=op0, op1=op1), reads, writes)

    def stt(out, in0, scalar, in1, op0, op1, reads, writes):
        k.op("dve", lambda e: e.scalar_tensor_tensor(out=out, in0=in0, scalar=scalar, in1=in1, op0=op0, op1=op1),
             reads, writes)

    def cp(eng, out, in_, reads, writes):
        if eng == "act":
            k.op("act", lambda e: e.copy(out=out, in_=in_), reads, writes)
        else:
            k.op(eng, lambda e: e.tensor_copy(out=out, in_=in_), reads, writes)

    def recip(out, in_, reads, writes):
        k.op("dve", lambda e: e.reciprocal(out=out, in_=in_), reads, writes)

    def memset(eng, ap, val, writes):
        k.op(eng, lambda e: e.memset(ap, val), (), writes)

    hTf = A.alloc([128, PAD + 8 * S + PAD], BF16)
    ident = A.alloc([128, 128], BF16)
    ones_bd = A.alloc([128, 128], BF16)
    ones128 = A.alloc([128, 128], BF16)
    tri4 = A.alloc([128, 3, 128], F32)
    scanmask = A.alloc([128, 512], F32)
    qkw = A.alloc([128, 6], F32)
    lbt = A.alloc([128, 32], F32)
    lbd = A.alloc([128, 16], F32)
    oml = A.alloc([128, 16], F32)
    hgw = A.alloc([128, 1], F32)
    persist_mark = A.mark()

    def hT_nat(c, t0, n):
        o = PAD + c * S + t0
        return hTf[:, o:o + n]

    def hT_perm(c, d, n0, cnt):
        L = S // d
        r0 = n0 // L
        l0 = n0 % L
        base = PAD + c * S
        if d == 1:
            return hTf[:, base + n0: base + n0 + cnt], None
        if cnt <= L - l0:
            st = base + l0 * d + r0
            return hTf[:, st: st + (cnt - 1) * d + 1: d], None
        assert l0 == 0 and cnt % L == 0
        nres = cnt // L
        st = base + r0
        v = hTf[:, st: st + L * d].rearrange("p (l r) -> p r l", r=d)[:, 0:nres, :]
        return v, L

    def hT_vtile(c, d, r, i):
        st = PAD + c * S + (128 * i - 64) * d + r
        if d == 1:
            return hTf[:, st: st + 128]
        return hTf[:, st: st + 127 * d + 1: d]

    def acc_perm(acc, d, n0, cnt):
        L = S // d
        r0 = n0 // L
        l0 = n0 % L
        if d == 1:
            return acc[:, n0:n0 + cnt], None
        if cnt <= L - l0:
            st = l0 * d + r0
            return acc[:, st: st + (cnt - 1) * d + 1: d], None
        assert l0 == 0 and cnt % L == 0
        nres = cnt // L
        v = acc.rearrange("p (l r) -> p r l", r=d)[:, r0:r0 + nres, :]
        return v, L

    memset("pool", hTf[:, 0:PAD], 0.0, ["hTpad"])
    memset("pool", hTf[:, PAD + 8 * S: PAD + 8 * S + PAD], 0.0, ["hTpad"])
    memset("pool", ident, 0.0, ["const"])
    k.op("pool", lambda e: e.affine_select(out=ident, in_=ident, pattern=[[-1, 128]], compare_op=ALU.not_equal,
                                           fill=1.0, base=0, channel_multiplier=1), ["const"], ["const"])
    memset("pool", ones_bd, 0.0, ["const"])
    memset("pool", ones_bd[0:64, 0:64], 1.0, ["const"])
    memset("pool", ones_bd[64:128, 64:128], 1.0, ["const"])
    memset("pool", ones128, 1.0, ["const"])
    memset("pool", tri4, 1.0, ["const"])
    for idx in (0, 2):
        k.op("pool", lambda e, idx=idx: e.affine_select(out=tri4[:, idx, :], in_=tri4[:, idx, :], pattern=[[-1, 128]],
                                                        compare_op=ALU.is_ge, fill=0.0, base=0, channel_multiplier=1),
             ["const"], ["const"])
    k.op("pool", lambda e: e.affine_select(out=tri4[:, 1, :], in_=tri4[:, 1, :], pattern=[[1, 128]],
                                           compare_op=ALU.is_ge, fill=0.0, base=0, channel_multiplier=-1),
         ["const"], ["const"])
    memset("pool", scanmask, 1.0, ["const"])
    memset("pool", scanmask.rearrange("p (c t) -> p c t", t=128)[:, :, 0:1], 0.0, ["const"])
    k.dma("sp", qkw, qkw_d, writes=["qkw"])
    k.dma("sp", lbt[:, 0:16], lbf_d, writes=["lbt"], sem="d_lbt")
    k.dma("sp", lbt[:, 16:32], lbb_d, writes=["lbt"], sem="d_lbt")
    k.dma("sp", hgw, hgw_d, writes=["hgw"])
    for g in range(3):
        ts("dve", qkw[:, 2 * g + 1:2 * g + 2], qkw[:, 2 * g + 1:2 * g + 2], 8.0, ALU.mult, ["qkw"], ["qkw"])
    lbv = lbt.rearrange("p (d l h) -> p d l h", d=2, l=2)
    for dd in range(2):
        tt("dve", lbd[:, dd * 8:(dd + 1) * 8], lbv[:, dd, 0, :], lbv[:, dd, 1, :], ALU.subtract, ["lbt"], ["lbd"])
    act(oml, lbd, AF.Sigmoid, ["lbd"], ["oml"], scale=-1.0)

    if 1 in phases:
        m0 = A.mark()
        normw_rep = A.alloc([128, D], F32)
        xts = [A.alloc([128, D], F32) for _ in range(2)]
        xn = [A.alloc([128, D], BF16) for _ in range(2)]
        junk = A.alloc([128, D], BF16)
        ssq = A.alloc([128, 32], F32)
        rstd = A.alloc([128, 32], F32)
        k.dma("sp", normw_rep, normw_d.partition_broadcast(128), writes=["normw"])
        memset("pool", ssq, 0.0, [("ssq", t) for t in range(32)])
        hT_ct = hTf[:, PAD:PAD + 8 * S].rearrange("p (c t) -> p c t", t=S)
        for t in range(32):
            sl = t % 2
            k.dma("sp" if sl == 0 else "act", xts[sl], x_d[t * 128:(t + 1) * 128, :], writes=[("xt", sl)])
            act(junk, xts[sl], AF.Square, [("xt", sl)], ["junk", ("ssq", t)], accum=ssq[:, t:t + 1])
            ts("dve", rstd[:, t:t + 1], ssq[:, t:t + 1], 1.0 / D, ALU.mult, [("ssq", t)], [("rstd", t)], s2=EPS, op1=ALU.add)
            act(rstd[:, t:t + 1], rstd[:, t:t + 1], AF.Sqrt, [("rstd", t)], [("rstd", t)])
            recip(rstd[:, t:t + 1], rstd[:, t:t + 1], [("rstd", t)], [("rstd", t)])
            stt(xn[sl], xts[sl], rstd[:, t:t + 1], normw_rep, ALU.mult, ALU.mult,
                [("xt", sl), ("rstd", t), "normw"], [("xn", sl)])
            for c in range(8):
                tr(PT[:, c * 128:(c + 1) * 128], xn[sl][:, c * 128:(c + 1) * 128], [("xn", sl)], ["PT"], track=(c == 7))
            cp("act" if t % 2 == 0 else "dve", hT_ct[:, :, t * 128:(t + 1) * 128],
               PT[:, :].rearrange("p (c t) -> p c t", t=128), ["PT"], [("hT", t)])
        k.barrier()
        A.reset(m0)
        if debug:
            k.dma("sp", hT_dbg, hTf[:, PAD:PAD + 8 * S], reads=["x"], sem="dbg")

    if 2 in phases:
        m0 = A.mark()
        wA = [A.alloc([128, 8, 384], BF16) for _ in range(2)]
        wG = A.alloc([128, 8, 512], BF16)
        qn = A.alloc([128, S], BF16)
        knp = A.alloc([128, 64 + S + 64], BF16)
        vTp = A.alloc([128, 64 + S + 64], BF16)
        vP = A.alloc([128, 33 * 256], BF16)
        acc = [A.alloc([128, S], F32) for _ in range(2)]
        bM = A.alloc([128, 512], F32)
        Mrep = A.alloc([128, 1024], F32)
        sq = [[A.alloc([128, 512], BF16) for _ in range(2)] for _ in range(2)]
        lnb = [A.alloc([128, 512], F32) for _ in range(2)]
        ee = [A.alloc([128, 512], F32) for _ in range(3)]
        pT = [A.alloc([128, 512], BF16) for _ in range(3)]
        sgh = [A.alloc([128, 512], F32) for _ in range(2)]
        rD = [A.alloc([128, 512], F32) for _ in range(2)]
        t1 = [A.alloc([128, 512], F32) for _ in range(2)]
        oaS = [A.alloc([128, 512], BF16) for _ in range(4)]
        vP4 = vP.rearrange("p (t h d) -> p t h d", h=2, d=128)
        memset("pool", knp[:, 0:64], 0.0, [("kn", -1)])
        memset("pool", knp[:, 64 + S:64 + S + 64], 0.0, [("kn", 8)])
        memset("pool", vTp[:, 0:64], 0.0, [("vT", -1)])
        memset("pool", vTp[:, 64 + S:64 + S + 64], 0.0, [("vT", 8)])
        memset("pool", vP4[:, :, :, 64:128], 1.0, ["vPones"])
        for seg in range(4):
            k.dma("pool", wG[:, :, seg * 128:(seg + 1) * 128], win_v[:, :, 4608 + seg * 128: 4608 + (seg + 1) * 128],
                  writes=["wG"], sem="d_wG")
        maskAB = tri4[:, 0:2, :]
        units = [(hp, g) for hp in range(DBG.get("nhp", 4)) for g in range(DBG.get("ng", 3))]

        def load_unit_w(u):
            hp_, g_ = units[u]
            slot_ = u % 2
            for seg in range(3):
                col0 = g_ * 1536 + seg * 512 + hp_ * 128
                k.dma("pool", wA[slot_][:, :, seg * 128:(seg + 1) * 128], win_v[:, :, col0:col0 + 128],
                      writes=[("wA", slot_)], sem="d_wA%d" % slot_)
        load_unit_w(0)
        sccnt = 0
        oacnt = 0
        for u, (hp, g) in enumerate(units):
            win_, d = GROUPS[g]
            L = S // d
            TPS = L // 128
            slot = u % 2
            w = wA[slot]
            if u + 1 < len(units):
                load_unit_w(u + 1)
            bcol = (g * 8 + 2 * hp) * 256
            k.dma("sp", bM, bias_d[:, bcol:bcol + 512], writes=["bM"])
            act(bM, bM, AF.Exp, ["bM"], ["bM"])
            bMv = bM.rearrange("p (h a c) -> p h a c", h=2, a=2)
            Mv = Mrep.rearrange("p (h m a c) -> p h m a c", h=2, m=2, a=2)
            for hl in range(2):
                for mi in range(2):
                    tt("pool", Mv[:, hl, mi], bMv[:, hl], maskAB, ALU.mult, ["bM", "const"], ["Mrep"])

            nl = 512 // d

            def pview(ap_nat):
                if d == 1:
                    return ap_nat
                return ap_nat.rearrange("p (l r) -> p r l", r=d)

            def dview(buf, off, j):
                if d == 1:
                    return buf[:, off + 512 * j: off + 512 * (j + 1)]
                return buf[:, off:off + S].rearrange("p (r l) -> p r l", l=L)[:, :, nl * j: nl * (j + 1)]

            def wkeys(nm, j):
                return [(nm, j)] if d == 1 else [(nm, jj) for jj in range(8)]

            def proj_front(j):
                pqb, pkb, pvb = j % 2, 2 + j % 2, 4
                for (pb, woff) in ((pqb, 0), (pkb, 128), (pvb, 256)):
                    for c in range(8):
                        mm(P[pb][:, :], w[:, c, woff:woff + 128], hT_nat(c, 512 * j, 512), c == 0, c == 7,
                           [("wA", slot), "hT"], ["ps%d" % pb], track=(c == 7))
                act(sq[0][j % 2], P[pqb][:, :], AF.Square, ["ps%d" % pqb], [("sq", 0, j % 2)])
                act(sq[1][j % 2], P[pkb][:, :], AF.Square, ["ps%d" % pkb], [("sq", 1, j % 2)])
                cp("dve", dview(vTp, 64, j), pview(P[pvb][:, :]), ["ps%d" % pvb], wkeys("vT", j))

            def proj_back(j):
                pqb, pkb = j % 2, 2 + j % 2
                items = ((pqb, 2 * g, dview(qn, 0, j), wkeys("qn", j)),
                         (pkb, 2 * g + 1, dview(knp, 64, j), wkeys("kn", j)))
                for qi in range(2):
                    mm(P[5 + qi][:, :], ones_bd, sq[qi][j % 2], True, True, [("sq", qi, j % 2), "const"], ["ps%d" % (5 + qi)])
                for qi in range(2):
                    act(lnb[qi], P[5 + qi][:, :], AF.Ln, ["ps%d" % (5 + qi)], [("lnb", qi)], bias=64.0 * EPS)
                for qi in range(2):
                    act(lnb[qi], lnb[qi], AF.Exp, [("lnb", qi)], [("lnb", qi)], scale=-0.5)
                for qi, (pb, wcol, dst, dkeys) in enumerate(items):
                    stt(dst, pview(P[pb][:, :]), qkw[:, wcol:wcol + 1], pview(lnb[qi]), ALU.mult, ALU.mult,
                        ["ps%d" % pb, ("lnb", qi), "qkw"], dkeys)
            if "q" in DBG.get("st", "qvaf"):
                proj_front(0)
                for j in range(8):
                    if j + 1 < 8:
                        proj_front(j + 1)
                    proj_back(j)
                for t0_ in range(0, 33, 8):
                    nb = min(8, 33 - t0_)
                    for s_ in range(nb):
                        vt = t0_ + s_
                        lo = (128 * vt - 64) // 512 if vt > 0 else -1
                        hi = min((128 * vt + 63) // 512, 8)
                        tr(PT[:, s_ * 128:(s_ + 1) * 128], vTp[:, 128 * vt:128 * vt + 128],
                           [("vT", lo), ("vT", hi)], ["PT"], track=(s_ == nb - 1))
                    cp("act", vP4[:, t0_:t0_ + nb, :, 0:64],
                       PT[:, 0:nb * 128].rearrange("p (t h d) -> p t h d", h=2, d=64), ["PT"], [("vP", t0_ // 8)])

            its = [(mg, mp, hl) for mg in range(8) for mp in range(2) for hl in range(2)]
            state = {}

            def att_front(ii):
                nonlocal sccnt
                mg, mp, hl = its[ii]
                rows = slice(hl * 64, hl * 64 + 64)
                sb_ = sccnt % 3
                sl2 = sccnt % 3
                sccnt += 1
                state[ii] = (sb_, sl2)
                psc = P[sb_]
                psck = "ps%d" % sb_
                for mi in range(2):
                    m = mg * 4 + mp * 2 + mi
                    jq = m // 4
                    for ab in range(2):
                        kst = 128 * m + 128 * ab
                        kj = [("kn", (kst - 64) // 512 if kst >= 64 else -1), ("kn", min((kst + 127 - 64) // 512, 8))]
                        mm(psc[:, (mi * 2 + ab) * 128:(mi * 2 + ab + 1) * 128],
                           knp[rows, kst:kst + 128], qn[rows, 128 * m:128 * m + 128], True, True,
                           [("qn", jq)] + kj, [psck], track=(mi == 1 and ab == 1), tp=(hl * 64, 0))
                act(ee[sl2], psc[:, :], AF.Exp, [psck], [("ee", sl2)])
                tt("dve" if ii % 2 == 0 else "pool", pT[sl2], ee[sl2], Mrep[:, hl * 512:(hl + 1) * 512], ALU.mult,
                   [("ee", sl2), "Mrep"], [("pT", sl2)])

            def att_back(ii):
                mg, mp, hl = its[ii]
                sb_, sl2 = state[ii]
                pb = 3 + (mg % 2) * 2 + hl
                pnd = P[pb]
                pndk = "ps%d" % pb
                for mi in range(2):
                    m = mg * 4 + mp * 2 + mi
                    n0 = 128 * m
                    i = (n0 % L) // 128
                    first = (i == 0)
                    last = (i == TPS - 1)
                    vA = m
                    vB = m + 1
                    col = (mp * 2 + mi) * 128
                    KA = slice(64, 128) if first else slice(0, 128)
                    KBs = slice(0, 64) if last else slice(0, 128)
                    vkeys = [("vP", vA // 8), ("vP", vB // 8), "vPones"]
                    mm(pnd[:, col:col + 128], vP4[KA, vA, hl, :],
                       pT[sl2][KA, (mi * 2) * 128:(mi * 2 + 1) * 128], True, False,
                       [("pT", sl2)] + vkeys, [pndk], track=False, tp=(KA.start, 0))
                    mm(pnd[:, col:col + 128], vP4[KBs, vB, hl, :],
                       pT[sl2][KBs, (mi * 2 + 1) * 128:(mi * 2 + 2) * 128], False, True,
                       [("pT", sl2)] + vkeys, [pndk], track=(mi == 1), tp=(0, 0))
                if mp == 1:
                    av, inner = acc_perm(acc[hl], d, 512 * mg, 512)
                    pv_ = pnd[:, :] if inner is None else pnd[:, :].rearrange("p (a b) -> p a b", b=inner)
                    akey = ("acc", hl)
                    if g == 0:
                        cp("act", av, pv_, [pndk], [akey])
                    else:
                        tt("dve", av, av, pv_, ALU.add, [pndk, akey], [akey])
            if "a" in DBG.get("st", "qvaf"):
                LA = 2
                for ii in range(min(LA, len(its))):
                    att_front(ii)
                for ii in range(len(its)):
                    if ii + LA < len(its):
                        att_front(ii + LA)
                    att_back(ii)

            if g == len([1 for (hp_, g_) in units if hp_ == hp]) - 1 and "f" in DBG.get("st", "qvaf"):
                for j in range(8):
                    pg = P[j % 3]
                    pgk = "ps%d" % (j % 3)
                    for c in range(8):
                        mm(pg[:, :], wG[:, c, hp * 128:(hp + 1) * 128], hT_nat(c, 512 * j, 512), c == 0, c == 7,
                           ["wG", "hT"], [pgk], track=(c == 7))
                    HL = (0, 1)
                    for hl in HL:
                        act(sgh[hl][0:64, :], pg[hl * 64:(hl + 1) * 64, :], AF.Silu, [pgk], [("sgh", hl)])
                    for hl in HL:
                        act(rD[hl][0:64, :], acc[hl][64:128, 512 * j:512 * (j + 1)], AF.Ln, [("acc", hl)], [("rD", hl)])
                    for hl in HL:
                        act(rD[hl][0:64, :], rD[hl][0:64, :], AF.Exp, [("rD", hl)], [("rD", hl)], scale=-1.0)
                    for hl in HL:
                        tt("pool", t1[hl][0:64, :], acc[hl][0:64, 512 * j:512 * (j + 1)], rD[hl][0:64, :], ALU.mult,
                           [("acc", hl), ("rD", hl)], [("t1", hl)])
                    for hl in HL:
                        os_ = oacnt % 4
                        oacnt += 1
                        tt("dve", oaS[os_][0:64, :], t1[hl][0:64, :], sgh[hl][0:64, :], ALU.mult,
                           [("t1", hl), ("sgh", hl)], [("oaS", os_)])
                        k.dma("sp", oa_d[(2 * hp + hl) * 64:(2 * hp + hl + 1) * 64, 512 * j:512 * (j + 1)], oaS[os_][0:64, :],
                              reads=[("oaS", os_)], sem="d_oaS%d" % os_)
        k.barrier()
        A.reset(m0)

    if 3 in phases:
        m0 = A.mark()
        wH = [A.alloc([128, 8, 640], BF16) for _ in range(2)]
        vH = A.alloc([128, S], BF16)
        gbT = A.alloc([128, S], BF16)
        qtT = [A.alloc([128, S], BF16) for _ in range(2)]
        ktT = [A.alloc([128, S], BF16) for _ in range(2)]
        ktok = [A.alloc([128, S], BF16) for _ in range(2)]
        dec = [A.alloc([128, 32], F32) for _ in range(2)]
        Ubf = [A.alloc([128, S], BF16) for _ in range(2)]
        iT = Ubf[0]
        Wst = [[A.alloc([128, 128], F32) for _ in range(2)] for _ in range(2)]
        su = [A.alloc([128, 512], F32) for _ in range(2)]
        sL1 = [A.alloc([128, 512], F32) for _ in range(2)]
        sg_ = [A.alloc([128, 512], F32) for _ in range(2)]
        sb_ = [A.alloc([128, 512], F32) for _ in range(2)]
        sr = [A.alloc([128, 512], F32) for _ in range(2)]
        se = [A.alloc([128, 512], F32) for _ in range(2)]
        AT = [A.alloc([128, 256], BF16) for _ in range(2)]
        sqh = A.alloc([128, 512], BF16)
        lnh = A.alloc([128, 512], F32)
        on = lnh
        obS = [A.alloc([128, 512], BF16) for _ in range(2)]
        lbv2 = A.alloc([128, 16], F32)
        act(lbv2, lbd, AF.Sigmoid, ["lbd"], ["lbv2"])
        triH = tri4[:, 1:3, :]
        NH = DBG.get("nh", 8)

        def load_wH(h_):
            for seg in range(5):
                col0 = 5120 + seg * 1024 + h_ * 128
                k.dma("pool", wH[h_ % 2][:, :, seg * 128:(seg + 1) * 128], win_v[:, :, col0:col0 + 128],
                      writes=[("wH", h_ % 2)], sem="d_wH%d" % (h_ % 2))
        if DBG.get("pref", 1):
            load_wH(0)
        for h in range(NH):
            hs = h % 2
            w = wH[hs]
            if DBG.get("pref", 1):
                if h + 1 < NH:
                    load_wH(h + 1)
            else:
                load_wH(h)
            for j in range(8):
                cols = slice(512 * j, 512 * (j + 1))
                pb = j % 2
                for c in range(8):
                    mm(P[pb][:, :], w[:, c, 512:640], hT_nat(c, 512 * j, 512), c == 0, c == 7,
                       [("wH", hs), "hT"], ["ps%d" % pb], track=(c == 7))
                act(gbT[:, cols], P[pb][:, :], AF.Silu, ["ps%d" % pb], [("gbT", j)])
            for j in range(8):
                cols = slice(512 * j, 512 * (j + 1))
                pb = 2 + j % 2
                for c in range(8):
                    mm(P[pb][:, :], w[:, c, 384:512], hT_nat(c, 512 * j, 512), c == 0, c == 7,
                       [("wH", hs), "hT"], ["ps%d" % pb], track=(c == 7))
                cp("dve", iT[:, cols], P[pb][:, :], ["ps%d" % pb], [("iT", j)])
            for t0_ in range(0, 32, 8):
                for s_ in range(8):
                    t = t0_ + s_
                    tr(PT[:, s_ * 128:(s_ + 1) * 128], iT[:, t * 128:(t + 1) * 128], [("iT", t // 4)], ["PT"], track=(s_ == 7))
                cp("act" if (t0_ // 8) % 2 == 0 else "dve", vH[:, t0_ * 128:(t0_ + 8) * 128], PT[:, :], ["PT"], [("vH", t0_ // 8)])

            def gate_front(j):
                for (pi_, woff) in ((0, 0), (1, 128), (2, 256)):
                    pb = pi_ * 2 + j % 2
                    for c in range(8):
                        mm(P[pb][:, :], w[:, c, woff:woff + 128], hT_nat(c, 512 * j, 512), c == 0, c == 7,
                           [("wH", hs), "hT"], ["ps%d" % pb], track=(c == 7))
                cols = slice(512 * j, 512 * (j + 1))
                pqk = "ps%d" % (j % 2)
                pq = P[j % 2]
                DR = (0, 1)

                def K_(nm, dr):
                    return (nm, dr)
                src = [sb_[0], sr[1]]
                srck = [K_("b", 0), K_("r", 1)]
                dcol = [slice(127, 512, 128), slice(0, 512, 128)]
                for dr in DR:
                    pzb_ = (1 + dr) * 2 + j % 2
                    act(su[dr], P[pzb_][:, :], AF.Exp, ["ps%d" % pzb_], [K_("u", dr)])
                for dr in DR:
                    act(sL1[dr], su[dr], AF.Ln, [K_("u", dr)], [K_("L1", dr)], bias=1.0)
                for dr in DR:
                    act(sg_[dr], su[dr], AF.Ln, [K_("u", dr)], [K_("g", dr)], bias=lbv2[:, dr * 8 + h: dr * 8 + h + 1])
                for dr in DR:
                    tt("pool", sg_[dr], sg_[dr], sL1[dr], ALU.subtract, [K_("g", dr), K_("L1", dr)], [K_("g", dr)])
                for dr in DR:
                    k.op("dve", lambda e, o=sb_[dr], g_=sg_[dr]: e.tensor_tensor_scan(out=o, data0=scanmask, data1=g_, initial=0.0,
                                                                                op0=ALU.mult, op1=ALU.add),
                         [K_("g", dr), "const"], [K_("b", dr)])
                tt("pool", sr[1], sg_[1], sb_[1], ALU.subtract, [K_("g", 1), K_("b", 1)], [K_("r", 1)])
                bbv = sb_[1].rearrange("p (c t) -> p c t", t=128)
                rrv = sr[1].rearrange("p (c t) -> p c t", t=128)
                tt("dve", rrv, rrv, bbv[:, :, 127:128].to_broadcast([128, 4, 128]), ALU.add, [K_("r", 1), K_("b", 1)], [K_("r", 1)])
                for dr in DR:
                    act(se[dr], src[dr], AF.Exp, [srck[dr]], [K_("e", dr)])
                for dr in DR:
                    tt("dve", qtT[dr][:, cols], pq[:, :], se[dr], ALU.mult, [pqk, K_("e", dr)], [("qtT", dr, j)])
                    cp(DBG.get("deceng", "pool"), dec[dr][:, 4 * j:4 * j + 4], se[dr][:, dcol[dr]], [K_("e", dr)], [("dec", dr)])
                for dr in DR:
                    tt("pool", sL1[dr], sL1[dr], src[dr], ALU.add, [K_("L1", dr), srck[dr]], [K_("L1", dr)])
                for dr in DR:
                    act(su[dr], sL1[dr], AF.Exp, [K_("L1", dr)], [K_("u", dr)], scale=-1.0)
                for dr in DR:
                    ts("dve", ktT[dr][:, cols], su[dr], oml[:, dr * 8 + h: dr * 8 + h + 1], ALU.mult,
                       [K_("u", dr), "oml"], [("ktT", dr, j)])

            def gate_front_old(j):
                for (pi_, woff) in ((0, 0), (1, 128), (2, 256)):
                    pb = pi_ * 2 + j % 2
                    for c in range(8):
                        mm(P[pb][:, :], w[:, c, woff:woff + 128], hT_nat(c, 512 * j, 512), c == 0, c == 7,
                           [("wH", hs), "hT"], ["ps%d" % pb], track=(c == 7))
                cols = slice(512 * j, 512 * (j + 1))
                pqk = "ps%d" % (j % 2)
                pq = P[j % 2]
                for dr in range(2):
                    pzb_ = (1 + dr) * 2 + j % 2
                    pz = P[pzb_]
                    pzk = "ps%d" % pzb_
                    K_ = lambda nm: (nm, dr)
                    act(su[dr], pz[:, :], AF.Exp, [pzk], [K_("u")])
                    act(sL1[dr], su[dr], AF.Ln, [K_("u")], [K_("L1")], bias=1.0)
                    act(sg_[dr], su[dr], AF.Ln, [K_("u")], [K_("g")], bias=lbv2[:, dr * 8 + h: dr * 8 + h + 1])
                    tt("pool", sg_[dr], sg_[dr], sL1[dr], ALU.subtract, [K_("g"), K_("L1")], [K_("g")])
                    k.op("dve", lambda e, o=sb_[dr], g_=sg_[dr]: e.tensor_tensor_scan(out=o, data0=scanmask, data1=g_, initial=0.0,
                                                                                op0=ALU.mult, op1=ALU.add),
                         [K_("g"), "const"], [K_("b")])
                    if dr == 0:
                        src = sb_[dr]
                        srck = K_("b")
                        dcol = slice(127, 512, 128)
                    else:
                        tt("pool", sr[dr], sg_[dr], sb_[dr], ALU.subtract, [K_("g"), K_("b")], [K_("r")])
                        bbv = sb_[dr].rearrange("p (c t) -> p c t", t=128)
                        rrv = sr[dr].rearrange("p (c t) -> p c t", t=128)
                        tt("dve", rrv, rrv, bbv[:, :, 127:128].to_broadcast([128, 4, 128]), ALU.add, [K_("r"), K_("b")], [K_("r")])
                        src = sr[dr]
                        srck = K_("r")
                        dcol = slice(0, 512, 128)
                    act(se[dr], src, AF.Exp, [srck], [K_("e")])
                    tt("dve", qtT[dr][:, cols], pq[:, :], se[dr], ALU.mult, [pqk, K_("e")], [("qtT", dr, j)])
                    cp("pool", dec[dr][:, 4 * j:4 * j + 4], se[dr][:, dcol], [K_("e")], [("dec", dr)])
                    tt("pool", sL1[dr], sL1[dr], src, ALU.add, [K_("L1"), srck], [K_("L1")])
                    act(su[dr], sL1[dr], AF.Exp, [K_("L1")], [K_("u")], scale=-1.0)
                    ts("dve", ktT[dr][:, cols], su[dr], oml[:, dr * 8 + h: dr * 8 + h + 1], ALU.mult,
                       [K_("u"), "oml"], [("ktT", dr, j)])

            def gate_back(j):
                cols = slice(512 * j, 512 * (j + 1))
                for dr in range(2):
                    for cc in range(4):
                        ch = 4 * j + cc
                        tr(PT[:, (dr * 4 + cc) * 128:(dr * 4 + cc + 1) * 128], ktT[dr][:, ch * 128:(ch + 1) * 128],
                           [("ktT", dr, j)], ["PT"], track=(cc == 3 and dr == 1))
                ev = "act" if j % 2 == 0 else "dve"
                cp(ev, ktok[0][:, cols], PT[:, 0:512], ["PT"], [("ktok", 0, j)])
                cp(ev, ktok[1][:, cols], PT[:, 512:1024], ["PT"], [("ktok", 1, j)])
            gf = gate_front if DBG.get("wave", 1) else gate_front_old
            gf(0)
            for j in range(8):
                if j + 1 < 8:
                    gf(j + 1)
                gate_back(j)
            k.op("pool", lambda e: e.memset(Ubf[0][:, 0:128], 0.0), [("vH", 0)], [("U", 0, 0)])
            memset("pool", Ubf[1][:, 31 * 128:32 * 128], 0.0, [("U", 1, 31)])
            pslot = 0
            for step in range(32):
                for dr in range(2):
                    c = step if dr == 0 else 31 - step
                    pb = pslot % 4
                    pslot += 1
                    pP = P[pb][:, 0:128]
                    ppk = "ps%d" % pb
                    mm(pP, ktok[dr][:, c * 128:(c + 1) * 128], vH[:, c * 128:(c + 1) * 128], True, True,
                       [("ktok", dr, c // 4), ("vH", c // 8)], [ppk])
                    Wn = Wst[dr][step % 2]
                    Wo = Wst[dr][(step + 1) % 2]
                    if step == 0:
                        cp("dve", Wn, pP, [ppk], [("W", dr, step % 2)])
                    else:
                        cprev = c - 1 if dr == 0 else c + 1
                        dcl = dec[dr][:, cprev:cprev + 1]
                        if dr == 0:
                            k.op("act", lambda e, o=Ubf[dr][:, c * 128:(c + 1) * 128], i_=Wo, s_=dcl:
                                 e.activation(out=o, in_=i_, func=AF.Copy, scale=s_),
                                 [("W", dr, (step + 1) % 2), ("dec", dr)], [("U", dr, c)])
                        else:
                            ts("dve", Ubf[dr][:, c * 128:(c + 1) * 128], Wo, dcl, ALU.mult,
                               [("W", dr, (step + 1) % 2), ("dec", dr)], [("U", dr, c)])
                        if step < 31:
                            stt(Wn, Wo, dcl, pP, ALU.mult, ALU.add,
                                [("W", dr, (step + 1) % 2), ("dec", dr), ppk], [("W", dr, step % 2)])
            def out_front(c):
                j = c // 4
                ch = slice(c * 128, (c + 1) * 128)
                pA = P[2 + c % 2][:, 0:256]
                pAk = "ps%d" % (2 + c % 2)
                for dr in range(2):
                    mm(pA[:, dr * 128:(dr + 1) * 128], ktT[dr][:, ch], qtT[dr][:, ch], True, True,
                       [("ktT", dr, j), ("qtT", dr, j)], [pAk], track=(dr == 1))
                tt("dve", AT[c % 2].rearrange("p (a b) -> p a b", b=128), pA.rearrange("p (a b) -> p a b", b=128),
                   triH, ALU.mult, [pAk, "const"], [("AT", c % 2)])

            def out_back(c):
                j = c // 4
                cc = c % 4
                ch = slice(c * 128, (c + 1) * 128)
                pob = 4 + j % 2
                po = P[pob]
                pok = "ps%d" % pob
                oc = po[:, cc * 128:(cc + 1) * 128]
                mm(oc, vH[:, ch], AT[c % 2][:, 0:128], True, False, [("vH", c // 8), ("AT", c % 2)], [pok], track=False)
                mm(oc, vH[:, ch], AT[c % 2][:, 128:256], False, False, [("vH", c // 8), ("AT", c % 2)], [pok], track=False)
                mm(oc, Ubf[0][:, ch], qtT[0][:, ch], False, False, [("U", 0, c), ("qtT", 0, j)], [pok], track=False)
                mm(oc, Ubf[1][:, ch], qtT[1][:, ch], False, True, [("U", 1, c), ("qtT", 1, j)], [pok], track=True)
                if cc == 3:
                    cols = slice(512 * j, 512 * (j + 1))
                    act(sqh, po[:, :], AF.Square, [pok], ["sqh"])
                    mm(P[6][:, :], ones128, sqh, True, True, ["sqh", "const"], ["ps6"])
                    act(lnh, P[6][:, :], AF.Ln, ["ps6"], ["lnh"], scale=1.0 / 128, bias=EPS)
                    act(lnh, lnh, AF.Exp, ["lnh"], ["lnh"], scale=-0.5)
                    stt(on, po[:, :], hgw[:, 0:1], lnh, ALU.mult, ALU.mult, [pok, "lnh", "hgw"], ["lnh"])
                    tt("pool", obS[j % 2], on, gbT[:, cols], ALU.mult, ["lnh", ("gbT", j)], [("obS", j % 2)])
                    k.dma("sp", ob_d[h * 128:(h + 1) * 128, cols], obS[j % 2], reads=[("obS", j % 2)], sem="d_obS%d" % (j % 2))
            out_front(0)
            for c in range(32):
                if c + 1 < 32:
                    out_front(c + 1)
                out_back(c)
            k.barrier()
        k.barrier()
        A.reset(m0)

    if 4 in phases:
        m0 = A.mark()
        wZ = A.alloc([128, 8, 2048], BF16)
        wpa = A.alloc([128, 4, 1024], BF16)
        wpb = A.alloc([128, 8, 1024], BF16)
        wo = A.alloc([128, 8, 1024], BF16)
        oaj = [A.alloc([128, 4, 512], BF16) for _ in range(2)]
        obj = [A.alloc([128, 8, 512], BF16) for _ in range(2)]
        mg_ = A.alloc([128, 8, 512], BF16)
        sa = [A.alloc([128, 512], F32) for _ in range(2)]
        sb2 = [A.alloc([128, 512], F32) for _ in range(2)]
        ma = [A.alloc([128, 512], F32) for _ in range(2)]
        mb = [A.alloc([128, 512], F32) for _ in range(2)]
        xr = [A.alloc([128, D], F32) for _ in range(2)]
        for seg in range(4):
            k.dma("pool", wZ[:, :, seg * 512:(seg + 1) * 512], win_v[:, :, 10240 + seg * 512:10240 + (seg + 1) * 512],
                  writes=["wZ"], sem="d_wZ")
        k.dma("pool", wpa, wpa_d.rearrange("(c p) n -> p c n", p=128), writes=["wpa"])
        wpb_v = wpb_d.rearrange("(c p) n -> p c n", p=128)
        wo_v = wo_d.rearrange("(c p) n -> p c n", p=128)
        for hh in range(2):
            k.dma("pool", wpb[:, hh * 4:(hh + 1) * 4, :], wpb_v[:, hh * 4:(hh + 1) * 4, :], writes=["wpb"], sem="d_wpb")
            k.dma("pool", wo[:, hh * 4:(hh + 1) * 4, :], wo_v[:, hh * 4:(hh + 1) * 4, :], writes=["wo"], sem="d_wo")
        oa_v = oa_d.rearrange("(c p) t -> p c t", p=128)
        ob_v = ob_d.rearrange("(c p) t -> p c t", p=128)

        def load_ab(j_):
            cols_ = slice(512 * j_, 512 * (j_ + 1))
            k.dma("sp", oaj[j_ % 2], oa_v[:, :, cols_], writes=[("oaj", j_ % 2)])
            k.dma("sp", obj[j_ % 2], ob_v[:, :, cols_], writes=[("obj", j_ % 2)])
        load_ab(0)
        xcnt = 0
        ocnt = 0
        for j in range(8):
            sl = j % 2
            if j + 1 < 8:
                load_ab(j + 1)
            for dc in range(8):
                dcs = slice(dc * 128, (dc + 1) * 128)
                d2 = dc % 2
                pya, pyb = P[d2], P[2 + d2]
                pyak, pybk = "ps%d" % d2, "ps%d" % (2 + d2)
                for c in range(8):
                    mm(P[4][:, :], wZ[:, c, dcs], hT_nat(c, 512 * j, 512), c == 0, c == 7, ["wZ", "hT"], ["ps4"], track=(c == 7))
                for c in range(8):
                    mm(P[5][:, :], wZ[:, c, 1024 + dc * 128:1024 + (dc + 1) * 128], hT_nat(c, 512 * j, 512), c == 0, c == 7,
                       ["wZ", "hT"], ["ps5"], track=(c == 7))
                for c in range(4):
                    mm(pya[:, :], wpa[:, c, dcs], oaj[sl][:, c, :], c == 0, c == 3, ["wpa", ("oaj", sl)], [pyak], track=(c == 3))
                for c in range(8):
                    mm(pyb[:, :], wpb[:, c, dcs], obj[sl][:, c, :], c == 0, c == 7, ["wpb", ("obj", sl)], [pybk], track=(c == 7))
                act(sa[d2], P[4][:, :], AF.Sigmoid, ["ps4"], [("sa", d2)])
                act(sb2[d2], P[5][:, :], AF.Sigmoid, ["ps5"], [("sb2", d2)])
                tt("dve", ma[d2], pya[:, :], sa[d2], ALU.mult, [pyak, ("sa", d2)], [("ma", d2)])
                tt("dve", mb[d2], pyb[:, :], sb2[d2], ALU.mult, [pybk, ("sb2", d2)], [("mb", d2)])
                tt("pool", mg_[:, dc, :], ma[d2], mb[d2], ALU.add, [("ma", d2), ("mb", d2)], [("mg", dc)])
            for tq in range(4):
                t = 4 * j + tq
                xs = xcnt % 2
                xcnt += 1
                k.dma("sp", xr[xs], x_d[t * 128:(t + 1) * 128, :], writes=[("xr", xs)])
                for half in range(2):
                    pb = 4 + ocnt % 3
                    ocnt += 1
                    for c in range(8):
                        mm(P[pb][:, :], mg_[:, c, tq * 128:(tq + 1) * 128], wo[:, c, half * 512:(half + 1) * 512],
                           c == 0, c == 7, [("mg", c), "wo"], ["ps%d" % pb], track=(c == 7))
                    tt("dve", xr[xs][:, half * 512:(half + 1) * 512], P[pb][:, :], xr[xs][:, half * 512:(half + 1) * 512],
                       ALU.add, ["ps%d" % pb, ("xr", xs)], [("xr", xs)])
                k.dma("sp", out_d[t * 128:(t + 1) * 128, :], xr[xs], reads=[("xr", xs)], sem="d_ot%d" % xs)
    k.finish("sp")
    print("instructions:", k.ninstr, "arena peak:", A.peak, {e: k.count[e] for e in ENGS})
    k.emit()
    k.close()
    es.close()
    return nc


def _bias_tiles(rel_bias):
    a = np.arange(128)[:, None]
    c = np.arange(128)[None, :]
    out = np.zeros((128, 3, 8, 2, 128), np.float32)
    for g, (_, d) in enumerate(GROUPS):
        for ab in range(2):
            rel = (a - 64 - c) if ab == 0 else (a + 64 - c)
            rel = np.clip(rel, -64, 64)
            idx = t5_bucket(rel * d)
            for h in range(8):
                out[:, g, h, ab, :] = rel_bias[idx, g * 8 + h]
    return out.reshape(128, -1)


_NC_CACHE = {}


def kernel(x, norm_w, w_in, q_norm_w, k_norm_w, rel_bias, lb_fwd, lb_bwd, hg_norm_w,
           w_proj_a, w_proj_b, w_out, _debug=False, _phases=(1, 2, 3, 4)):
    x = np.asarray(x, np.float32)
    B = x.shape[0]
    key = (_debug, tuple(_phases))
    if key not in _NC_CACHE:
        _NC_CACHE[key] = build_program(_debug, _phases)
    nc = _NC_CACHE[key]
    qkw = np.zeros((128, 6), np.float32)
    for g in range(3):
        qkw[:, 2 * g] = np.tile(np.asarray(q_norm_w, np.float32)[0, g], 2)
        qkw[:, 2 * g + 1] = np.tile(np.asarray(k_norm_w, np.float32)[0, g], 2)
    lbf = np.ascontiguousarray(np.asarray(lb_fwd, np.float32).reshape(2, 8, 128).transpose(2, 0, 1).reshape(128, 16))
    lbb = np.ascontiguousarray(np.asarray(lb_bwd, np.float32).reshape(2, 8, 128).transpose(2, 0, 1).reshape(128, 16))
    common = {
        "w_in": np.ascontiguousarray(np.asarray(w_in, np.float32)[0]),
        "norm_w": np.ascontiguousarray(np.asarray(norm_w, np.float32)[0:1]),
        "qkw": qkw,
        "biasT": _bias_tiles(np.asarray(rel_bias, np.float32)),
        "lbf": lbf, "lbb": lbb,
        "hgw": np.ascontiguousarray(np.asarray(hg_norm_w, np.float32)[0].reshape(128, 1)),
        "w_proj_a": np.ascontiguousarray(np.asarray(w_proj_a, np.float32)[0]),
        "w_proj_b": np.ascontiguousarray(np.asarray(w_proj_b, np.float32)[0]),
        "w_out": np.ascontiguousarray(np.asarray(w_out, np.float32)[0]),
    }
    in_maps = []
    for b in range(B):
        m = dict(common)
        m["x"] = np.ascontiguousarray(x[b])
        in_maps.append(m)
    res = run_bass_kernel_spmd(nc, in_maps, core_ids=list(range(B)))
    if _debug:
        return res.results
    return np.stack([np.asarray(r["out"], np.float32) for r in res.results], axis=0)
```

```python
import numpy as np
from contextlib import ExitStack
import concourse.bass as bass
import concourse.mybir as mybir
from concourse.bass_utils import run_bass_kernel_spmd

F32 = mybir.dt.float32
BF16 = mybir.dt.bfloat16
U8 = mybir.dt.uint8
AF = mybir.ActivationFunctionType
ALU = mybir.AluOpType

S = 4096
D = 1024
PAD = 1024
EPS = 1e-6
GROUPS = ((128, 1), (512, 4), (2048, 16))
NUM_BUCKETS = 32
REL_MAX_DISTANCE = 1024
IN_COLS = 12288
ENGS = ("pe", "act", "dve", "pool", "sp")
DBG = {}


class KB:
    def __init__(self, nc):
        self.nc = nc
        self.prog = {e: [] for e in ENGS}
        self.count = {}
        self.sems = {}
        self.seen = {e: {} for e in ENGS}
        self.snap = {}
        self.last_w = {}
        self.readers = {}
        self.pending = {e: ([], []) for e in ENGS}
        self._ctx = []
        self.ninstr = 0
        for e in ENGS:
            self._mksem(e)

    def _mksem(self, name):
        cm = self.nc.semaphore("s_" + name)
        h = cm.__enter__()
        self._ctx.append(cm)
        self.sems[name] = h
        self.count[name] = 0
        return h

    def close(self):
        for cm in reversed(self._ctx):
            cm.__exit__(None, None, None)

    def _deps(self, e, reads, writes):
        deps = {}

        def add(cv):
            if cv is None:
                return
            c, v = cv
            if deps.get(c, 0) < v:
                deps[c] = v
        for r in reads:
            add(self.last_w.get(r))
        for w in writes:
            add(self.last_w.get(w))
            for rd in self.readers.get(w, ()):
                add(rd)
        waits = []
        seen = self.seen[e]
        for c, v in deps.items():
            if c == e and e == "pe":
                continue
            if seen.get(c, 0) < v:
                waits.append((c, v))
                seen[c] = v
                sn = self.snap.get((c, v))
                if sn:
                    for c2, v2 in sn.items():
                        if c2 != e and seen.get(c2, 0) < v2:
                            seen[c2] = v2
        return waits

    def _commit(self, token, reads, writes):
        for r in reads:
            self.readers.setdefault(r, []).append(token)
        for w in writes:
            self.last_w[w] = token
            self.readers[w] = []

    def op(self, e, fn, reads=(), writes=(), track=True):
        reads = tuple(reads)
        writes = tuple(writes)
        waits = self._deps(e, reads, writes)
        self.ninstr += 1
        if track:
            self.count[e] += 1
            token = (e, self.count[e])
            pr, pw = self.pending[e]
            self._commit(token, list(reads) + pr, list(writes) + pw)
            self.pending[e] = ([], [])
            self.snap[token] = dict(self.seen[e])
            self.prog[e].append((waits, fn, (e, 1)))
        else:
            pr, pw = self.pending[e]
            pr.extend(reads)
            pw.extend(writes)
            self.prog[e].append((waits, fn, None))

    def dma(self, q, out, in_, reads=(), writes=(), sem=None):
        reads = tuple(reads)
        writes = tuple(writes)
        if sem is None:
            sem = "d_" + str(writes[0] if writes else reads[0])
        if sem not in self.sems:
            self._mksem(sem)
        waits = self._deps(q, reads, writes)
        self.count[sem] += 16
        token = (sem, self.count[sem])
        self._commit(token, reads, writes)
        self.snap[token] = dict(self.seen[q])
        self.ninstr += 1

        def fn(eng, out=out, in_=in_):
            return eng.dma_start(out=out, in_=in_)
        self.prog[q].append((waits, fn, (sem, 16)))
        return token

    def barrier(self):
        for e in ENGS:
            waits = []
            for c, v in self.count.items():
                if v > 0 and self.seen[e].get(c, 0) < v:
                    if c == e and e in ("pe", "sp"):
                        continue
                    waits.append((c, v))
                    self.seen[e][c] = v
            if waits:
                self.prog[e].append((waits, None, None))
        self.last_w = {}
        self.readers = {}

    def finish(self, e="sp"):
        waits = []
        for c, v in self.count.items():
            if v > 0 and self.seen[e].get(c, 0) < v and c != e:
                waits.append((c, v))
        self.prog[e].append((waits, None, None))

    def emit(self):
        nc = self.nc
        with nc.Block() as block:
            for e in ENGS:
                items = self.prog[e]
                if not items:
                    continue

                def body(eng, items=items):
                    for waits, fn, inc in items:
                        for c, v in waits:
                            eng.wait_ge(self.sems[c], v)
                        if fn is None:
                            continue
                        ins = fn(eng)
                        if inc is not None:
                            ins.then_inc(self.sems[inc[0]], inc[1])
                reg = {"pe": block.tensor, "act": block.scalar, "dve": block.vector,
                       "pool": block.gpsimd, "sp": block.sync}[e]
                reg(body)


class Arena:
    def __init__(self, ap, nbytes):
        self.ap = ap
        self.nbytes = nbytes
        self.off = 0
        self.peak = 0

    def alloc(self, shape, dt):
        n = int(np.prod(shape[1:]))
        nb = n * (4 if dt == F32 else 2)
        nb = (nb + 63) // 64 * 64
        assert self.off + nb <= self.nbytes, ("SBUF arena overflow", self.off, nb)
        a = self.ap[:, self.off:self.off + nb].bitcast(dt)[:, 0:n]
        self.off += nb
        self.peak = max(self.peak, self.off)
        if len(shape) == 3:
            a = a.rearrange("p (a b) -> p a b", b=shape[2])
        return a

    def mark(self):
        return self.off

    def reset(self, m):
        self.off = m


def t5_bucket(rel):
    half = NUM_BUCKETS // 2
    max_exact = half // 2
    n = np.abs(rel)
    large = max_exact + (np.log(np.maximum(n, 1) / max_exact)
                         / np.log(REL_MAX_DISTANCE / max_exact) * (half - max_exact)).astype(np.int32)
    large = np.minimum(large, half - 1)
    return np.where(rel > 0, half, 0) + np.where(n < max_exact, n, large)


def build_program(debug=False, phases=(1, 2, 3, 4)):
    nc = bass.Bass("TRN2", target_bir_lowering=False)
    x_d = nc.dram_tensor("x", [S, D], F32, kind="ExternalInput").ap()
    watt_d = nc.dram_tensor("w_att", [12, 128, 8 * 384], F32, kind="ExternalInput").ap()
    wg_d = nc.dram_tensor("w_g", [128, 8 * 512], F32, kind="ExternalInput").ap()
    wh_d = nc.dram_tensor("w_h", [8, 128, 8 * 640], F32, kind="ExternalInput").ap()
    wz_d = nc.dram_tensor("w_z", [128, 8 * 2048], F32, kind="ExternalInput").ap()
    normw_d = nc.dram_tensor("norm_w", [1, D], F32, kind="ExternalInput").ap()
    qkw_d = nc.dram_tensor("qkw", [128, 6], F32, kind="ExternalInput").ap()
    bias_d = nc.dram_tensor("biasT", [128, 3 * 8 * 2 * 128], F32, kind="ExternalInput").ap()
    lbf_d = nc.dram_tensor("lbf", [128, 16], F32, kind="ExternalInput").ap()
    lbb_d = nc.dram_tensor("lbb", [128, 16], F32, kind="ExternalInput").ap()
    hgw_d = nc.dram_tensor("hgw", [128, 1], F32, kind="ExternalInput").ap()
    wpa_d = nc.dram_tensor("w_pa", [128, 4 * D], F32, kind="ExternalInput").ap()
    wpb_d = nc.dram_tensor("w_pb", [128, 8 * D], F32, kind="ExternalInput").ap()
    wo_d = nc.dram_tensor("w_o", [128, 8 * D], F32, kind="ExternalInput").ap()
    out_d = nc.dram_tensor("out", [S, D], F32, kind="ExternalOutput").ap()
    skind = "ExternalOutput" if debug else "Internal"
    oa_d = nc.dram_tensor("oa_scr", [512, S], BF16, kind=skind).ap()
    ob_d = nc.dram_tensor("ob_scr", [D, S], BF16, kind=skind).ap()
    if debug:
        hT_dbg = nc.dram_tensor("hT_dbg", [128, 8 * S], BF16, kind="ExternalOutput").ap()

    def flat(ap3):
        return ap3.rearrange("p c n -> p (c n)")

    es = ExitStack()
    ARENA_BYTES = 209920
    arena_t = es.enter_context(nc.sbuf_tensor("arena", [128, ARENA_BYTES], U8))
    A = Arena(arena_t, ARENA_BYTES)
    P = [es.enter_context(nc.psum_tensor("ps%d" % i, [128, 512], F32)) for i in range(7)]
    PT = es.enter_context(nc.psum_tensor("pst", [128, 1024], BF16))
    k = KB(nc)

    def mm(out, lhsT, rhs, start, stop, reads, writes, track=True, tp=None):
        def fn(e):
            if tp is None:
                return e.matmul(out, lhsT=lhsT, rhs=rhs, start=start, stop=stop)
            return e.matmul(out, lhsT=lhsT, rhs=rhs, start=start, stop=stop, tile_position=tp)
        k.op("pe", fn, reads, writes, track)

    def tr(out, in_, reads, writes, track=True):
        k.op("pe", lambda e: e.transpose(out, in_, ident), tuple(reads) + ("const",), writes, track)

    def act(out, in_, func, reads, writes, scale=None, bias=None, accum=None):
        kw = {}
        if scale is not None:
            kw["scale"] = scale
        if bias is not None:
            kw["bias"] = bias
        if accum is not None:
            kw["accum_out"] = accum
        k.op("act", lambda e: e.activation(out=out, in_=in_, func=func, **kw), reads, writes)

    def tt(eng, out, in0, in1, op, reads, writes):
        k.op(eng, lambda e: e.tensor_tensor(out=out, in0=in0, in1=in1, op=op), reads, writes)

    def ts(eng, out, in0, s1, op0, reads, writes, s2=None, op1=None):
        if op1 is None:
            k.op(eng, lambda e: e.tensor_scalar(out=out, in0=in0, scalar1=s1, scalar2=None, op0=op0), reads, writes)
        else:
            k.op(eng, lambda e: e.tensor_scalar(out=out, in0=in0, scalar1=s1, scalar2=s2, op0=op0, op1=op1), reads, writes)

    def stt(out, in0, scalar, in1, op0, op1, reads, writes):
        k.op("dve", lambda e: e.scalar_tensor_tensor(out=out, in0=in0, scalar=scalar, in1=in1, op0=op0, op1=op1),
             reads, writes)

    def cp(eng, out, in_, reads, writes):
        if eng == "act":
            k.op("act", lambda e: e.copy(out=out, in_=in_), reads, writes)
        else:
            k.op(eng, lambda e: e.tensor_copy(out=out, in_=in_), reads, writes)

    def recip(out, in_, reads, writes):
        k.op("dve", lambda e: e.reciprocal(out=out, in_=in_), reads, writes)

    def memset(eng, ap, val, writes):
        k.op(eng, lambda e: e.memset(ap, val), (), writes)

    hTf = A.alloc([128, PAD + 8 * S + PAD], BF16)
    ident = A.alloc([128, 128], BF16)
    ones_bd = A.alloc([128, 128], BF16)
    ones128 = A.alloc([128, 128], BF16)
    tri4 = A.alloc([128, 3, 128], F32)
    scanmask = A.alloc([128, 512], F32)
    qkw = A.alloc([128, 6], F32)
    lbt = A.alloc([128, 32], F32)
    lbd = A.alloc([128, 16], F32)
    oml = A.alloc([128, 16], F32)
    hgw = A.alloc([128, 1], F32)
    persist_mark = A.mark()

    def hT_nat(c, t0, n):
        o = PAD + c * S + t0
        return hTf[:, o:o + n]

    def hT_perm(c, d, n0, cnt):
        L = S // d
        r0 = n0 // L
        l0 = n0 % L
        base = PAD + c * S
        if d == 1:
            return hTf[:, base + n0: base + n0 + cnt], None
        if cnt <= L - l0:
            st = base + l0 * d + r0
            return hTf[:, st: st + (cnt - 1) * d + 1: d], None
        assert l0 == 0 and cnt % L == 0
        nres = cnt // L
        st = base + r0
        v = hTf[:, st: st + L * d].rearrange("p (l r) -> p r l", r=d)[:, 0:nres, :]
        return v, L

    def hT_vtile(c, d, r, i):
        st = PAD + c * S + (128 * i - 64) * d + r
        if d == 1:
            return hTf[:, st: st + 128]
        return hTf[:, st: st + 127 * d + 1: d]

    def acc_perm(acc, d, n0, cnt):
        L = S // d
        r0 = n0 // L
        l0 = n0 % L
        if d == 1:
            return acc[:, n0:n0 + cnt], None
        if cnt <= L - l0:
            st = l0 * d + r0
            return acc[:, st: st + (cnt - 1) * d + 1: d], None
        assert l0 == 0 and cnt % L == 0
        nres = cnt // L
        v = acc.rearrange("p (l r) -> p r l", r=d)[:, r0:r0 + nres, :]
        return v, L

    memset("pool", hTf[:, 0:PAD], 0.0, ["hTpad"])
    memset("pool", hTf[:, PAD + 8 * S: PAD + 8 * S + PAD], 0.0, ["hTpad"])
    memset("pool", ident, 0.0, ["const"])
    k.op("pool", lambda e: e.affine_select(out=ident, in_=ident, pattern=[[-1, 128]], compare_op=ALU.not_equal,
                                           fill=1.0, base=0, channel_multiplier=1), ["const"], ["const"])
    memset("pool", ones_bd, 0.0, ["const"])
    memset("pool", ones_bd[0:64, 0:64], 1.0, ["const"])
    memset("pool", ones_bd[64:128, 64:128], 1.0, ["const"])
    memset("pool", ones128, 1.0, ["const"])
    memset("pool", tri4, 1.0, ["const"])
    for idx in (0, 2):
        k.op("pool", lambda e, idx=idx: e.affine_select(out=tri4[:, idx, :], in_=tri4[:, idx, :], pattern=[[-1, 128]],
                                                        compare_op=ALU.is_ge, fill=0.0, base=0, channel_multiplier=1),
             ["const"], ["const"])
    k.op("pool", lambda e: e.affine_select(out=tri4[:, 1, :], in_=tri4[:, 1, :], pattern=[[1, 128]],
                                           compare_op=ALU.is_ge, fill=0.0, base=0, channel_multiplier=-1),
         ["const"], ["const"])
    memset("pool", scanmask, 1.0, ["const"])
    memset("pool", scanmask.rearrange("p (c t) -> p c t", t=128)[:, :, 0:1], 0.0, ["const"])
    k.dma("sp", qkw, qkw_d, writes=["qkw"])
    k.dma("sp", lbt[:, 0:16], lbf_d, writes=["lbt"], sem="d_lbt")
    k.dma("sp", lbt[:, 16:32], lbb_d, writes=["lbt"], sem="d_lbt")
    k.dma("sp", hgw, hgw_d, writes=["hgw"])
    for g in range(3):
        ts("dve", qkw[:, 2 * g + 1:2 * g + 2], qkw[:, 2 * g + 1:2 * g + 2], 8.0, ALU.mult, ["qkw"], ["qkw"])
    lbv = lbt.rearrange("p (d l h) -> p d l h", d=2, l=2)
    for dd in range(2):
        tt("dve", lbd[:, dd * 8:(dd + 1) * 8], lbv[:, dd, 0, :], lbv[:, dd, 1, :], ALU.subtract, ["lbt"], ["lbd"])
    act(oml, lbd, AF.Sigmoid, ["lbd"], ["oml"], scale=-1.0)

    if 1 in phases:
        m0 = A.mark()
        normw_rep = A.alloc([128, D], F32)
        xts = [A.alloc([128, D], F32) for _ in range(2)]
        xn = [A.alloc([128, D], BF16) for _ in range(2)]
        junk = A.alloc([128, D], BF16)
        ssq = A.alloc([128, 32], F32)
        rstd = A.alloc([128, 32], F32)
        k.dma("sp", normw_rep, normw_d.partition_broadcast(128), writes=["normw"])
        memset("pool", ssq, 0.0, [("ssq", t) for t in range(32)])
        hT_ct = hTf[:, PAD:PAD + 8 * S].rearrange("p (c t) -> p c t", t=S)
        for t in range(32):
            sl = t % 2
            k.dma("sp" if sl == 0 else "act", xts[sl], x_d[t * 128:(t + 1) * 128, :], writes=[("xt", sl)])
            act(junk, xts[sl], AF.Square, [("xt", sl)], ["junk", ("ssq", t)], accum=ssq[:, t:t + 1])
            ts("dve", rstd[:, t:t + 1], ssq[:, t:t + 1], 1.0 / D, ALU.mult, [("ssq", t)], [("rstd", t)], s2=EPS, op1=ALU.add)
            act(rstd[:, t:t + 1], rstd[:, t:t + 1], AF.Sqrt, [("rstd", t)], [("rstd", t)])
            recip(rstd[:, t:t + 1], rstd[:, t:t + 1], [("rstd", t)], [("rstd", t)])
            stt(xn[sl], xts[sl], rstd[:, t:t + 1], normw_rep, ALU.mult, ALU.mult,
                [("xt", sl), ("rstd", t), "normw"], [("xn", sl)])
            for c in range(8):
                tr(PT[:, c * 128:(c + 1) * 128], xn[sl][:, c * 128:(c + 1) * 128], [("xn", sl)], ["PT"], track=(c == 7))
            cp("act" if t % 2 == 0 else "dve", hT_ct[:, :, t * 128:(t + 1) * 128],
               PT[:, :].rearrange("p (c t) -> p c t", t=128), ["PT"], [("hT", t)])
        k.barrier()
        A.reset(m0)
        if debug:
            k.dma("sp", hT_dbg, hTf[:, PAD:PAD + 8 * S], reads=["x"], sem="dbg")

    if 2 in phases:
        m0 = A.mark()
        wA = [A.alloc([128, 8, 384], BF16) for _ in range(2)]
        wG = A.alloc([128, 8, 512], BF16)
        qn = A.alloc([128, S], BF16)
        knp = A.alloc([128, 64 + S + 64], BF16)
        vTp = A.alloc([128, 64 + S + 64], BF16)
        vP = A.alloc([128, 33 * 256], BF16)
        acc = [A.alloc([128, S], F32) for _ in range(2)]
        bM = A.alloc([128, 512], F32)
        Mrep = A.alloc([128, 1024], F32)
        sq = [[A.alloc([128, 512], BF16) for _ in range(2)] for _ in range(2)]
        lnb = [A.alloc([128, 512], F32) for _ in range(2)]
        ee = [A.alloc([128, 512], F32) for _ in range(3)]
        pT = [A.alloc([128, 512], BF16) for _ in range(3)]
        sgh = [A.alloc([128, 512], F32) for _ in range(2)]
        rD = [A.alloc([128, 512], F32) for _ in range(2)]
        t1 = [A.alloc([128, 512], F32) for _ in range(2)]
        oaS = [A.alloc([128, 512], BF16) for _ in range(4)]
        vP4 = vP.rearrange("p (t h d) -> p t h d", h=2, d=128)
        memset("pool", knp[:, 0:64], 0.0, [("kn", -1)])
        memset("pool", knp[:, 64 + S:64 + S + 64], 0.0, [("kn", 8)])
        memset("pool", vTp[:, 0:64], 0.0, [("vT", -1)])
        memset("pool", vTp[:, 64 + S:64 + S + 64], 0.0, [("vT", 8)])
        memset("pool", vP4[:, :, :, 64:128], 1.0, ["vPones"])
        k.dma("pool", flat(wG), wg_d, writes=["wG"], sem="d_wG")
        maskAB = tri4[:, 0:2, :]
        units = [(hp, g) for hp in range(DBG.get("nhp", 4)) for g in range(DBG.get("ng", 3))]

        def load_unit_w(u):
            hp_, g_ = units[u]
            slot_ = u % 2
            k.dma("pool", flat(wA[slot_]), watt_d[hp_ * 3 + g_], writes=[("wA", slot_)], sem="d_wA%d" % slot_)
        load_unit_w(0)
        sccnt = 0
        oacnt = 0
        for u, (hp, g) in enumerate(units):
            win_, d = GROUPS[g]
            L = S // d
            TPS = L // 128
            slot = u % 2
            w = wA[slot]
            if u + 1 < len(units):
                load_unit_w(u + 1)
            bcol = (g * 8 + 2 * hp) * 256
            k.dma("sp", bM, bias_d[:, bcol:bcol + 512], writes=["bM"])
            act(bM, bM, AF.Exp, ["bM"], ["bM"])
            bMv = bM.rearrange("p (h a c) -> p h a c", h=2, a=2)
            Mv = Mrep.rearrange("p (h m a c) -> p h m a c", h=2, m=2, a=2)
            for hl in range(2):
                for mi in range(2):
                    tt("pool", Mv[:, hl, mi], bMv[:, hl], maskAB, ALU.mult, ["bM", "const"], ["Mrep"])

            nl = 512 // d

            def pview(ap_nat):
                if d == 1:
                    return ap_nat
                return ap_nat.rearrange("p (l r) -> p r l", r=d)

            def dview(buf, off, j):
                if d == 1:
                    return buf[:, off + 512 * j: off + 512 * (j + 1)]
                return buf[:, off:off + S].rearrange("p (r l) -> p r l", l=L)[:, :, nl * j: nl * (j + 1)]

            def wkeys(nm, j):
                return [(nm, j)] if d == 1 else [(nm, jj) for jj in range(8)]

            def proj_front(j):
                pqb, pkb, pvb = j % 2, 2 + j % 2, 4
                for (pb, woff) in ((pqb, 0), (pkb, 128), (pvb, 256)):
                    for c in range(8):
                        mm(P[pb][:, :], w[:, c, woff:woff + 128], hT_nat(c, 512 * j, 512), c == 0, c == 7,
                           [("wA", slot), "hT"], ["ps%d" % pb], track=(c == 7))
                act(sq[0][j % 2], P[pqb][:, :], AF.Square, ["ps%d" % pqb], [("sq", 0, j % 2)])
                act(sq[1][j % 2], P[pkb][:, :], AF.Square, ["ps%d" % pkb], [("sq", 1, j % 2)])
                cp("dve", dview(vTp, 64, j), pview(P[pvb][:, :]), ["ps%d" % pvb], wkeys("vT", j))

            def proj_back(j):
                pqb, pkb = j % 2, 2 + j % 2
                items = ((pqb, 2 * g, dview(qn, 0, j), wkeys("qn", j)),
                         (pkb, 2 * g + 1, dview(knp, 64, j), wkeys("kn", j)))
                for qi in range(2):
                    mm(P[5 + qi][:, :], ones_bd, sq[qi][j % 2], True, True, [("sq", qi, j % 2), "const"], ["ps%d" % (5 + qi)])
                for qi in range(2):
                    act(lnb[qi], P[5 + qi][:, :], AF.Ln, ["ps%d" % (5 + qi)], [("lnb", qi)], bias=64.0 * EPS)
                for qi in range(2):
                    act(lnb[qi], lnb[qi], AF.Exp, [("lnb", qi)], [("lnb", qi)], scale=-0.5)
                for qi, (pb, wcol, dst, dkeys) in enumerate(items):
                    stt(dst, pview(P[pb][:, :]), qkw[:, wcol:wcol + 1], pview(lnb[qi]), ALU.mult, ALU.mult,
                        ["ps%d" % pb, ("lnb", qi), "qkw"], dkeys)
            if "q" in DBG.get("st", "qvaf"):
                proj_front(0)
                for j in range(8):
                    if j + 1 < 8:
                        proj_front(j + 1)
                    proj_back(j)
                for t0_ in range(0, 33, 8):
                    nb = min(8, 33 - t0_)
                    for s_ in range(nb):
                        vt = t0_ + s_
                        lo = (128 * vt - 64) // 512 if vt > 0 else -1
                        hi = min((128 * vt + 63) // 512, 8)
                        tr(PT[:, s_ * 128:(s_ + 1) * 128], vTp[:, 128 * vt:128 * vt + 128],
                           [("vT", lo), ("vT", hi)], ["PT"], track=(s_ == nb - 1))
                    cp("act", vP4[:, t0_:t0_ + nb, :, 0:64],
                       PT[:, 0:nb * 128].rearrange("p (t h d) -> p t h d", h=2, d=64), ["PT"], [("vP", t0_ // 8)])

            its = [(mg, mp, hl) for mg in range(8) for mp in range(2) for hl in range(2)]
            state = {}

            def att_front(ii):
                nonlocal sccnt
                mg, mp, hl = its[ii]
                rows = slice(hl * 64, hl * 64 + 64)
                sb_ = sccnt % 3
                sl2 = sccnt % 3
                sccnt += 1
                state[ii] = (sb_, sl2)
                psc = P[sb_]
                psck = "ps%d" % sb_
                for mi in range(2):
                    m = mg * 4 + mp * 2 + mi
                    jq = m // 4
                    for ab in range(2):
                        kst = 128 * m + 128 * ab
                        kj = [("kn", (kst - 64) // 512 if kst >= 64 else -1), ("kn", min((kst + 127 - 64) // 512, 8))]
                        mm(psc[:, (mi * 2 + ab) * 128:(mi * 2 + ab + 1) * 128],
                           knp[rows, kst:kst + 128], qn[rows, 128 * m:128 * m + 128], True, True,
                           [("qn", jq)] + kj, [psck], track=(mi == 1 and ab == 1), tp=(hl * 64, 0))
                act(ee[sl2], psc[:, :], AF.Exp, [psck], [("ee", sl2)])
                tt("dve" if ii % 2 == 0 else "pool", pT[sl2], ee[sl2], Mrep[:, hl * 512:(hl + 1) * 512], ALU.mult,
                   [("ee", sl2), "Mrep"], [("pT", sl2)])

            def att_back(ii):
                mg, mp, hl = its[ii]
                sb_, sl2 = state[ii]
                pb = 3 + (mg % 2) * 2 + hl
                pnd = P[pb]
                pndk = "ps%d" % pb
                for mi in range(2):
                    m = mg * 4 + mp * 2 + mi
                    n0 = 128 * m
                    i = (n0 % L) // 128
                    first = (i == 0)
                    last = (i == TPS - 1)
                    vA = m
                    vB = m + 1
                    col = (mp * 2 + mi) * 128
                    KA = slice(64, 128) if first else slice(0, 128)
                    KBs = slice(0, 64) if last else slice(0, 128)
                    vkeys = [("vP", vA // 8), ("vP", vB // 8), "vPones"]
                    mm(pnd[:, col:col + 128], vP4[KA, vA, hl, :],
                       pT[sl2][KA, (mi * 2) * 128:(mi * 2 + 1) * 128], True, False,
                       [("pT", sl2)] + vkeys, [pndk], track=False, tp=(KA.start, 0))
                    mm(pnd[:, col:col + 128], vP4[KBs, vB, hl, :],
                       pT[sl2][KBs, (mi * 2 + 1) * 128:(mi * 2 + 2) * 128], False, True,
                       [("pT", sl2)] + vkeys, [pndk], track=(mi == 1), tp=(0, 0))
                if mp == 1:
                    av, inner = acc_perm(acc[hl], d, 512 * mg, 512)
                    pv_ = pnd[:, :] if inner is None else pnd[:, :].rearrange("p (a b) -> p a b", b=inner)
                    akey = ("acc", hl)
                    if g == 0:
                        cp("act", av, pv_, [pndk], [akey])
                    else:
                        tt("dve", av, av, pv_, ALU.add, [pndk, akey], [akey])
            if "a" in DBG.get("st", "qvaf"):
                LA = 2
                for ii in range(min(LA, len(its))):
                    att_front(ii)
                for ii in range(len(its)):
                    if ii + LA < len(its):
                        att_front(ii + LA)
                    att_back(ii)

            if g == len([1 for (hp_, g_) in units if hp_ == hp]) - 1 and "f" in DBG.get("st", "qvaf"):
                for j in range(8):
                    pg = P[j % 3]
                    pgk = "ps%d" % (j % 3)
                    for c in range(8):
                        mm(pg[:, :], wG[:, c, hp * 128:(hp + 1) * 128], hT_nat(c, 512 * j, 512), c == 0, c == 7,
                           ["wG", "hT"], [pgk], track=(c == 7))
                    HL = (0, 1)
                    for hl in HL:
                        act(sgh[hl][0:64, :], pg[hl * 64:(hl + 1) * 64, :], AF.Silu, [pgk], [("sgh", hl)])
                    for hl in HL:
                        act(rD[hl][0:64, :], acc[hl][64:128, 512 * j:512 * (j + 1)], AF.Ln, [("acc", hl)], [("rD", hl)])
                    for hl in HL:
                        act(rD[hl][0:64, :], rD[hl][0:64, :], AF.Exp, [("rD", hl)], [("rD", hl)], scale=-1.0)
                    for hl in HL:
                        tt("pool", t1[hl][0:64, :], acc[hl][0:64, 512 * j:512 * (j + 1)], rD[hl][0:64, :], ALU.mult,
                           [("acc", hl), ("rD", hl)], [("t1", hl)])
                    for hl in HL:
                        os_ = oacnt % 4
                        oacnt += 1
                        tt("dve", oaS[os_][0:64, :], t1[hl][0:64, :], sgh[hl][0:64, :], ALU.mult,
                           [("t1", hl), ("sgh", hl)], [("oaS", os_)])
                        k.dma("sp", oa_d[(2 * hp + hl) * 64:(2 * hp + hl + 1) * 64, 512 * j:512 * (j + 1)], oaS[os_][0:64, :],
                              reads=[("oaS", os_)], sem="d_oaS%d" % os_)
        k.barrier()
        A.reset(m0)

    if 3 in phases:
        m0 = A.mark()
        wH = [A.alloc([128, 8, 640], BF16) for _ in range(2)]
        vH = A.alloc([128, S], BF16)
        gbT = A.alloc([128, S], BF16)
        qtT = [A.alloc([128, S], BF16) for _ in range(2)]
        ktT = [A.alloc([128, S], BF16) for _ in range(2)]
        ktok = [A.alloc([128, S], BF16) for _ in range(2)]
        dec = [A.alloc([128, 32], F32) for _ in range(2)]
        Ubf = [A.alloc([128, S], BF16) for _ in range(2)]
        iT = Ubf[0]
        Wst = [[A.alloc([128, 128], F32) for _ in range(2)] for _ in range(2)]
        su = [A.alloc([128, 512], F32) for _ in range(2)]
        sL1 = [A.alloc([128, 512], F32) for _ in range(2)]
        sg_ = [A.alloc([128, 512], F32) for _ in range(2)]
        sb_ = [A.alloc([128, 512], F32) for _ in range(2)]
        sr = [A.alloc([128, 512], F32) for _ in range(2)]
        se = [A.alloc([128, 512], F32) for _ in range(2)]
        AT = [A.alloc([128, 256], BF16) for _ in range(2)]
        sqh = A.alloc([128, 512], BF16)
        lnh = A.alloc([128, 512], F32)
        on = lnh
        obS = [A.alloc([128, 512], BF16) for _ in range(2)]
        lbv2 = A.alloc([128, 16], F32)
        act(lbv2, lbd, AF.Sigmoid, ["lbd"], ["lbv2"])
        triH = tri4[:, 1:3, :]
        NH = DBG.get("nh", 8)

        def load_wH(h_):
            k.dma("pool", flat(wH[h_ % 2]), wh_d[h_], writes=[("wH", h_ % 2)], sem="d_wH%d" % (h_ % 2))
        if DBG.get("pref", 1):
            load_wH(0)
        for h in range(NH):
            hs = h % 2
            w = wH[hs]
            if DBG.get("pref", 1):
                if h + 1 < NH:
                    load_wH(h + 1)
            else:
                load_wH(h)
            for j in range(8):
                cols = slice(512 * j, 512 * (j + 1))
                pb = j % 2
                for c in range(8):
                    mm(P[pb][:, :], w[:, c, 512:640], hT_nat(c, 512 * j, 512), c == 0, c == 7,
                       [("wH", hs), "hT"], ["ps%d" % pb], track=(c == 7))
                act(gbT[:, cols], P[pb][:, :], AF.Silu, ["ps%d" % pb], [("gbT", j)])
            for j in range(8):
                cols = slice(512 * j, 512 * (j + 1))
                pb = 2 + j % 2
                for c in range(8):
                    mm(P[pb][:, :], w[:, c, 384:512], hT_nat(c, 512 * j, 512), c == 0, c == 7,
                       [("wH", hs), "hT"], ["ps%d" % pb], track=(c == 7))
                cp("dve", iT[:, cols], P[pb][:, :], ["ps%d" % pb], [("iT", j)])
            for t0_ in range(0, 32, 8):
                for s_ in range(8):
                    t = t0_ + s_
                    tr(PT[:, s_ * 128:(s_ + 1) * 128], iT[:, t * 128:(t + 1) * 128], [("iT", t // 4)], ["PT"], track=(s_ == 7))
                cp("act" if (t0_ // 8) % 2 == 0 else "dve", vH[:, t0_ * 128:(t0_ + 8) * 128], PT[:, :], ["PT"], [("vH", t0_ // 8)])

            def gate_front(j):
                for (pi_, woff) in ((0, 0), (1, 128), (2, 256)):
                    pb = pi_ * 2 + j % 2
                    for c in range(8):
                        mm(P[pb][:, :], w[:, c, woff:woff + 128], hT_nat(c, 512 * j, 512), c == 0, c == 7,
                           [("wH", hs), "hT"], ["ps%d" % pb], track=(c == 7))
                cols = slice(512 * j, 512 * (j + 1))
                pqk = "ps%d" % (j % 2)
                pq = P[j % 2]
                DR = (0, 1)

                def K_(nm, dr):
                    return (nm, dr)
                src = [sb_[0], sr[1]]
                srck = [K_("b", 0), K_("r", 1)]
                dcol = [slice(127, 512, 128), slice(0, 512, 128)]
                for dr in DR:
                    pzb_ = (1 + dr) * 2 + j % 2
                    act(su[dr], P[pzb_][:, :], AF.Exp, ["ps%d" % pzb_], [K_("u", dr)])
                for dr in DR:
                    act(sL1[dr], su[dr], AF.Ln, [K_("u", dr)], [K_("L1", dr)], bias=1.0)
                for dr in DR:
                    act(sg_[dr], su[dr], AF.Ln, [K_("u", dr)], [K_("g", dr)], bias=lbv2[:, dr * 8 + h: dr * 8 + h + 1])
                for dr in DR:
                    tt("pool", sg_[dr], sg_[dr], sL1[dr], ALU.subtract, [K_("g", dr), K_("L1", dr)], [K_("g", dr)])
                for dr in DR:
                    k.op("dve", lambda e, o=sb_[dr], g_=sg_[dr]: e.tensor_tensor_scan(out=o, data0=scanmask, data1=g_, initial=0.0,
                                                                                op0=ALU.mult, op1=ALU.add),
                         [K_("g", dr), "const"], [K_("b", dr)])
                tt("pool", sr[1], sg_[1], sb_[1], ALU.subtract, [K_("g", 1), K_("b", 1)], [K_("r", 1)])
                bbv = sb_[1].rearrange("p (c t) -> p c t", t=128)
                rrv = sr[1].rearrange("p (c t) -> p c t", t=128)
                tt("dve", rrv, rrv, bbv[:, :, 127:128].to_broadcast([128, 4, 128]), ALU.add, [K_("r", 1), K_("b", 1)], [K_("r", 1)])
                for dr in DR:
                    act(se[dr], src[dr], AF.Exp, [srck[dr]], [K_("e", dr)])
                for dr in DR:
                    tt("dve", qtT[dr][:, cols], pq[:, :], se[dr], ALU.mult, [pqk, K_("e", dr)], [("qtT", dr, j)])
                    cp(DBG.get("deceng", "pool"), dec[dr][:, 4 * j:4 * j + 4], se[dr][:, dcol[dr]], [K_("e", dr)], [("dec", dr)])
                for dr in DR:
                    tt("pool", sL1[dr], sL1[dr], src[dr], ALU.add, [K_("L1", dr), srck[dr]], [K_("L1", dr)])
                for dr in DR:
                    act(su[dr], sL1[dr], AF.Exp, [K_("L1", dr)], [K_("u", dr)], scale=-1.0)
                for dr in DR:
                    ts("dve", ktT[dr][:, cols], su[dr], oml[:, dr * 8 + h: dr * 8 + h + 1], ALU.mult,
                       [K_("u", dr), "oml"], [("ktT", dr, j)])

            def gate_front_old(j):
                for (pi_, woff) in ((0, 0), (1, 128), (2, 256)):
                    pb = pi_ * 2 + j % 2
                    for c in range(8):
                        mm(P[pb][:, :], w[:, c, woff:woff + 128], hT_nat(c, 512 * j, 512), c == 0, c == 7,
                           [("wH", hs), "hT"], ["ps%d" % pb], track=(c == 7))
                cols = slice(512 * j, 512 * (j + 1))
                pqk = "ps%d" % (j % 2)
                pq = P[j % 2]
                for dr in range(2):
                    pzb_ = (1 + dr) * 2 + j % 2
                    pz = P[pzb_]
                    pzk = "ps%d" % pzb_
                    K_ = lambda nm: (nm, dr)
                    act(su[dr], pz[:, :], AF.Exp, [pzk], [K_("u")])
                    act(sL1[dr], su[dr], AF.Ln, [K_("u")], [K_("L1")], bias=1.0)
                    act(sg_[dr], su[dr], AF.Ln, [K_("u")], [K_("g")], bias=lbv2[:, dr * 8 + h: dr * 8 + h + 1])
                    tt("pool", sg_[dr], sg_[dr], sL1[dr], ALU.subtract, [K_("g"), K_("L1")], [K_("g")])
                    k.op("dve", lambda e, o=sb_[dr], g_=sg_[dr]: e.tensor_tensor_scan(out=o, data0=scanmask, data1=g_, initial=0.0,
                                                                                op0=ALU.mult, op1=ALU.add),
                         [K_("g"), "const"], [K_("b")])
                    if dr == 0:
                        src = sb_[dr]
                        srck = K_("b")
                        dcol = slice(127, 512, 128)
                    else:
                        tt("pool", sr[dr], sg_[dr], sb_[dr], ALU.subtract, [K_("g"), K_("b")], [K_("r")])
                        bbv = sb_[dr].rearrange("p (c t) -> p c t", t=128)
                        rrv = sr[dr].rearrange("p (c t) -> p c t", t=128)
                        tt("dve", rrv, rrv, bbv[:, :, 127:128].to_broadcast([128, 4, 128]), ALU.add, [K_("r"), K_("b")], [K_("r")])
                        src = sr[dr]
                        srck = K_("r")
                        dcol = slice(0, 512, 128)
                    act(se[dr], src, AF.Exp, [srck], [K_("e")])
                    tt("dve", qtT[dr][:, cols], pq[:, :], se[dr], ALU.mult, [pqk, K_("e")], [("qtT", dr, j)])
                    cp("pool", dec[dr][:, 4 * j:4 * j + 4], se[dr][:, dcol], [K_("e")], [("dec", dr)])
                    tt("pool", sL1[dr], sL1[dr], src, ALU.add, [K_("L1"), srck], [K_("L1")])
                    act(su[dr], sL1[dr], AF.Exp, [K_("L1")], [K_("u")], scale=-1.0)
                    ts("dve", ktT[dr][:, cols], su[dr], oml[:, dr * 8 + h: dr * 8 + h + 1], ALU.mult,
                       [K_("u"), "oml"], [("ktT", dr, j)])

            def gate_back(j):
                cols = slice(512 * j, 512 * (j + 1))
                for dr in range(2):
                    for cc in range(4):
                        ch = 4 * j + cc
                        tr(PT[:, (dr * 4 + cc) * 128:(dr * 4 + cc + 1) * 128], ktT[dr][:, ch * 128:(ch + 1) * 128],
                           [("ktT", dr, j)], ["PT"], track=(cc == 3 and dr == 1))
                ev = "act" if j % 2 == 0 else "dve"
                cp(ev, ktok[0][:, cols], PT[:, 0:512], ["PT"], [("ktok", 0, j)])
                cp(ev, ktok[1][:, cols], PT[:, 512:1024], ["PT"], [("ktok", 1, j)])
            gf = gate_front if DBG.get("wave", 1) else gate_front_old
            gf(0)
            for j in range(8):
                if j + 1 < 8:
                    gf(j + 1)
                gate_back(j)
            k.op("pool", lambda e: e.memset(Ubf[0][:, 0:128], 0.0), [("vH", 0)], [("U", 0, 0)])
            memset("pool", Ubf[1][:, 31 * 128:32 * 128], 0.0, [("U", 1, 31)])
            pslot = 0
            for step in range(32):
                for dr in range(2):
                    c = step if dr == 0 else 31 - step
                    pb = pslot % 4
                    pslot += 1
                    pP = P[pb][:, 0:128]
                    ppk = "ps%d" % pb
                    mm(pP, ktok[dr][:, c * 128:(c + 1) * 128], vH[:, c * 128:(c + 1) * 128], True, True,
                       [("ktok", dr, c // 4), ("vH", c // 8)], [ppk])
                    Wn = Wst[dr][step % 2]
                    Wo = Wst[dr][(step + 1) % 2]
                    if step == 0:
                        cp("dve", Wn, pP, [ppk], [("W", dr, step % 2)])
                    else:
                        cprev = c - 1 if dr == 0 else c + 1
                        dcl = dec[dr][:, cprev:cprev + 1]
                        if dr == 0:
                            k.op("act", lambda e, o=Ubf[dr][:, c * 128:(c + 1) * 128], i_=Wo, s_=dcl:
                                 e.activation(out=o, in_=i_, func=AF.Copy, scale=s_),
                                 [("W", dr, (step + 1) % 2), ("dec", dr)], [("U", dr, c)])
                        else:
                            ts("dve", Ubf[dr][:, c * 128:(c + 1) * 128], Wo, dcl, ALU.mult,
                               [("W", dr, (step + 1) % 2), ("dec", dr)], [("U", dr, c)])
                        if step < 31:
                            stt(Wn, Wo, dcl, pP, ALU.mult, ALU.add,
                                [("W", dr, (step + 1) % 2), ("dec", dr), ppk], [("W", dr, step % 2)])
            def out_front(c):
                j = c // 4
                ch = slice(c * 128, (c + 1) * 128)
                pA = P[2 + c % 2][:, 0:256]
                pAk = "ps%d" % (2 + c % 2)
                for dr in range(2):
                    mm(pA[:, dr * 128:(dr + 1) * 128], ktT[dr][:, ch], qtT[dr][:, ch], True, True,
                       [("ktT", dr, j), ("qtT", dr, j)], [pAk], track=(dr == 1))
                tt("dve", AT[c % 2].rearrange("p (a b) -> p a b", b=128), pA.rearrange("p (a b) -> p a b", b=128),
                   triH, ALU.mult, [pAk, "const"], [("AT", c % 2)])

            def out_back(c):
                j = c // 4
                cc = c % 4
                ch = slice(c * 128, (c + 1) * 128)
                pob = 4 + j % 2
                po = P[pob]
                pok = "ps%d" % pob
                oc = po[:, cc * 128:(cc + 1) * 128]
                mm(oc, vH[:, ch], AT[c % 2][:, 0:128], True, False, [("vH", c // 8), ("AT", c % 2)], [pok], track=False)
                mm(oc, vH[:, ch], AT[c % 2][:, 128:256], False, False, [("vH", c // 8), ("AT", c % 2)], [pok], track=False)
                mm(oc, Ubf[0][:, ch], qtT[0][:, ch], False, False, [("U", 0, c), ("qtT", 0, j)], [pok], track=False)
                mm(oc, Ubf[1][:, ch], qtT[1][:, ch], False, True, [("U", 1, c), ("qtT", 1, j)], [pok], track=True)
                if cc == 3:
                    cols = slice(512 * j, 512 * (j + 1))
                    act(sqh, po[:, :], AF.Square, [pok], ["sqh"])
                    mm(P[6][:, :], ones128, sqh, True, True, ["sqh", "const"], ["ps6"])
                    act(lnh, P[6][:, :], AF.Ln, ["ps6"], ["lnh"], scale=1.0 / 128, bias=EPS)
                    act(lnh, lnh, AF.Exp, ["lnh"], ["lnh"], scale=-0.5)
                    stt(on, po[:, :], hgw[:, 0:1], lnh, ALU.mult, ALU.mult, [pok, "lnh", "hgw"], ["lnh"])
                    tt("pool", obS[j % 2], on, gbT[:, cols], ALU.mult, ["lnh", ("gbT", j)], [("obS", j % 2)])
                    k.dma("sp", ob_d[h * 128:(h + 1) * 128, cols], obS[j % 2], reads=[("obS", j % 2)], sem="d_obS%d" % (j % 2))
            out_front(0)
            for c in range(32):
                if c + 1 < 32:
                    out_front(c + 1)
                out_back(c)
            k.barrier()
        k.barrier()
        A.reset(m0)

    if 4 in phases:
        m0 = A.mark()
        wZ = A.alloc([128, 8, 2048], BF16)
        wpa = A.alloc([128, 4, 1024], BF16)
        wpb = A.alloc([128, 8, 1024], BF16)
        wo = A.alloc([128, 8, 1024], BF16)
        oaj = [A.alloc([128, 4, 512], BF16) for _ in range(2)]
        obj = [A.alloc([128, 8, 512], BF16) for _ in range(2)]
        mg_ = A.alloc([128, 8, 512], BF16)
        sa = [A.alloc([128, 512], F32) for _ in range(2)]
        sb2 = [A.alloc([128, 512], F32) for _ in range(2)]
        ma = [A.alloc([128, 512], F32) for _ in range(2)]
        mb = [A.alloc([128, 512], F32) for _ in range(2)]
        xr = [A.alloc([128, D], F32) for _ in range(2)]
        for seg in range(4):
            k.dma("pool", flat(wZ[:, 2 * seg:2 * seg + 2, :]), wz_d[:, seg * 4096:(seg + 1) * 4096], writes=["wZ"], sem="d_wZ")
        k.dma("pool", flat(wpa), wpa_d, writes=["wpa"])
        for hh in range(2):
            k.dma("pool", flat(wpb[:, hh * 4:(hh + 1) * 4, :]), wpb_d[:, hh * 4096:(hh + 1) * 4096], writes=["wpb"], sem="d_wpb")
            k.dma("pool", flat(wo[:, hh * 4:(hh + 1) * 4, :]), wo_d[:, hh * 4096:(hh + 1) * 4096], writes=["wo"], sem="d_wo")
        oa_v = oa_d.rearrange("(c p) t -> p c t", p=128)
        ob_v = ob_d.rearrange("(c p) t -> p c t", p=128)

        def load_ab(j_):
            cols_ = slice(512 * j_, 512 * (j_ + 1))
            k.dma("sp", oaj[j_ % 2], oa_v[:, :, cols_], writes=[("oaj", j_ % 2)])
            k.dma("sp", obj[j_ % 2], ob_v[:, :, cols_], writes=[("obj", j_ % 2)])
        load_ab(0)
        xcnt = 0
        ocnt = 0
        for j in range(8):
            sl = j % 2
            if j + 1 < 8:
                load_ab(j + 1)
            for dc in range(8):
                dcs = slice(dc * 128, (dc + 1) * 128)
                d2 = dc % 2
                pya, pyb = P[d2], P[2 + d2]
                pyak, pybk = "ps%d" % d2, "ps%d" % (2 + d2)
                for c in range(8):
                    mm(P[4][:, :], wZ[:, c, dcs], hT_nat(c, 512 * j, 512), c == 0, c == 7, ["wZ", "hT"], ["ps4"], track=(c == 7))
                for c in range(8):
                    mm(P[5][:, :], wZ[:, c, 1024 + dc * 128:1024 + (dc + 1) * 128], hT_nat(c, 512 * j, 512), c == 0, c == 7,
                       ["wZ", "hT"], ["ps5"], track=(c == 7))
                for c in range(4):
                    mm(pya[:, :], wpa[:, c, dcs], oaj[sl][:, c, :], c == 0, c == 3, ["wpa", ("oaj", sl)], [pyak], track=(c == 3))
                for c in range(8):
                    mm(pyb[:, :], wpb[:, c, dcs], obj[sl][:, c, :], c == 0, c == 7, ["wpb", ("obj", sl)], [pybk], track=(c == 7))
                act(sa[d2], P[4][:, :], AF.Sigmoid, ["ps4"], [("sa", d2)])
                act(sb2[d2], P[5][:, :], AF.Sigmoid, ["ps5"], [("sb2", d2)])
                tt("dve", ma[d2], pya[:, :], sa[d2], ALU.mult, [pyak, ("sa", d2)], [("ma", d2)])
                tt("dve", mb[d2], pyb[:, :], sb2[d2], ALU.mult, [pybk, ("sb2", d2)], [("mb", d2)])
                tt("pool", mg_[:, dc, :], ma[d2], mb[d2], ALU.add, [("ma", d2), ("mb", d2)], [("mg", dc)])
            for tq in range(4):
                t = 4 * j + tq
                xs = xcnt % 2
                xcnt += 1
                k.dma("sp", xr[xs], x_d[t * 128:(t + 1) * 128, :], writes=[("xr", xs)])
                for half in range(2):
                    pb = 4 + ocnt % 3
                    ocnt += 1
                    for c in range(8):
                        mm(P[pb][:, :], mg_[:, c, tq * 128:(tq + 1) * 128], wo[:, c, half * 512:(half + 1) * 512],
                           c == 0, c == 7, [("mg", c), "wo"], ["ps%d" % pb], track=(c == 7))
                    tt("dve", xr[xs][:, half * 512:(half + 1) * 512], P[pb][:, :], xr[xs][:, half * 512:(half + 1) * 512],
                       ALU.add, ["ps%d" % pb, ("xr", xs)], [("xr", xs)])
                k.dma("sp", out_d[t * 128:(t + 1) * 128, :], xr[xs], reads=[("xr", xs)], sem="d_ot%d" % xs)
    k.finish("sp")
    print("instructions:", k.ninstr, "arena peak:", A.peak, {e: k.count[e] for e in ENGS})
    k.emit()
    k.close()
    es.close()
    return nc


def _bias_tiles(rel_bias):
    a = np.arange(128)[:, None]
    c = np.arange(128)[None, :]
    out = np.zeros((128, 3, 8, 2, 128), np.float32)
    for g, (_, d) in enumerate(GROUPS):
        for ab in range(2):
            rel = (a - 64 - c) if ab == 0 else (a + 64 - c)
            rel = np.clip(rel, -64, 64)
            idx = t5_bucket(rel * d)
            for h in range(8):
                out[:, g, h, ab, :] = rel_bias[idx, g * 8 + h]
    return out.reshape(128, -1)


_NC_CACHE = {}


def kernel(x, norm_w, w_in, q_norm_w, k_norm_w, rel_bias, lb_fwd, lb_bwd, hg_norm_w,
           w_proj_a, w_proj_b, w_out, _debug=False, _phases=(1, 2, 3, 4)):
    x = np.asarray(x, np.float32)
    B = x.shape[0]
    key = (_debug, tuple(_phases))
    if key not in _NC_CACHE:
        _NC_CACHE[key] = build_program(_debug, _phases)
    nc = _NC_CACHE[key]
    qkw = np.zeros((128, 6), np.float32)
    for g in range(3):
        qkw[:, 2 * g] = np.tile(np.asarray(q_norm_w, np.float32)[0, g], 2)
        qkw[:, 2 * g + 1] = np.tile(np.asarray(k_norm_w, np.float32)[0, g], 2)
    lbf = np.ascontiguousarray(np.asarray(lb_fwd, np.float32).reshape(2, 8, 128).transpose(2, 0, 1).reshape(128, 16))
    lbb = np.ascontiguousarray(np.asarray(lb_bwd, np.float32).reshape(2, 8, 128).transpose(2, 0, 1).reshape(128, 16))
    w_in0 = np.asarray(w_in, np.float32)[0]
    wv = np.ascontiguousarray(w_in0.reshape(8, 128, IN_COLS).transpose(1, 0, 2))
    ar = np.arange(128)
    w_att = np.empty((12, 128, 8, 384), np.float32)
    for hp in range(4):
        for g in range(3):
            cols = np.concatenate([g * 1536 + seg * 512 + hp * 128 + ar for seg in range(3)])
            w_att[hp * 3 + g] = wv[:, :, cols]
    w_h = np.empty((8, 128, 8, 640), np.float32)
    for h in range(8):
        cols = np.concatenate([5120 + seg * 1024 + h * 128 + ar for seg in range(5)])
        w_h[h] = wv[:, :, cols]

    def pmaj(wm):
        nchunk = wm.shape[0] // 128
        return np.ascontiguousarray(wm.reshape(nchunk, 128, wm.shape[1]).transpose(1, 0, 2).reshape(128, -1))
    common = {
        "w_att": w_att.reshape(12, 128, -1),
        "w_g": np.ascontiguousarray(wv[:, :, 4608:5120]).reshape(128, -1),
        "w_h": w_h.reshape(8, 128, -1),
        "w_z": np.ascontiguousarray(wv[:, :, 10240:12288]).reshape(128, -1),
        "norm_w": np.ascontiguousarray(np.asarray(norm_w, np.float32)[0:1]),
        "qkw": qkw,
        "biasT": _bias_tiles(np.asarray(rel_bias, np.float32)),
        "lbf": lbf, "lbb": lbb,
        "hgw": np.ascontiguousarray(np.asarray(hg_norm_w, np.float32)[0].reshape(128, 1)),
        "w_pa": pmaj(np.asarray(w_proj_a, np.float32)[0]),
        "w_pb": pmaj(np.asarray(w_proj_b, np.float32)[0]),
        "w_o": pmaj(np.asarray(w_out, np.float32)[0]),
    }
    in_maps = []
    for b in range(B):
        m = dict(common)
        m["x"] = np.ascontiguousarray(x[b])
        in_maps.append(m)
    res = run_bass_kernel_spmd(nc, in_maps, core_ids=list(range(B)))
    if _debug:
        return res.results
    return np.stack([np.asarray(r["out"], np.float32) for r in res.results], axis=0)
```

```python
import numpy as np
from contextlib import ExitStack
import concourse.bass as bass
import concourse.mybir as mybir
from concourse.bass_utils import run_bass_kernel_spmd

F32 = mybir.dt.float32
BF16 = mybir.dt.bfloat16
U8 = mybir.dt.uint8
AF = mybir.ActivationFunctionType
ALU = mybir.AluOpType

S = 4096
D = 1024
PAD = 1024
EPS = 1e-6
GROUPS = ((128, 1), (512, 4), (2048, 16))
NUM_BUCKETS = 32
REL_MAX_DISTANCE = 1024
IN_COLS = 12288
ENGS = ("pe", "act", "dve", "pool", "sp")
DBG = {}


class KB:
    def __init__(self, nc):
        self.nc = nc
        self.prog = {e: [] for e in ENGS}
        self.count = {}
        self.sems = {}
        self.seen = {e: {} for e in ENGS}
        self.snap = {}
        self.last_w = {}
        self.readers = {}
        self.pending = {e: ([], []) for e in ENGS}
        self._ctx = []
        self.ninstr = 0
        for e in ENGS:
            self._mksem(e)

    def _mksem(self, name):
        cm = self.nc.semaphore("s_" + name)
        h = cm.__enter__()
        self._ctx.append(cm)
        self.sems[name] = h
        self.count[name] = 0
        return h

    def close(self):
        for cm in reversed(self._ctx):
            cm.__exit__(None, None, None)

    def _deps(self, e, reads, writes):
        deps = {}

        def add(cv):
            if cv is None:
                return
            c, v = cv
            if deps.get(c, 0) < v:
                deps[c] = v
        for r in reads:
            add(self.last_w.get(r))
        for w in writes:
            add(self.last_w.get(w))
            for rd in self.readers.get(w, ()):
                add(rd)
        waits = []
        seen = self.seen[e]
        for c, v in deps.items():
            if c == e and e == "pe":
                continue
            if seen.get(c, 0) < v:
                waits.append((c, v))
                seen[c] = v
                sn = self.snap.get((c, v))
                if sn:
                    for c2, v2 in sn.items():
                        if c2 != e and seen.get(c2, 0) < v2:
                            seen[c2] = v2
        return waits

    def _commit(self, token, reads, writes):
        for r in reads:
            self.readers.setdefault(r, []).append(token)
        for w in writes:
            self.last_w[w] = token
            self.readers[w] = []

    def op(self, e, fn, reads=(), writes=(), track=True):
        reads = tuple(reads)
        writes = tuple(writes)
        waits = self._deps(e, reads, writes)
        self.ninstr += 1
        if track:
            self.count[e] += 1
            token = (e, self.count[e])
            pr, pw = self.pending[e]
            self._commit(token, list(reads) + pr, list(writes) + pw)
            self.pending[e] = ([], [])
            self.snap[token] = dict(self.seen[e])
            self.prog[e].append((waits, fn, (e, 1)))
        else:
            pr, pw = self.pending[e]
            pr.extend(reads)
            pw.extend(writes)
            self.prog[e].append((waits, fn, None))

    def dma(self, q, out, in_, reads=(), writes=(), sem=None):
        reads = tuple(reads)
        writes = tuple(writes)
        if sem is None:
            sem = "d_" + str(writes[0] if writes else reads[0])
        if sem not in self.sems:
            self._mksem(sem)
        waits = self._deps(q, reads, writes)
        self.count[sem] += 16
        token = (sem, self.count[sem])
        self._commit(token, reads, writes)
        self.snap[token] = dict(self.seen[q])
        self.ninstr += 1

        def fn(eng, out=out, in_=in_):
            return eng.dma_start(out=out, in_=in_)
        self.prog[q].append((waits, fn, (sem, 16)))
        return token

    def barrier(self):
        for e in ENGS:
            waits = []
            for c, v in self.count.items():
                if v > 0 and self.seen[e].get(c, 0) < v:
                    if c == e and e in ("pe", "sp"):
                        continue
                    waits.append((c, v))
                    self.seen[e][c] = v
            if waits:
                self.prog[e].append((waits, None, None))
        self.last_w = {}
        self.readers = {}

    def finish(self, e="sp"):
        waits = []
        for c, v in self.count.items():
            if v > 0 and self.seen[e].get(c, 0) < v and c != e:
                waits.append((c, v))
        self.prog[e].append((waits, None, None))

    def emit(self):
        nc = self.nc
        with nc.Block() as block:
            for e in ENGS:
                items = self.prog[e]
                if not items:
                    continue

                def body(eng, items=items):
                    for waits, fn, inc in items:
                        for c, v in waits:
                            eng.wait_ge(self.sems[c], v)
                        if fn is None:
                            continue
                        ins = fn(eng)
                        if inc is not None:
                            ins.then_inc(self.sems[inc[0]], inc[1])
                reg = {"pe": block.tensor, "act": block.scalar, "dve": block.vector,
                       "pool": block.gpsimd, "sp": block.sync}[e]
                reg(body)


class Arena:
    def __init__(self, ap, nbytes):
        self.ap = ap
        self.nbytes = nbytes
        self.off = 0
        self.peak = 0

    def alloc(self, shape, dt):
        n = int(np.prod(shape[1:]))
        nb = n * (4 if dt == F32 else 2)
        nb = (nb + 63) // 64 * 64
        assert self.off + nb <= self.nbytes, ("SBUF arena overflow", self.off, nb)
        a = self.ap[:, self.off:self.off + nb].bitcast(dt)[:, 0:n]
        self.off += nb
        self.peak = max(self.peak, self.off)
        if len(shape) == 3:
            a = a.rearrange("p (a b) -> p a b", b=shape[2])
        return a

    def mark(self):
        return self.off

    def reset(self, m):
        self.off = m


def t5_bucket(rel):
    half = NUM_BUCKETS // 2
    max_exact = half // 2
    n = np.abs(rel)
    large = max_exact + (np.log(np.maximum(n, 1) / max_exact)
                         / np.log(REL_MAX_DISTANCE / max_exact) * (half - max_exact)).astype(np.int32)
    large = np.minimum(large, half - 1)
    return np.where(rel > 0, half, 0) + np.where(n < max_exact, n, large)


def build_program(debug=False, phases=(1, 2, 3, 4)):
    nc = bass.Bass("TRN2", target_bir_lowering=False)
    x_d = nc.dram_tensor("x", [S, D], F32, kind="ExternalInput").ap()
    watt_d = nc.dram_tensor("w_att", [12, 128, 8 * 384], F32, kind="ExternalInput").ap()
    wg_d = nc.dram_tensor("w_g", [128, 8 * 512], F32, kind="ExternalInput").ap()
    wh_d = nc.dram_tensor("w_h", [8, 128, 8 * 640], F32, kind="ExternalInput").ap()
    wz_d = nc.dram_tensor("w_z", [128, 8 * 2048], F32, kind="ExternalInput").ap()
    normw_d = nc.dram_tensor("norm_w", [1, D], F32, kind="ExternalInput").ap()
    qkw_d = nc.dram_tensor("qkw", [128, 6], F32, kind="ExternalInput").ap()
    bias_d = nc.dram_tensor("biasT", [128, 3 * 8 * 2 * 128], F32, kind="ExternalInput").ap()
    lbf_d = nc.dram_tensor("lbf", [128, 16], F32, kind="ExternalInput").ap()
    lbb_d = nc.dram_tensor("lbb", [128, 16], F32, kind="ExternalInput").ap()
    hgw_d = nc.dram_tensor("hgw", [128, 1], F32, kind="ExternalInput").ap()
    wpa_d = nc.dram_tensor("w_pa", [128, 4 * D], F32, kind="ExternalInput").ap()
    wpb_d = nc.dram_tensor("w_pb", [128, 8 * D], F32, kind="ExternalInput").ap()
    wo_d = nc.dram_tensor("w_o", [128, 8 * D], F32, kind="ExternalInput").ap()
    out_d = nc.dram_tensor("out", [S, D], F32, kind="ExternalOutput").ap()
    skind = "ExternalOutput" if debug else "Internal"
    oa_d = nc.dram_tensor("oa_scr", [512, S], BF16, kind=skind).ap()
    ob_d = nc.dram_tensor("ob_scr", [D, S], BF16, kind=skind).ap()
    if debug:
        hT_dbg = nc.dram_tensor("hT_dbg", [128, 8 * S], BF16, kind="ExternalOutput").ap()

    def flat(ap3):
        return ap3.rearrange("p c n -> p (c n)")

    es = ExitStack()
    ARENA_BYTES = 209920
    arena_t = es.enter_context(nc.sbuf_tensor("arena", [128, ARENA_BYTES], U8))
    A = Arena(arena_t, ARENA_BYTES)
    P = [es.enter_context(nc.psum_tensor("ps%d" % i, [128, 512], F32)) for i in range(7)]
    PT = es.enter_context(nc.psum_tensor("pst", [128, 1024], BF16))
    k = KB(nc)

    def mm(out, lhsT, rhs, start, stop, reads, writes, track=True, tp=None):
        def fn(e):
            if tp is None:
                return e.matmul(out, lhsT=lhsT, rhs=rhs, start=start, stop=stop)
            return e.matmul(out, lhsT=lhsT, rhs=rhs, start=start, stop=stop, tile_position=tp)
        k.op("pe", fn, reads, writes, track)

    def tr(out, in_, reads, writes, track=True):
        k.op("pe", lambda e: e.transpose(out, in_, ident), tuple(reads) + ("const",), writes, track)

    def act(out, in_, func, reads, writes, scale=None, bias=None, accum=None):
        kw = {}
        if scale is not None:
            kw["scale"] = scale
        if bias is not None:
            kw["bias"] = bias
        if accum is not None:
            kw["accum_out"] = accum
        k.op("act", lambda e: e.activation(out=out, in_=in_, func=func, **kw), reads, writes)

    def tt(eng, out, in0, in1, op, reads, writes):
        k.op(eng, lambda e: e.tensor_tensor(out=out, in0=in0, in1=in1, op=op), reads, writes)

    def ts(eng, out, in0, s1, op0, reads, writes, s2=None, op1=None):
        if op1 is None:
            k.op(eng, lambda e: e.tensor_scalar(out=out, in0=in0, scalar1=s1, scalar2=None, op0=op0), reads, writes)
        else:
            k.op(eng, lambda e: e.tensor_scalar(out=out, in0=in0, scalar1=s1, scalar2=s2, op0=op0, op1=op1), reads, writes)

    def stt(out, in0, scalar, in1, op0, op1, reads, writes):
        k.op("dve", lambda e: e.scalar_tensor_tensor(out=out, in0=in0, scalar=scalar, in1=in1, op0=op0, op1=op1),
             reads, writes)

    def cp(eng, out, in_, reads, writes):
        if eng == "act":
            k.op("act", lambda e: e.copy(out=out, in_=in_), reads, writes)
        else:
            k.op(eng, lambda e: e.tensor_copy(out=out, in_=in_), reads, writes)

    def recip(out, in_, reads, writes):
        k.op("dve", lambda e: e.reciprocal(out=out, in_=in_), reads, writes)

    def memset(eng, ap, val, writes):
        k.op(eng, lambda e: e.memset(ap, val), (), writes)

    hTf = A.alloc([128, PAD + 8 * S + PAD], BF16)
    ident = A.alloc([128, 128], BF16)
    ones_bd = A.alloc([128, 128], BF16)
    ones128 = A.alloc([128, 128], BF16)
    tri4 = A.alloc([128, 3, 128], F32)
    scanmask = A.alloc([128, 512], F32)
    qkw = A.alloc([128, 6], F32)
    lbt = A.alloc([128, 32], F32)
    lbd = A.alloc([128, 16], F32)
    oml = A.alloc([128, 16], F32)
    hgw = A.alloc([128, 1], F32)
    persist_mark = A.mark()

    def hT_nat(c, t0, n):
        o = PAD + c * S + t0
        return hTf[:, o:o + n]

    def hT_perm(c, d, n0, cnt):
        L = S // d
        r0 = n0 // L
        l0 = n0 % L
        base = PAD + c * S
        if d == 1:
            return hTf[:, base + n0: base + n0 + cnt], None
        if cnt <= L - l0:
            st = base + l0 * d + r0
            return hTf[:, st: st + (cnt - 1) * d + 1: d], None
        assert l0 == 0 and cnt % L == 0
        nres = cnt // L
        st = base + r0
        v = hTf[:, st: st + L * d].rearrange("p (l r) -> p r l", r=d)[:, 0:nres, :]
        return v, L

    def hT_vtile(c, d, r, i):
        st = PAD + c * S + (128 * i - 64) * d + r
        if d == 1:
            return hTf[:, st: st + 128]
        return hTf[:, st: st + 127 * d + 1: d]

    def acc_perm(acc, d, n0, cnt):
        L = S // d
        r0 = n0 // L
        l0 = n0 % L
        if d == 1:
            return acc[:, n0:n0 + cnt], None
        if cnt <= L - l0:
            st = l0 * d + r0
            return acc[:, st: st + (cnt - 1) * d + 1: d], None
        assert l0 == 0 and cnt % L == 0
        nres = cnt // L
        v = acc.rearrange("p (l r) -> p r l", r=d)[:, r0:r0 + nres, :]
        return v, L

    memset("pool", hTf[:, 0:PAD], 0.0, ["hTpad"])
    memset("pool", hTf[:, PAD + 8 * S: PAD + 8 * S + PAD], 0.0, ["hTpad"])
    memset("pool", ident, 0.0, ["const"])
    k.op("pool", lambda e: e.affine_select(out=ident, in_=ident, pattern=[[-1, 128]], compare_op=ALU.not_equal,
                                           fill=1.0, base=0, channel_multiplier=1), ["const"], ["const"])
    memset("pool", ones_bd, 0.0, ["const"])
    memset("pool", ones_bd[0:64, 0:64], 1.0, ["const"])
    memset("pool", ones_bd[64:128, 64:128], 1.0, ["const"])
    memset("pool", ones128, 1.0, ["const"])
    memset("pool", tri4, 1.0, ["const"])
    for idx in (0, 2):
        k.op("pool", lambda e, idx=idx: e.affine_select(out=tri4[:, idx, :], in_=tri4[:, idx, :], pattern=[[-1, 128]],
                                                        compare_op=ALU.is_ge, fill=0.0, base=0, channel_multiplier=1),
             ["const"], ["const"])
    k.op("pool", lambda e: e.affine_select(out=tri4[:, 1, :], in_=tri4[:, 1, :], pattern=[[1, 128]],
                                           compare_op=ALU.is_ge, fill=0.0, base=0, channel_multiplier=-1),
         ["const"], ["const"])
    memset("pool", scanmask, 1.0, ["const"])
    memset("pool", scanmask.rearrange("p (c t) -> p c t", t=128)[:, :, 0:1], 0.0, ["const"])
    k.dma("sp", qkw, qkw_d, writes=["qkw"])
    k.dma("sp", lbt[:, 0:16], lbf_d, writes=["lbt"], sem="d_lbt")
    k.dma("sp", lbt[:, 16:32], lbb_d, writes=["lbt"], sem="d_lbt")
    k.dma("sp", hgw, hgw_d, writes=["hgw"])
    for g in range(3):
        ts("dve", qkw[:, 2 * g + 1:2 * g + 2], qkw[:, 2 * g + 1:2 * g + 2], 8.0, ALU.mult, ["qkw"], ["qkw"])
    lbv = lbt.rearrange("p (d l h) -> p d l h", d=2, l=2)
    for dd in range(2):
        tt("dve", lbd[:, dd * 8:(dd + 1) * 8], lbv[:, dd, 0, :], lbv[:, dd, 1, :], ALU.subtract, ["lbt"], ["lbd"])
    act(oml, lbd, AF.Sigmoid, ["lbd"], ["oml"], scale=-1.0)

    if 1 in phases:
        m0 = A.mark()
        normw_rep = A.alloc([128, D], F32)
        xts = [A.alloc([128, D], F32) for _ in range(2)]
        xn = [A.alloc([128, D], BF16) for _ in range(2)]
        junk = A.alloc([128, D], BF16)
        ssq = A.alloc([128, 32], F32)
        rstd = A.alloc([128, 32], F32)
        k.dma("sp", normw_rep, normw_d.partition_broadcast(128), writes=["normw"])
        memset("pool", ssq, 0.0, [("ssq", t) for t in range(32)])
        hT_ct = hTf[:, PAD:PAD + 8 * S].rearrange("p (c t) -> p c t", t=S)
        for t in range(32):
            sl = t % 2
            k.dma("sp" if sl == 0 else "act", xts[sl], x_d[t * 128:(t + 1) * 128, :], writes=[("xt", sl)])
            act(junk, xts[sl], AF.Square, [("xt", sl)], ["junk", ("ssq", t)], accum=ssq[:, t:t + 1])
            ts("dve", rstd[:, t:t + 1], ssq[:, t:t + 1], 1.0 / D, ALU.mult, [("ssq", t)], [("rstd", t)], s2=EPS, op1=ALU.add)
            act(rstd[:, t:t + 1], rstd[:, t:t + 1], AF.Sqrt, [("rstd", t)], [("rstd", t)])
            recip(rstd[:, t:t + 1], rstd[:, t:t + 1], [("rstd", t)], [("rstd", t)])
            stt(xn[sl], xts[sl], rstd[:, t:t + 1], normw_rep, ALU.mult, ALU.mult,
                [("xt", sl), ("rstd", t), "normw"], [("xn", sl)])
            for c in range(8):
                tr(PT[:, c * 128:(c + 1) * 128], xn[sl][:, c * 128:(c + 1) * 128], [("xn", sl)], ["PT"], track=(c == 7))
            cp("act" if t % 2 == 0 else "dve", hT_ct[:, :, t * 128:(t + 1) * 128],
               PT[:, :].rearrange("p (c t) -> p c t", t=128), ["PT"], [("hT", t)])
        k.barrier()
        A.reset(m0)
        if debug:
            k.dma("sp", hT_dbg, hTf[:, PAD:PAD + 8 * S], reads=["x"], sem="dbg")

    if 2 in phases:
        m0 = A.mark()
        wA = [A.alloc([128, 8, 384], BF16) for _ in range(2)]
        wG = A.alloc([128, 8, 512], BF16)
        qn = A.alloc([128, S], BF16)
        knp = A.alloc([128, 64 + S + 64], BF16)
        vTp = A.alloc([128, 64 + S + 64], BF16)
        vP = A.alloc([128, 33 * 256], BF16)
        acc = [A.alloc([128, S], F32) for _ in range(2)]
        bM = A.alloc([128, 512], F32)
        Mrep = A.alloc([128, 1024], F32)
        sq = [[A.alloc([128, 512], BF16) for _ in range(2)] for _ in range(2)]
        lnb = [A.alloc([128, 512], F32) for _ in range(2)]
        ee = [A.alloc([128, 512], F32) for _ in range(3)]
        pT = [A.alloc([128, 512], BF16) for _ in range(3)]
        sgh = [A.alloc([128, 512], F32) for _ in range(2)]
        rD = [A.alloc([128, 512], F32) for _ in range(2)]
        t1 = [A.alloc([128, 512], F32) for _ in range(2)]
        oaS = [A.alloc([128, 512], BF16) for _ in range(4)]
        vP4 = vP.rearrange("p (t h d) -> p t h d", h=2, d=128)
        memset("pool", knp[:, 0:64], 0.0, [("kn", -1)])
        memset("pool", knp[:, 64 + S:64 + S + 64], 0.0, [("kn", 8)])
        memset("pool", vTp[:, 0:64], 0.0, [("vT", -1)])
        memset("pool", vTp[:, 64 + S:64 + S + 64], 0.0, [("vT", 8)])
        memset("pool", vP4[:, :, :, 64:128], 1.0, ["vPones"])
        k.dma("pool", flat(wG), wg_d, writes=["wG"], sem="d_wG")
        maskAB = tri4[:, 0:2, :]
        units = [(hp, g) for hp in range(DBG.get("nhp", 4)) for g in range(DBG.get("ng", 3))]

        def load_unit_w(u):
            hp_, g_ = units[u]
            slot_ = u % 2
            k.dma("pool", flat(wA[slot_]), watt_d[hp_ * 3 + g_], writes=[("wA", slot_)], sem="d_wA%d" % slot_)
        load_unit_w(0)
        sccnt = 0
        oacnt = 0
        for u, (hp, g) in enumerate(units):
            win_, d = GROUPS[g]
            L = S // d
            TPS = L // 128
            slot = u % 2
            w = wA[slot]
            if u + 1 < len(units):
                load_unit_w(u + 1)
            bcol = (g * 8 + 2 * hp) * 256
            k.dma("sp", bM, bias_d[:, bcol:bcol + 512], writes=["bM"])
            act(bM, bM, AF.Exp, ["bM"], ["bM"])
            bMv = bM.rearrange("p (h a c) -> p h a c", h=2, a=2)
            Mv = Mrep.rearrange("p (h m a c) -> p h m a c", h=2, m=2, a=2)
            for hl in range(2):
                for mi in range(2):
                    tt("pool", Mv[:, hl, mi], bMv[:, hl], maskAB, ALU.mult, ["bM", "const"], ["Mrep"])

            nl = 512 // d

            def pview(ap_nat):
                if d == 1:
                    return ap_nat
                return ap_nat.rearrange("p (l r) -> p r l", r=d)

            def dview(buf, off, j):
                if d == 1:
                    return buf[:, off + 512 * j: off + 512 * (j + 1)]
                return buf[:, off:off + S].rearrange("p (r l) -> p r l", l=L)[:, :, nl * j: nl * (j + 1)]

            def wkeys(nm, j):
                return [(nm, j)] if d == 1 else [(nm, jj) for jj in range(8)]

            def proj_front(j):
                pqb, pkb, pvb = j % 2, 2 + j % 2, 4
                for (pb, woff) in ((pqb, 0), (pkb, 128), (pvb, 256)):
                    for c in range(8):
                        mm(P[pb][:, :], w[:, c, woff:woff + 128], hT_nat(c, 512 * j, 512), c == 0, c == 7,
                           [("wA", slot), "hT"], ["ps%d" % pb], track=(c == 7))
                act(sq[0][j % 2], P[pqb][:, :], AF.Square, ["ps%d" % pqb], [("sq", 0, j % 2)])
                act(sq[1][j % 2], P[pkb][:, :], AF.Square, ["ps%d" % pkb], [("sq", 1, j % 2)])
                cp("dve", dview(vTp, 64, j), pview(P[pvb][:, :]), ["ps%d" % pvb], wkeys("vT", j))

            def proj_back(j):
                pqb, pkb = j % 2, 2 + j % 2
                items = ((pqb, 2 * g, dview(qn, 0, j), wkeys("qn", j)),
                         (pkb, 2 * g + 1, dview(knp, 64, j), wkeys("kn", j)))
                for qi in range(2):
                    mm(P[5 + qi][:, :], ones_bd, sq[qi][j % 2], True, True, [("sq", qi, j % 2), "const"], ["ps%d" % (5 + qi)])
                for qi in range(2):
                    act(lnb[qi], P[5 + qi][:, :], AF.Ln, ["ps%d" % (5 + qi)], [("lnb", qi)], bias=64.0 * EPS)
                for qi in range(2):
                    act(lnb[qi], lnb[qi], AF.Exp, [("lnb", qi)], [("lnb", qi)], scale=-0.5)
                for qi, (pb, wcol, dst, dkeys) in enumerate(items):
                    stt(dst, pview(P[pb][:, :]), qkw[:, wcol:wcol + 1], pview(lnb[qi]), ALU.mult, ALU.mult,
                        ["ps%d" % pb, ("lnb", qi), "qkw"], dkeys)
            if "q" in DBG.get("st", "qvaf"):
                proj_front(0)
                for j in range(8):
                    if j + 1 < 8:
                        proj_front(j + 1)
                    proj_back(j)
                for t0_ in range(0, 33, 8):
                    nb = min(8, 33 - t0_)
                    for s_ in range(nb):
                        vt = t0_ + s_
                        lo = (128 * vt - 64) // 512 if vt > 0 else -1
                        hi = min((128 * vt + 63) // 512, 8)
                        tr(PT[:, s_ * 128:(s_ + 1) * 128], vTp[:, 128 * vt:128 * vt + 128],
                           [("vT", lo), ("vT", hi)], ["PT"], track=(s_ == nb - 1))
                    cp("act", vP4[:, t0_:t0_ + nb, :, 0:64],
                       PT[:, 0:nb * 128].rearrange("p (t h d) -> p t h d", h=2, d=64), ["PT"], [("vP", t0_ // 8)])

            its = [(mg, mp, hl) for mg in range(8) for mp in range(2) for hl in range(2)]
            state = {}

            def att_front(ii):
                nonlocal sccnt
                mg, mp, hl = its[ii]
                rows = slice(hl * 64, hl * 64 + 64)
                sb_ = sccnt % 3
                sl2 = sccnt % 3
                sccnt += 1
                state[ii] = (sb_, sl2)
                psc = P[sb_]
                psck = "ps%d" % sb_
                for mi in range(2):
                    m = mg * 4 + mp * 2 + mi
                    jq = m // 4
                    for ab in range(2):
                        kst = 128 * m + 128 * ab
                        kj = [("kn", (kst - 64) // 512 if kst >= 64 else -1), ("kn", min((kst + 127 - 64) // 512, 8))]
                        mm(psc[:, (mi * 2 + ab) * 128:(mi * 2 + ab + 1) * 128],
                           knp[rows, kst:kst + 128], qn[rows, 128 * m:128 * m + 128], True, True,
                           [("qn", jq)] + kj, [psck], track=(mi == 1 and ab == 1), tp=(hl * 64, 0))
                act(ee[sl2], psc[:, :], AF.Exp, [psck], [("ee", sl2)])
                tt("dve" if ii % 2 == 0 else "pool", pT[sl2], ee[sl2], Mrep[:, hl * 512:(hl + 1) * 512], ALU.mult,
                   [("ee", sl2), "Mrep"], [("pT", sl2)])

            def att_back(ii):
                mg, mp, hl = its[ii]
                sb_, sl2 = state[ii]
                pb = 3 + (mg % 2) * 2 + hl
                pnd = P[pb]
                pndk = "ps%d" % pb
                for mi in range(2):
                    m = mg * 4 + mp * 2 + mi
                    n0 = 128 * m
                    i = (n0 % L) // 128
                    first = (i == 0)
                    last = (i == TPS - 1)
                    vA = m
                    vB = m + 1
                    col = (mp * 2 + mi) * 128
                    KA = slice(64, 128) if first else slice(0, 128)
                    KBs = slice(0, 64) if last else slice(0, 128)
                    vkeys = [("vP", vA // 8), ("vP", vB // 8), "vPones"]
                    mm(pnd[:, col:col + 128], vP4[KA, vA, hl, :],
                       pT[sl2][KA, (mi * 2) * 128:(mi * 2 + 1) * 128], True, False,
                       [("pT", sl2)] + vkeys, [pndk], track=False, tp=(KA.start, 0))
                    mm(pnd[:, col:col + 128], vP4[KBs, vB, hl, :],
                       pT[sl2][KBs, (mi * 2 + 1) * 128:(mi * 2 + 2) * 128], False, True,
                       [("pT", sl2)] + vkeys, [pndk], track=(mi == 1), tp=(0, 0))
                if mp == 1:
                    av, inner = acc_perm(acc[hl], d, 512 * mg, 512)
                    pv_ = pnd[:, :] if inner is None else pnd[:, :].rearrange("p (a b) -> p a b", b=inner)
                    akey = ("acc", hl)
                    if g == 0:
                        cp("act", av, pv_, [pndk], [akey])
                    else:
                        tt("dve", av, av, pv_, ALU.add, [pndk, akey], [akey])
            if "a" in DBG.get("st", "qvaf"):
                LA = 2
                for ii in range(min(LA, len(its))):
                    att_front(ii)
                for ii in range(len(its)):
                    if ii + LA < len(its):
                        att_front(ii + LA)
                    att_back(ii)

            if g == len([1 for (hp_, g_) in units if hp_ == hp]) - 1 and "f" in DBG.get("st", "qvaf"):
                for j in range(8):
                    pg = P[j % 3]
                    pgk = "ps%d" % (j % 3)
                    for c in range(8):
                        mm(pg[:, :], wG[:, c, hp * 128:(hp + 1) * 128], hT_nat(c, 512 * j, 512), c == 0, c == 7,
                           ["wG", "hT"], [pgk], track=(c == 7))
                    HL = (0, 1)
                    for hl in HL:
                        act(sgh[hl][0:64, :], pg[hl * 64:(hl + 1) * 64, :], AF.Silu, [pgk], [("sgh", hl)])
                    for hl in HL:
                        act(rD[hl][0:64, :], acc[hl][64:128, 512 * j:512 * (j + 1)], AF.Ln, [("acc", hl)], [("rD", hl)])
                    for hl in HL:
                        act(rD[hl][0:64, :], rD[hl][0:64, :], AF.Exp, [("rD", hl)], [("rD", hl)], scale=-1.0)
                    for hl in HL:
                        tt("pool", t1[hl][0:64, :], acc[hl][0:64, 512 * j:512 * (j + 1)], rD[hl][0:64, :], ALU.mult,
                           [("acc", hl), ("rD", hl)], [("t1", hl)])
                    for hl in HL:
                        os_ = oacnt % 4
                        oacnt += 1
                        tt("dve", oaS[os_][0:64, :], t1[hl][0:64, :], sgh[hl][0:64, :], ALU.mult,
                           [("t1", hl), ("sgh", hl)], [("oaS", os_)])
                        k.dma("sp", oa_d[(2 * hp + hl) * 64:(2 * hp + hl + 1) * 64, 512 * j:512 * (j + 1)], oaS[os_][0:64, :],
                              reads=[("oaS", os_)], sem="d_oaS%d" % os_)
        k.barrier()
        A.reset(m0)

    if 3 in phases:
        m0 = A.mark()
        wH = [A.alloc([128, 8, 640], BF16) for _ in range(2)]
        vH = A.alloc([128, S], BF16)
        gbT = A.alloc([128, S], BF16)
        qtT = [A.alloc([128, S], BF16) for _ in range(2)]
        ktT = [A.alloc([128, S], BF16) for _ in range(2)]
        ktok = [A.alloc([128, S], BF16) for _ in range(2)]
        dec = [A.alloc([128, 32], F32) for _ in range(2)]
        Ubf = [A.alloc([128, S], BF16) for _ in range(2)]
        iT = Ubf[0]
        Wst = [[A.alloc([128, 128], F32) for _ in range(2)] for _ in range(2)]
        su = [A.alloc([128, 512], F32) for _ in range(2)]
        sL1 = [A.alloc([128, 512], F32) for _ in range(2)]
        sg_ = [A.alloc([128, 512], F32) for _ in range(2)]
        sb_ = [A.alloc([128, 512], F32) for _ in range(2)]
        sr = [A.alloc([128, 512], F32) for _ in range(2)]
        se = [A.alloc([128, 512], F32) for _ in range(2)]
        AT = [A.alloc([128, 256], BF16) for _ in range(2)]
        sqh = A.alloc([128, 512], BF16)
        lnh = A.alloc([128, 512], F32)
        on = lnh
        obS = [A.alloc([128, 512], BF16) for _ in range(2)]
        lbv2 = A.alloc([128, 16], F32)
        act(lbv2, lbd, AF.Sigmoid, ["lbd"], ["lbv2"])
        noml = A.alloc([128, 16], F32)
        ts("dve", noml, oml, -1.0, ALU.mult, ["oml"], ["noml"])
        triH = tri4[:, 1:3, :]
        NH = DBG.get("nh", 8)

        def load_wH(h_):
            k.dma("pool", flat(wH[h_ % 2]), wh_d[h_], writes=[("wH", h_ % 2)], sem="d_wH%d" % (h_ % 2))
        if DBG.get("pref", 1):
            load_wH(0)
        for h in range(NH):
            hs = h % 2
            w = wH[hs]
            if DBG.get("pref", 1):
                if h + 1 < NH:
                    load_wH(h + 1)
            else:
                load_wH(h)
            for j in range(8):
                cols = slice(512 * j, 512 * (j + 1))
                pb = j % 2
                for c in range(8):
                    mm(P[pb][:, :], w[:, c, 512:640], hT_nat(c, 512 * j, 512), c == 0, c == 7,
                       [("wH", hs), "hT"], ["ps%d" % pb], track=(c == 7))
                act(gbT[:, cols], P[pb][:, :], AF.Silu, ["ps%d" % pb], [("gbT", j)])
            for j in range(8):
                cols = slice(512 * j, 512 * (j + 1))
                pb = 2 + j % 2
                for c in range(8):
                    mm(P[pb][:, :], w[:, c, 384:512], hT_nat(c, 512 * j, 512), c == 0, c == 7,
                       [("wH", hs), "hT"], ["ps%d" % pb], track=(c == 7))
                cp("dve", iT[:, cols], P[pb][:, :], ["ps%d" % pb], [("iT", j)])
            for t0_ in range(0, 32, 8):
                for s_ in range(8):
                    t = t0_ + s_
                    tr(PT[:, s_ * 128:(s_ + 1) * 128], iT[:, t * 128:(t + 1) * 128], [("iT", t // 4)], ["PT"], track=(s_ == 7))
                cp("act" if (t0_ // 8) % 2 == 0 else "dve", vH[:, t0_ * 128:(t0_ + 8) * 128], PT[:, :], ["PT"], [("vH", t0_ // 8)])

            def gate_front(j):
                for (pi_, woff) in ((0, 0), (1, 128), (2, 256)):
                    pb = pi_ * 2 + j % 2
                    for c in range(8):
                        mm(P[pb][:, :], w[:, c, woff:woff + 128], hT_nat(c, 512 * j, 512), c == 0, c == 7,
                           [("wH", hs), "hT"], ["ps%d" % pb], track=(c == 7))
                cols = slice(512 * j, 512 * (j + 1))
                pqk = "ps%d" % (j % 2)
                pq = P[j % 2]
                DR = (0, 1)

                def K_(nm, dr):
                    return (nm, dr)
                src = [sb_[0], sr[1]]
                srck = [K_("b", 0), K_("r", 1)]
                dcol = [slice(127, 512, 128), slice(0, 512, 128)]
                for dr in DR:
                    pzb_ = (1 + dr) * 2 + j % 2
                    act(su[dr], P[pzb_][:, :], AF.Exp, ["ps%d" % pzb_], [K_("u", dr)])
                for dr in DR:
                    act(sL1[dr], su[dr], AF.Ln, [K_("u", dr)], [K_("L1", dr)], bias=1.0)
                for dr in DR:
                    act(su[dr], sL1[dr], AF.Exp, [K_("L1", dr)], [K_("u", dr)], scale=-1.0)
                for dr in DR:
                    act(sg_[dr], su[dr], AF.Ln, [K_("u", dr)], [K_("g", dr)], scale=noml[:, dr * 8 + h: dr * 8 + h + 1], bias=1.0)
                for dr in DR:
                    k.op("dve", lambda e, o=sb_[dr], g_=sg_[dr]: e.tensor_tensor_scan(out=o, data0=scanmask, data1=g_, initial=0.0,
                                                                                op0=ALU.mult, op1=ALU.add),
                         [K_("g", dr), "const"], [K_("b", dr)])
                tt("pool", sr[1], sg_[1], sb_[1], ALU.subtract, [K_("g", 1), K_("b", 1)], [K_("r", 1)])
                bbv = sb_[1].rearrange("p (c t) -> p c t", t=128)
                rrv = sr[1].rearrange("p (c t) -> p c t", t=128)
                tt("dve", rrv, rrv, bbv[:, :, 127:128].to_broadcast([128, 4, 128]), ALU.add, [K_("r", 1), K_("b", 1)], [K_("r", 1)])
                for dr in DR:
                    act(se[dr], src[dr], AF.Exp, [srck[dr]], [K_("e", dr)])
                for dr in DR:
                    act(sL1[dr], src[dr], AF.Exp, [srck[dr]], [K_("L1", dr)], scale=-1.0)
                for dr in DR:
                    tt("dve", qtT[dr][:, cols], pq[:, :], se[dr], ALU.mult, [pqk, K_("e", dr)], [("qtT", dr, j)])
                    cp("pool", dec[dr][:, 4 * j:4 * j + 4], se[dr][:, dcol[dr]], [K_("e", dr)], [("dec", dr)])
                for dr in DR:
                    stt(ktT[dr][:, cols], su[dr], oml[:, dr * 8 + h: dr * 8 + h + 1], sL1[dr], ALU.mult, ALU.mult,
                        [K_("u", dr), K_("L1", dr), "oml"], [("ktT", dr, j)])

            def gate_front_old(j):
                for (pi_, woff) in ((0, 0), (1, 128), (2, 256)):
                    pb = pi_ * 2 + j % 2
                    for c in range(8):
                        mm(P[pb][:, :], w[:, c, woff:woff + 128], hT_nat(c, 512 * j, 512), c == 0, c == 7,
                           [("wH", hs), "hT"], ["ps%d" % pb], track=(c == 7))
                cols = slice(512 * j, 512 * (j + 1))
                pqk = "ps%d" % (j % 2)
                pq = P[j % 2]
                for dr in range(2):
                    pzb_ = (1 + dr) * 2 + j % 2
                    pz = P[pzb_]
                    pzk = "ps%d" % pzb_
                    K_ = lambda nm: (nm, dr)
                    act(su[dr], pz[:, :], AF.Exp, [pzk], [K_("u")])
                    act(sL1[dr], su[dr], AF.Ln, [K_("u")], [K_("L1")], bias=1.0)
                    act(sg_[dr], su[dr], AF.Ln, [K_("u")], [K_("g")], bias=lbv2[:, dr * 8 + h: dr * 8 + h + 1])
                    tt("pool", sg_[dr], sg_[dr], sL1[dr], ALU.subtract, [K_("g"), K_("L1")], [K_("g")])
                    k.op("dve", lambda e, o=sb_[dr], g_=sg_[dr]: e.tensor_tensor_scan(out=o, data0=scanmask, data1=g_, initial=0.0,
                                                                                op0=ALU.mult, op1=ALU.add),
                         [K_("g"), "const"], [K_("b")])
                    if dr == 0:
                        src = sb_[dr]
                        srck = K_("b")
                        dcol = slice(127, 512, 128)
                    else:
                        tt("pool", sr[dr], sg_[dr], sb_[dr], ALU.subtract, [K_("g"), K_("b")], [K_("r")])
                        bbv = sb_[dr].rearrange("p (c t) -> p c t", t=128)
                        rrv = sr[dr].rearrange("p (c t) -> p c t", t=128)
                        tt("dve", rrv, rrv, bbv[:, :, 127:128].to_broadcast([128, 4, 128]), ALU.add, [K_("r"), K_("b")], [K_("r")])
                        src = sr[dr]
                        srck = K_("r")
                        dcol = slice(0, 512, 128)
                    act(se[dr], src, AF.Exp, [srck], [K_("e")])
                    tt("dve", qtT[dr][:, cols], pq[:, :], se[dr], ALU.mult, [pqk, K_("e")], [("qtT", dr, j)])
                    cp("pool", dec[dr][:, 4 * j:4 * j + 4], se[dr][:, dcol], [K_("e")], [("dec", dr)])
                    tt("pool", sL1[dr], sL1[dr], src, ALU.add, [K_("L1"), srck], [K_("L1")])
                    act(su[dr], sL1[dr], AF.Exp, [K_("L1")], [K_("u")], scale=-1.0)
                    ts("dve", ktT[dr][:, cols], su[dr], oml[:, dr * 8 + h: dr * 8 + h + 1], ALU.mult,
                       [K_("u"), "oml"], [("ktT", dr, j)])

            def gate_back(j):
                cols = slice(512 * j, 512 * (j + 1))
                for dr in range(2):
                    for cc in range(4):
                        ch = 4 * j + cc
                        tr(PT[:, (dr * 4 + cc) * 128:(dr * 4 + cc + 1) * 128], ktT[dr][:, ch * 128:(ch + 1) * 128],
                           [("ktT", dr, j)], ["PT"], track=(cc == 3 and dr == 1))
                ev = "act" if j % 2 == 0 else "dve"
                cp(ev, ktok[0][:, cols], PT[:, 0:512], ["PT"], [("ktok", 0, j)])
                cp(ev, ktok[1][:, cols], PT[:, 512:1024], ["PT"], [("ktok", 1, j)])
            gf = gate_front if DBG.get("wave", 1) else gate_front_old
            gf(0)
            for j in range(8):
                if j + 1 < 8:
                    gf(j + 1)
                gate_back(j)
            k.op("pool", lambda e: e.memset(Ubf[0][:, 0:128], 0.0), [("vH", 0)], [("U", 0, 0)])
            memset("pool", Ubf[1][:, 31 * 128:32 * 128], 0.0, [("U", 1, 31)])
            pslot = 0
            for step in range(32):
                for dr in range(2):
                    c = step if dr == 0 else 31 - step
                    pb = pslot % 4
                    pslot += 1
                    pP = P[pb][:, 0:128]
                    ppk = "ps%d" % pb
                    mm(pP, ktok[dr][:, c * 128:(c + 1) * 128], vH[:, c * 128:(c + 1) * 128], True, True,
                       [("ktok", dr, c // 4), ("vH", c // 8)], [ppk])
                    Wn = Wst[dr][step % 2]
                    Wo = Wst[dr][(step + 1) % 2]
                    if step == 0:
                        cp("dve", Wn, pP, [ppk], [("W", dr, step % 2)])
                    else:
                        cprev = c - 1 if dr == 0 else c + 1
                        dcl = dec[dr][:, cprev:cprev + 1]
                        if dr == 0:
                            k.op("act", lambda e, o=Ubf[dr][:, c * 128:(c + 1) * 128], i_=Wo, s_=dcl:
                                 e.activation(out=o, in_=i_, func=AF.Copy, scale=s_),
                                 [("W", dr, (step + 1) % 2), ("dec", dr)], [("U", dr, c)])
                        else:
                            ts("dve", Ubf[dr][:, c * 128:(c + 1) * 128], Wo, dcl, ALU.mult,
                               [("W", dr, (step + 1) % 2), ("dec", dr)], [("U", dr, c)])
                        if step < 31:
                            stt(Wn, Wo, dcl, pP, ALU.mult, ALU.add,
                                [("W", dr, (step + 1) % 2), ("dec", dr), ppk], [("W", dr, step % 2)])
            def out_front(c):
                j = c // 4
                ch = slice(c * 128, (c + 1) * 128)
                pA = P[2 + c % 2][:, 0:256]
                pAk = "ps%d" % (2 + c % 2)
                for dr in range(2):
                    mm(pA[:, dr * 128:(dr + 1) * 128], ktT[dr][:, ch], qtT[dr][:, ch], True, True,
                       [("ktT", dr, j), ("qtT", dr, j)], [pAk], track=(dr == 1))
                tt("dve", AT[c % 2].rearrange("p (a b) -> p a b", b=128), pA.rearrange("p (a b) -> p a b", b=128),
                   triH, ALU.mult, [pAk, "const"], [("AT", c % 2)])

            def out_back(c):
                j = c // 4
                cc = c % 4
                ch = slice(c * 128, (c + 1) * 128)
                pob = 4 + j % 2
                po = P[pob]
                pok = "ps%d" % pob
                oc = po[:, cc * 128:(cc + 1) * 128]
                mm(oc, vH[:, ch], AT[c % 2][:, 0:128], True, False, [("vH", c // 8), ("AT", c % 2)], [pok], track=False)
                mm(oc, vH[:, ch], AT[c % 2][:, 128:256], False, False, [("vH", c // 8), ("AT", c % 2)], [pok], track=False)
                mm(oc, Ubf[0][:, ch], qtT[0][:, ch], False, False, [("U", 0, c), ("qtT", 0, j)], [pok], track=False)
                mm(oc, Ubf[1][:, ch], qtT[1][:, ch], False, True, [("U", 1, c), ("qtT", 1, j)], [pok], track=True)
                if cc == 3:
                    cols = slice(512 * j, 512 * (j + 1))
                    act(sqh, po[:, :], AF.Square, [pok], ["sqh"])
                    mm(P[6][:, :], ones128, sqh, True, True, ["sqh", "const"], ["ps6"])
                    act(lnh, P[6][:, :], AF.Ln, ["ps6"], ["lnh"], scale=1.0 / 128, bias=EPS)
                    act(lnh, lnh, AF.Exp, ["lnh"], ["lnh"], scale=-0.5)
                    stt(on, po[:, :], hgw[:, 0:1], lnh, ALU.mult, ALU.mult, [pok, "lnh", "hgw"], ["lnh"])
                    tt("pool", obS[j % 2], on, gbT[:, cols], ALU.mult, ["lnh", ("gbT", j)], [("obS", j % 2)])
                    k.dma("sp", ob_d[h * 128:(h + 1) * 128, cols], obS[j % 2], reads=[("obS", j % 2)], sem="d_obS%d" % (j % 2))
            out_front(0)
            for c in range(32):
                if c + 1 < 32:
                    out_front(c + 1)
                out_back(c)
            k.barrier()
        k.barrier()
        A.reset(m0)

    if 4 in phases:
        m0 = A.mark()
        wZ = A.alloc([128, 8, 2048], BF16)
        wpa = A.alloc([128, 4, 1024], BF16)
        wpb = A.alloc([128, 8, 1024], BF16)
        wo = A.alloc([128, 8, 1024], BF16)
        oaj = [A.alloc([128, 4, 512], BF16) for _ in range(2)]
        obj = [A.alloc([128, 8, 512], BF16) for _ in range(2)]
        mg_ = A.alloc([128, 8, 512], BF16)
        sa = [A.alloc([128, 512], F32) for _ in range(2)]
        sb2 = [A.alloc([128, 512], F32) for _ in range(2)]
        ma = [A.alloc([128, 512], F32) for _ in range(2)]
        mb = [A.alloc([128, 512], F32) for _ in range(2)]
        xr = [A.alloc([128, D], F32) for _ in range(2)]
        for seg in range(4):
            k.dma("pool", flat(wZ[:, 2 * seg:2 * seg + 2, :]), wz_d[:, seg * 4096:(seg + 1) * 4096], writes=["wZ"], sem="d_wZ")
        k.dma("pool", flat(wpa), wpa_d, writes=["wpa"])
        for hh in range(2):
            k.dma("pool", flat(wpb[:, hh * 4:(hh + 1) * 4, :]), wpb_d[:, hh * 4096:(hh + 1) * 4096], writes=["wpb"], sem="d_wpb")
            k.dma("pool", flat(wo[:, hh * 4:(hh + 1) * 4, :]), wo_d[:, hh * 4096:(hh + 1) * 4096], writes=["wo"], sem="d_wo")
        oa_v = oa_d.rearrange("(c p) t -> p c t", p=128)
        ob_v = ob_d.rearrange("(c p) t -> p c t", p=128)

        def load_ab(j_):
            cols_ = slice(512 * j_, 512 * (j_ + 1))
            k.dma("sp", oaj[j_ % 2], oa_v[:, :, cols_], writes=[("oaj", j_ % 2)])
            k.dma("sp", obj[j_ % 2], ob_v[:, :, cols_], writes=[("obj", j_ % 2)])
        load_ab(0)
        xcnt = 0
        ocnt = 0
        for j in range(8):
            sl = j % 2
            if j + 1 < 8:
                load_ab(j + 1)
            for dc in range(8):
                dcs = slice(dc * 128, (dc + 1) * 128)
                d2 = dc % 2
                pya, pyb = P[d2], P[2 + d2]
                pyak, pybk = "ps%d" % d2, "ps%d" % (2 + d2)
                for c in range(8):
                    mm(P[4][:, :], wZ[:, c, dcs], hT_nat(c, 512 * j, 512), c == 0, c == 7, ["wZ", "hT"], ["ps4"], track=(c == 7))
                for c in range(8):
                    mm(P[5][:, :], wZ[:, c, 1024 + dc * 128:1024 + (dc + 1) * 128], hT_nat(c, 512 * j, 512), c == 0, c == 7,
                       ["wZ", "hT"], ["ps5"], track=(c == 7))
                for c in range(4):
                    mm(pya[:, :], wpa[:, c, dcs], oaj[sl][:, c, :], c == 0, c == 3, ["wpa", ("oaj", sl)], [pyak], track=(c == 3))
                for c in range(8):
                    mm(pyb[:, :], wpb[:, c, dcs], obj[sl][:, c, :], c == 0, c == 7, ["wpb", ("obj", sl)], [pybk], track=(c == 7))
                act(sa[d2], P[4][:, :], AF.Sigmoid, ["ps4"], [("sa", d2)])
                act(sb2[d2], P[5][:, :], AF.Sigmoid, ["ps5"], [("sb2", d2)])
                tt("dve", ma[d2], pya[:, :], sa[d2], ALU.mult, [pyak, ("sa", d2)], [("ma", d2)])
                tt("dve", mb[d2], pyb[:, :], sb2[d2], ALU.mult, [pybk, ("sb2", d2)], [("mb", d2)])
                tt("pool", mg_[:, dc, :], ma[d2], mb[d2], ALU.add, [("ma", d2), ("mb", d2)], [("mg", dc)])
            for tq in range(4):
                t = 4 * j + tq
                xs = xcnt % 2
                xcnt += 1
                k.dma("sp", xr[xs], x_d[t * 128:(t + 1) * 128, :], writes=[("xr", xs)])
                for half in range(2):
                    pb = 4 + ocnt % 3
                    ocnt += 1
                    for c in range(8):
                        mm(P[pb][:, :], mg_[:, c, tq * 128:(tq + 1) * 128], wo[:, c, half * 512:(half + 1) * 512],
                           c == 0, c == 7, [("mg", c), "wo"], ["ps%d" % pb], track=(c == 7))
                    tt("dve", xr[xs][:, half * 512:(half + 1) * 512], P[pb][:, :], xr[xs][:, half * 512:(half + 1) * 512],
                       ALU.add, ["ps%d" % pb, ("xr", xs)], [("xr", xs)])
                k.dma("sp", out_d[t * 128:(t + 1) * 128, :], xr[xs], reads=[("xr", xs)], sem="d_ot%d" % xs)
    k.finish("sp")
    print("instructions:", k.ninstr, "arena peak:", A.peak, {e: k.count[e] for e in ENGS})
    k.emit()
    k.close()
    es.close()
    return nc


def _bias_tiles(rel_bias):
    a = np.arange(128)[:, None]
    c = np.arange(128)[None, :]
    out = np.zeros((128, 3, 8, 2, 128), np.float32)
    for g, (_, d) in enumerate(GROUPS):
        for ab in range(2):
            rel = (a - 64 - c) if ab == 0 else (a + 64 - c)
            rel = np.clip(rel, -64, 64)
            idx = t5_bucket(rel * d)
            for h in range(8):
                out[:, g, h, ab, :] = rel_bias[idx, g * 8 + h]
    return out.reshape(128, -1)


_NC_CACHE = {}


def kernel(x, norm_w, w_in, q_norm_w, k_norm_w, rel_bias, lb_fwd, lb_bwd, hg_norm_w,
           w_proj_a, w_proj_b, w_out, _debug=False, _phases=(1, 2, 3, 4)):
    x = np.asarray(x, np.float32)
    B = x.shape[0]
    key = (_debug, tuple(_phases))
    if key not in _NC_CACHE:
        _NC_CACHE[key] = build_program(_debug, _phases)
    nc = _NC_CACHE[key]
    qkw = np.zeros((128, 6), np.float32)
    for g in range(3):
        qkw[:, 2 * g] = np.tile(np.asarray(q_norm_w, np.float32)[0, g], 2)
        qkw[:, 2 * g + 1] = np.tile(np.asarray(k_norm_w, np.float32)[0, g], 2)
    lbf = np.ascontiguousarray(np.asarray(lb_fwd, np.float32).reshape(2, 8, 128).transpose(2, 0, 1).reshape(128, 16))
    lbb = np.ascontiguousarray(np.asarray(lb_bwd, np.float32).reshape(2, 8, 128).transpose(2, 0, 1).reshape(128, 16))
    w_in0 = np.asarray(w_in, np.float32)[0]
    wv = np.ascontiguousarray(w_in0.reshape(8, 128, IN_COLS).transpose(1, 0, 2))
    ar = np.arange(128)
    w_att = np.empty((12, 128, 8, 384), np.float32)
    for hp in range(4):
        for g in range(3):
            cols = np.concatenate([g * 1536 + seg * 512 + hp * 128 + ar for seg in range(3)])
            w_att[hp * 3 + g] = wv[:, :, cols]
    w_h = np.empty((8, 128, 8, 640), np.float32)
    for h in range(8):
        cols = np.concatenate([5120 + seg * 1024 + h * 128 + ar for seg in range(5)])
        w_h[h] = wv[:, :, cols]

    def pmaj(wm):
        nchunk = wm.shape[0] // 128
        return np.ascontiguousarray(wm.reshape(nchunk, 128, wm.shape[1]).transpose(1, 0, 2).reshape(128, -1))
    common = {
        "w_att": w_att.reshape(12, 128, -1),
        "w_g": np.ascontiguousarray(wv[:, :, 4608:5120]).reshape(128, -1),
        "w_h": w_h.reshape(8, 128, -1),
        "w_z": np.ascontiguousarray(wv[:, :, 10240:12288]).reshape(128, -1),
        "norm_w": np.ascontiguousarray(np.asarray(norm_w, np.float32)[0:1]),
        "qkw": qkw,
        "biasT": _bias_tiles(np.asarray(rel_bias, np.float32)),
        "lbf": lbf, "lbb": lbb,
        "hgw": np.ascontiguousarray(np.asarray(hg_norm_w, np.float32)[0].reshape(128, 1)),
        "w_pa": pmaj(np.asarray(w_proj_a, np.float32)[0]),
        "w_pb": pmaj(np.asarray(w_proj_b, np.float32)[0]),
        "w_o": pmaj(np.asarray(w_out, np.float32)[0]),
    }
    in_maps = []
    for b in range(B):
        m = dict(common)
        m["x"] = np.ascontiguousarray(x[b])
        in_maps.append(m)
    res = run_bass_kernel_spmd(nc, in_maps, core_ids=list(range(B)))
    if _debug:
        return res.results
    return np.stack([np.asarray(r["out"], np.float32) for r in res.results], axis=0)
```

```python
import numpy as np
from contextlib import ExitStack
import concourse.bass as bass
import concourse.mybir as mybir
from concourse.bass_utils import run_bass_kernel_spmd

F32 = mybir.dt.float32
BF16 = mybir.dt.bfloat16
U8 = mybir.dt.uint8
AF = mybir.ActivationFunctionType
ALU = mybir.AluOpType

S = 4096
D = 1024
PAD = 1024
EPS = 1e-6
GROUPS = ((128, 1), (512, 4), (2048, 16))
NUM_BUCKETS = 32
REL_MAX_DISTANCE = 1024
IN_COLS = 12288
ENGS = ("pe", "act", "dve", "pool", "sp")
DBG = {}


class KB:
    def __init__(self, nc):
        self.nc = nc
        self.prog = {e: [] for e in ENGS}
        self.count = {}
        self.sems = {}
        self.seen = {e: {} for e in ENGS}
        self.snap = {}
        self.last_w = {}
        self.readers = {}
        self.pending = {e: ([], []) for e in ENGS}
        self._ctx = []
        self.ninstr = 0
        for e in ENGS:
            self._mksem(e)

    def _mksem(self, name):
        cm = self.nc.semaphore("s_" + name)
        h = cm.__enter__()
        self._ctx.append(cm)
        self.sems[name] = h
        self.count[name] = 0
        return h

    def close(self):
        for cm in reversed(self._ctx):
            cm.__exit__(None, None, None)

    def _deps(self, e, reads, writes):
        deps = {}

        def add(cv):
            if cv is None:
                return
            c, v = cv
            if deps.get(c, 0) < v:
                deps[c] = v
        for r in reads:
            add(self.last_w.get(r))
        for w in writes:
            add(self.last_w.get(w))
            for rd in self.readers.get(w, ()):
                add(rd)
        waits = []
        seen = self.seen[e]
        for c, v in deps.items():
            if c == e and e == "pe":
                continue
            if seen.get(c, 0) < v:
                waits.append((c, v))
                seen[c] = v
                sn = self.snap.get((c, v))
                if sn:
                    for c2, v2 in sn.items():
                        if c2 != e and seen.get(c2, 0) < v2:
                            seen[c2] = v2
        return waits

    def _commit(self, token, reads, writes):
        for r in reads:
            self.readers.setdefault(r, []).append(token)
        for w in writes:
            self.last_w[w] = token
            self.readers[w] = []

    def op(self, e, fn, reads=(), writes=(), track=True):
        reads = tuple(reads)
        writes = tuple(writes)
        waits = self._deps(e, reads, writes)
        self.ninstr += 1
        if track:
            self.count[e] += 1
            token = (e, self.count[e])
            pr, pw = self.pending[e]
            self._commit(token, list(reads) + pr, list(writes) + pw)
            self.pending[e] = ([], [])
            self.snap[token] = dict(self.seen[e])
            self.prog[e].append((waits, fn, (e, 1)))
        else:
            pr, pw = self.pending[e]
            pr.extend(reads)
            pw.extend(writes)
            self.prog[e].append((waits, fn, None))

    def dma(self, q, out, in_, reads=(), writes=(), sem=None):
        reads = tuple(reads)
        writes = tuple(writes)
        if sem is None:
            sem = "d_" + str(writes[0] if writes else reads[0])
        if sem not in self.sems:
            self._mksem(sem)
        waits = self._deps(q, reads, writes)
        self.count[sem] += 16
        token = (sem, self.count[sem])
        self._commit(token, reads, writes)
        self.snap[token] = dict(self.seen[q])
        self.ninstr += 1

        def fn(eng, out=out, in_=in_):
            return eng.dma_start(out=out, in_=in_)
        self.prog[q].append((waits, fn, (sem, 16)))
        return token

    def barrier(self):
        for e in ENGS:
            waits = []
            for c, v in self.count.items():
                if v > 0 and self.seen[e].get(c, 0) < v:
                    if c == e and e in ("pe", "sp"):
                        continue
                    waits.append((c, v))
                    self.seen[e][c] = v
            if waits:
                self.prog[e].append((waits, None, None))
        self.last_w = {}
        self.readers = {}

    def finish(self, e="sp"):
        waits = []
        for c, v in self.count.items():
            if v > 0 and self.seen[e].get(c, 0) < v and c != e:
                waits.append((c, v))
        self.prog[e].append((waits, None, None))

    def emit(self):
        nc = self.nc
        with nc.Block() as block:
            for e in ENGS:
                items = self.prog[e]
                if not items:
                    continue

                def body(eng, items=items):
                    for waits, fn, inc in items:
                        for c, v in waits:
                            eng.wait_ge(self.sems[c], v)
                        if fn is None:
                            continue
                        ins = fn(eng)
                        if inc is not None:
                            ins.then_inc(self.sems[inc[0]], inc[1])
                reg = {"pe": block.tensor, "act": block.scalar, "dve": block.vector,
                       "pool": block.gpsimd, "sp": block.sync}[e]
                reg(body)


class Arena:
    def __init__(self, ap, nbytes):
        self.ap = ap
        self.nbytes = nbytes
        self.off = 0
        self.peak = 0

    def alloc(self, shape, dt):
        n = int(np.prod(shape[1:]))
        nb = n * (4 if dt == F32 else 2)
        nb = (nb + 63) // 64 * 64
        assert self.off + nb <= self.nbytes, ("SBUF arena overflow", self.off, nb)
        a = self.ap[:, self.off:self.off + nb].bitcast(dt)[:, 0:n]
        self.off += nb
        self.peak = max(self.peak, self.off)
        if len(shape) == 3:
            a = a.rearrange("p (a b) -> p a b", b=shape[2])
        return a

    def mark(self):
        return self.off

    def reset(self, m):
        self.off = m


def t5_bucket(rel):
    half = NUM_BUCKETS // 2
    max_exact = half // 2
    n = np.abs(rel)
    large = max_exact + (np.log(np.maximum(n, 1) / max_exact)
                         / np.log(REL_MAX_DISTANCE / max_exact) * (half - max_exact)).astype(np.int32)
    large = np.minimum(large, half - 1)
    return np.where(rel > 0, half, 0) + np.where(n < max_exact, n, large)


def build_program(debug=False, phases=(1, 2, 3, 4)):
    nc = bass.Bass("TRN2", target_bir_lowering=False)
    x_d = nc.dram_tensor("x", [S, D], F32, kind="ExternalInput").ap()
    watt_d = nc.dram_tensor("w_att", [12, 128, 8 * 384], F32, kind="ExternalInput").ap()
    wg_d = nc.dram_tensor("w_g", [128, 8 * 512], F32, kind="ExternalInput").ap()
    wh_d = nc.dram_tensor("w_h", [8, 128, 8 * 640], F32, kind="ExternalInput").ap()
    wz_d = nc.dram_tensor("w_z", [128, 8 * 2048], F32, kind="ExternalInput").ap()
    normw_d = nc.dram_tensor("norm_w", [1, D], F32, kind="ExternalInput").ap()
    qkw_d = nc.dram_tensor("qkw", [128, 6], F32, kind="ExternalInput").ap()
    bias_d = nc.dram_tensor("biasT", [128, 3 * 8 * 2 * 128], F32, kind="ExternalInput").ap()
    lbf_d = nc.dram_tensor("lbf", [128, 16], F32, kind="ExternalInput").ap()
    lbb_d = nc.dram_tensor("lbb", [128, 16], F32, kind="ExternalInput").ap()
    hgw_d = nc.dram_tensor("hgw", [128, 1], F32, kind="ExternalInput").ap()
    wpa_d = nc.dram_tensor("w_pa", [128, 4 * D], F32, kind="ExternalInput").ap()
    wpb_d = nc.dram_tensor("w_pb", [128, 8 * D], F32, kind="ExternalInput").ap()
    wo_d = nc.dram_tensor("w_o", [128, 8 * D], F32, kind="ExternalInput").ap()
    out_d = nc.dram_tensor("out", [S, D], F32, kind="ExternalOutput").ap()
    skind = "ExternalOutput" if debug else "Internal"
    oa_d = nc.dram_tensor("oa_scr", [512, S], BF16, kind=skind).ap()
    ob_d = nc.dram_tensor("ob_scr", [D, S], BF16, kind=skind).ap()
    if debug:
        hT_dbg = nc.dram_tensor("hT_dbg", [128, 8 * S], BF16, kind="ExternalOutput").ap()

    def flat(ap3):
        return ap3.rearrange("p c n -> p (c n)")

    es = ExitStack()
    ARENA_BYTES = 209920
    arena_t = es.enter_context(nc.sbuf_tensor("arena", [128, ARENA_BYTES], U8))
    A = Arena(arena_t, ARENA_BYTES)
    P = [es.enter_context(nc.psum_tensor("ps%d" % i, [128, 512], F32)) for i in range(7)]
    PT = es.enter_context(nc.psum_tensor("pst", [128, 1024], BF16))
    k = KB(nc)

    def mm(out, lhsT, rhs, start, stop, reads, writes, track=True, tp=None):
        def fn(e):
            if tp is None:
                return e.matmul(out, lhsT=lhsT, rhs=rhs, start=start, stop=stop)
            return e.matmul(out, lhsT=lhsT, rhs=rhs, start=start, stop=stop, tile_position=tp)
        k.op("pe", fn, reads, writes, track)

    def tr(out, in_, reads, writes, track=True):
        k.op("pe", lambda e: e.transpose(out, in_, ident), tuple(reads) + ("const",), writes, track)

    def act(out, in_, func, reads, writes, scale=None, bias=None, accum=None):
        kw = {}
        if scale is not None:
            kw["scale"] = scale
        if bias is not None:
            kw["bias"] = bias
        if accum is not None:
            kw["accum_out"] = accum
        k.op("act", lambda e: e.activation(out=out, in_=in_, func=func, **kw), reads, writes)

    def tt(eng, out, in0, in1, op, reads, writes):
        k.op(eng, lambda e: e.tensor_tensor(out=out, in0=in0, in1=in1, op=op), reads, writes)

    def ts(eng, out, in0, s1, op0, reads, writes, s2=None, op1=None):
        if op1 is None:
            k.op(eng, lambda e: e.tensor_scalar(out=out, in0=in0, scalar1=s1, scalar2=None, op0=op0), reads, writes)
        else:
            k.op(eng, lambda e: e.tensor_scalar(out=out, in0=in0, scalar1=s1, scalar2=s2, op0=op0, op1=op1), reads, writes)

    def stt(out, in0, scalar, in1, op0, op1, reads, writes):
        k.op("dve", lambda e: e.scalar_tensor_tensor(out=out, in0=in0, scalar=scalar, in1=in1, op0=op0, op1=op1),
             reads, writes)

    def cp(eng, out, in_, reads, writes):
        if eng == "act":
            k.op("act", lambda e: e.copy(out=out, in_=in_), reads, writes)
        else:
            k.op(eng, lambda e: e.tensor_copy(out=out, in_=in_), reads, writes)

    def recip(out, in_, reads, writes):
        k.op("dve", lambda e: e.reciprocal(out=out, in_=in_), reads, writes)

    def memset(eng, ap, val, writes):
        k.op(eng, lambda e: e.memset(ap, val), (), writes)

    hTf = A.alloc([128, PAD + 8 * S + PAD], BF16)
    ident = A.alloc([128, 128], BF16)
    ones_bd = A.alloc([128, 128], BF16)
    ones128 = A.alloc([128, 128], BF16)
    tri4 = A.alloc([128, 3, 128], F32)
    scanmask = A.alloc([128, 512], F32)
    qkw = A.alloc([128, 6], F32)
    lbt = A.alloc([128, 32], F32)
    lbd = A.alloc([128, 16], F32)
    oml = A.alloc([128, 16], F32)
    hgw = A.alloc([128, 1], F32)
    persist_mark = A.mark()

    def hT_nat(c, t0, n):
        o = PAD + c * S + t0
        return hTf[:, o:o + n]

    def hT_perm(c, d, n0, cnt):
        L = S // d
        r0 = n0 // L
        l0 = n0 % L
        base = PAD + c * S
        if d == 1:
            return hTf[:, base + n0: base + n0 + cnt], None
        if cnt <= L - l0:
            st = base + l0 * d + r0
            return hTf[:, st: st + (cnt - 1) * d + 1: d], None
        assert l0 == 0 and cnt % L == 0
        nres = cnt // L
        st = base + r0
        v = hTf[:, st: st + L * d].rearrange("p (l r) -> p r l", r=d)[:, 0:nres, :]
        return v, L

    def hT_vtile(c, d, r, i):
        st = PAD + c * S + (128 * i - 64) * d + r
        if d == 1:
            return hTf[:, st: st + 128]
        return hTf[:, st: st + 127 * d + 1: d]

    def acc_perm(acc, d, n0, cnt):
        L = S // d
        r0 = n0 // L
        l0 = n0 % L
        if d == 1:
            return acc[:, n0:n0 + cnt], None
        if cnt <= L - l0:
            st = l0 * d + r0
            return acc[:, st: st + (cnt - 1) * d + 1: d], None
        assert l0 == 0 and cnt % L == 0
        nres = cnt // L
        v = acc.rearrange("p (l r) -> p r l", r=d)[:, r0:r0 + nres, :]
        return v, L

    memset("pool", hTf[:, 0:PAD], 0.0, ["hTpad"])
    memset("pool", hTf[:, PAD + 8 * S: PAD + 8 * S + PAD], 0.0, ["hTpad"])
    memset("pool", ident, 0.0, ["const"])
    k.op("pool", lambda e: e.affine_select(out=ident, in_=ident, pattern=[[-1, 128]], compare_op=ALU.not_equal,
                                           fill=1.0, base=0, channel_multiplier=1), ["const"], ["const"])
    memset("pool", ones_bd, 0.0, ["const"])
    memset("pool", ones_bd[0:64, 0:64], 1.0, ["const"])
    memset("pool", ones_bd[64:128, 64:128], 1.0, ["const"])
    memset("pool", ones128, 1.0, ["const"])
    memset("pool", tri4, 1.0, ["const"])
    for idx in (0, 2):
        k.op("pool", lambda e, idx=idx: e.affine_select(out=tri4[:, idx, :], in_=tri4[:, idx, :], pattern=[[-1, 128]],
                                                        compare_op=ALU.is_ge, fill=0.0, base=0, channel_multiplier=1),
             ["const"], ["const"])
    k.op("pool", lambda e: e.affine_select(out=tri4[:, 1, :], in_=tri4[:, 1, :], pattern=[[1, 128]],
                                           compare_op=ALU.is_ge, fill=0.0, base=0, channel_multiplier=-1),
         ["const"], ["const"])
    memset("pool", scanmask, 1.0, ["const"])
    memset("pool", scanmask.rearrange("p (c t) -> p c t", t=128)[:, :, 0:1], 0.0, ["const"])
    k.dma("sp", qkw, qkw_d, writes=["qkw"])
    k.dma("sp", lbt[:, 0:16], lbf_d, writes=["lbt"], sem="d_lbt")
    k.dma("sp", lbt[:, 16:32], lbb_d, writes=["lbt"], sem="d_lbt")
    k.dma("sp", hgw, hgw_d, writes=["hgw"])
    for g in range(3):
        ts("dve", qkw[:, 2 * g + 1:2 * g + 2], qkw[:, 2 * g + 1:2 * g + 2], 8.0, ALU.mult, ["qkw"], ["qkw"])
    lbv = lbt.rearrange("p (d l h) -> p d l h", d=2, l=2)
    for dd in range(2):
        tt("dve", lbd[:, dd * 8:(dd + 1) * 8], lbv[:, dd, 0, :], lbv[:, dd, 1, :], ALU.subtract, ["lbt"], ["lbd"])
    act(oml, lbd, AF.Sigmoid, ["lbd"], ["oml"], scale=-1.0)

    if 1 in phases:
        m0 = A.mark()
        normw_rep = A.alloc([128, D], F32)
        xts = [A.alloc([128, D], F32) for _ in range(2)]
        xn = [A.alloc([128, D], BF16) for _ in range(2)]
        junk = A.alloc([128, D], BF16)
        ssq = A.alloc([128, 32], F32)
        rstd = A.alloc([128, 32], F32)
        k.dma("sp", normw_rep, normw_d.partition_broadcast(128), writes=["normw"])
        memset("pool", ssq, 0.0, [("ssq", t) for t in range(32)])
        hT_ct = hTf[:, PAD:PAD + 8 * S].rearrange("p (c t) -> p c t", t=S)
        for t in range(32):
            sl = t % 2
            k.dma("sp" if sl == 0 else "act", xts[sl], x_d[t * 128:(t + 1) * 128, :], writes=[("xt", sl)])
            act(junk, xts[sl], AF.Square, [("xt", sl)], ["junk", ("ssq", t)], accum=ssq[:, t:t + 1])
            ts("dve", rstd[:, t:t + 1], ssq[:, t:t + 1], 1.0 / D, ALU.mult, [("ssq", t)], [("rstd", t)], s2=EPS, op1=ALU.add)
            act(rstd[:, t:t + 1], rstd[:, t:t + 1], AF.Sqrt, [("rstd", t)], [("rstd", t)])
            recip(rstd[:, t:t + 1], rstd[:, t:t + 1], [("rstd", t)], [("rstd", t)])
            stt(xn[sl], xts[sl], rstd[:, t:t + 1], normw_rep, ALU.mult, ALU.mult,
                [("xt", sl), ("rstd", t), "normw"], [("xn", sl)])
            for c in range(8):
                tr(PT[:, c * 128:(c + 1) * 128], xn[sl][:, c * 128:(c + 1) * 128], [("xn", sl)], ["PT"], track=(c == 7))
            cp("act" if t % 2 == 0 else "dve", hT_ct[:, :, t * 128:(t + 1) * 128],
               PT[:, :].rearrange("p (c t) -> p c t", t=128), ["PT"], [("hT", t)])
        k.barrier()
        A.reset(m0)
        if debug:
            k.dma("sp", hT_dbg, hTf[:, PAD:PAD + 8 * S], reads=["x"], sem="dbg")

    if 2 in phases:
        m0 = A.mark()
        wA = [A.alloc([128, 8, 384], BF16) for _ in range(2)]
        wG = A.alloc([128, 8, 512], BF16)
        qn = A.alloc([128, S], BF16)
        knp = A.alloc([128, 64 + S + 64], BF16)
        vTp = A.alloc([128, 64 + S + 64], BF16)
        vP = A.alloc([128, 33 * 256], BF16)
        acc = [A.alloc([128, S], F32) for _ in range(2)]
        bM = A.alloc([128, 512], F32)
        Mrep = A.alloc([128, 1024], F32)
        sq = [[A.alloc([128, 512], BF16) for _ in range(2)] for _ in range(2)]
        lnb = [A.alloc([128, 512], F32) for _ in range(2)]
        ee = [A.alloc([128, 512], F32) for _ in range(3)]
        pT = [A.alloc([128, 512], BF16) for _ in range(3)]
        sgh = [A.alloc([128, 512], F32) for _ in range(2)]
        rD = [A.alloc([128, 512], F32) for _ in range(2)]
        t1 = [A.alloc([128, 512], F32) for _ in range(2)]
        oaS = [A.alloc([128, 512], BF16) for _ in range(4)]
        zqk = [[rD[0], rD[1]], [t1[0], t1[1]]]
        vP4 = vP.rearrange("p (t h d) -> p t h d", h=2, d=128)
        memset("pool", knp[:, 0:64], 0.0, [("kn", -1)])
        memset("pool", knp[:, 64 + S:64 + S + 64], 0.0, [("kn", 8)])
        memset("pool", vTp[:, 0:64], 0.0, [("vT", -1)])
        memset("pool", vTp[:, 64 + S:64 + S + 64], 0.0, [("vT", 8)])
        memset("pool", vP4[:, :, :, 64:128], 1.0, ["vPones"])
        k.dma("pool", flat(wG), wg_d, writes=["wG"], sem="d_wG")
        maskAB = tri4[:, 0:2, :]
        units = [(hp, g) for hp in range(DBG.get("nhp", 4)) for g in range(DBG.get("ng", 3))]

        def load_unit_w(u):
            hp_, g_ = units[u]
            slot_ = u % 2
            k.dma("pool", flat(wA[slot_]), watt_d[hp_ * 3 + g_], writes=[("wA", slot_)], sem="d_wA%d" % slot_)
        load_unit_w(0)
        sccnt = 0
        oacnt = 0
        for u, (hp, g) in enumerate(units):
            win_, d = GROUPS[g]
            L = S // d
            TPS = L // 128
            slot = u % 2
            w = wA[slot]
            if u + 1 < len(units):
                load_unit_w(u + 1)
            bcol = (g * 8 + 2 * hp) * 256
            k.dma("sp", bM, bias_d[:, bcol:bcol + 512], writes=["bM"])
            act(bM, bM, AF.Exp, ["bM"], ["bM"])
            bMv = bM.rearrange("p (h a c) -> p h a c", h=2, a=2)
            Mv = Mrep.rearrange("p (h m a c) -> p h m a c", h=2, m=2, a=2)
            for hl in range(2):
                for mi in range(2):
                    tt("pool", Mv[:, hl, mi], bMv[:, hl], maskAB, ALU.mult, ["bM", "const"], ["Mrep"])

            nl = 512 // d

            def pview(ap_nat):
                if d == 1:
                    return ap_nat
                return ap_nat.rearrange("p (l r) -> p r l", r=d)

            def dview(buf, off, j):
                if d == 1:
                    return buf[:, off + 512 * j: off + 512 * (j + 1)]
                return buf[:, off:off + S].rearrange("p (r l) -> p r l", l=L)[:, :, nl * j: nl * (j + 1)]

            def wkeys(nm, j):
                return [(nm, j)] if d == 1 else [(nm, jj) for jj in range(8)]

            def proj_front(j):
                pqb, pkb, pvb = j % 2, 2 + j % 2, 4
                for (pb, woff) in ((pqb, 0), (pkb, 128), (pvb, 256)):
                    for c in range(8):
                        mm(P[pb][:, :], w[:, c, woff:woff + 128], hT_nat(c, 512 * j, 512), c == 0, c == 7,
                           [("wA", slot), "hT"], ["ps%d" % pb], track=(c == 7))
                act(sq[0][j % 2], P[pqb][:, :], AF.Square, ["ps%d" % pqb], [("sq", 0, j % 2)])
                cp("act", zqk[0][j % 2], P[pqb][:, :], ["ps%d" % pqb], [("zqk", 0, j % 2)])
                act(sq[1][j % 2], P[pkb][:, :], AF.Square, ["ps%d" % pkb], [("sq", 1, j % 2)])
                cp("act", zqk[1][j % 2], P[pkb][:, :], ["ps%d" % pkb], [("zqk", 1, j % 2)])
                cp("dve", dview(vTp, 64, j), pview(P[pvb][:, :]), ["ps%d" % pvb], wkeys("vT", j))

            def proj_back(j):
                pqb, pkb = j % 2, 2 + j % 2
                items = ((pqb, 2 * g, dview(qn, 0, j), wkeys("qn", j)),
                         (pkb, 2 * g + 1, dview(knp, 64, j), wkeys("kn", j)))
                for qi in range(2):
                    mm(P[5 + qi][:, :], ones_bd, sq[qi][j % 2], True, True, [("sq", qi, j % 2), "const"], ["ps%d" % (5 + qi)])
                for qi in range(2):
                    act(lnb[qi], P[5 + qi][:, :], AF.Ln, ["ps%d" % (5 + qi)], [("lnb", qi)], bias=64.0 * EPS)
                for qi in range(2):
                    act(lnb[qi], lnb[qi], AF.Exp, [("lnb", qi)], [("lnb", qi)], scale=-0.5)
                for qi, (pb, wcol, dst, dkeys) in enumerate(items):
                    stt(dst, pview(zqk[qi][j % 2]), qkw[:, wcol:wcol + 1], pview(lnb[qi]), ALU.mult, ALU.mult,
                        [("zqk", qi, j % 2), ("lnb", qi), "qkw"], dkeys)
            if "q" in DBG.get("st", "qvaf"):
                proj_front(0)
                for j in range(8):
                    if j + 1 < 8:
                        proj_front(j + 1)
                    proj_back(j)
                for t0_ in range(0, 33, 8):
                    nb = min(8, 33 - t0_)
                    for s_ in range(nb):
                        vt = t0_ + s_
                        lo = (128 * vt - 64) // 512 if vt > 0 else -1
                        hi = min((128 * vt + 63) // 512, 8)
                        tr(PT[:, s_ * 128:(s_ + 1) * 128], vTp[:, 128 * vt:128 * vt + 128],
                           [("vT", lo), ("vT", hi)], ["PT"], track=(s_ == nb - 1))
                    cp("act", vP4[:, t0_:t0_ + nb, :, 0:64],
                       PT[:, 0:nb * 128].rearrange("p (t h d) -> p t h d", h=2, d=64), ["PT"], [("vP", t0_ // 8)])

            its = [(mg, mp, hl) for mg in range(8) for mp in range(2) for hl in range(2)]
            state = {}

            def att_front(ii):
                nonlocal sccnt
                mg, mp, hl = its[ii]
                rows = slice(hl * 64, hl * 64 + 64)
                sb_ = sccnt % 3
                sl2 = sccnt % 3
                sccnt += 1
                state[ii] = (sb_, sl2)
                psc = P[sb_]
                psck = "ps%d" % sb_
                for mi in range(2):
                    m = mg * 4 + mp * 2 + mi
                    jq = m // 4
                    for ab in range(2):
                        kst = 128 * m + 128 * ab
                        kj = [("kn", (kst - 64) // 512 if kst >= 64 else -1), ("kn", min((kst + 127 - 64) // 512, 8))]
                        mm(psc[:, (mi * 2 + ab) * 128:(mi * 2 + ab + 1) * 128],
                           knp[rows, kst:kst + 128], qn[rows, 128 * m:128 * m + 128], True, True,
                           [("qn", jq)] + kj, [psck], track=(mi == 1 and ab == 1), tp=(hl * 64, 0))
                act(ee[sl2], psc[:, :], AF.Exp, [psck], [("ee", sl2)])
                tt("dve" if ii % 2 == 0 else "pool", pT[sl2], ee[sl2], Mrep[:, hl * 512:(hl + 1) * 512], ALU.mult,
                   [("ee", sl2), "Mrep"], [("pT", sl2)])

            def att_back(ii):
                mg, mp, hl = its[ii]
                sb_, sl2 = state[ii]
                pb = 3 + (mg % 2) * 2 + hl
                pnd = P[pb]
                pndk = "ps%d" % pb
                for mi in range(2):
                    m = mg * 4 + mp * 2 + mi
                    n0 = 128 * m
                    i = (n0 % L) // 128
                    first = (i == 0)
                    last = (i == TPS - 1)
                    vA = m
                    vB = m + 1
                    col = (mp * 2 + mi) * 128
                    KA = slice(64, 128) if first else slice(0, 128)
                    KBs = slice(0, 64) if last else slice(0, 128)
                    vkeys = [("vP", vA // 8), ("vP", vB // 8), "vPones"]
                    mm(pnd[:, col:col + 128], vP4[KA, vA, hl, :],
                       pT[sl2][KA, (mi * 2) * 128:(mi * 2 + 1) * 128], True, False,
                       [("pT", sl2)] + vkeys, [pndk], track=False, tp=(KA.start, 0))
                    mm(pnd[:, col:col + 128], vP4[KBs, vB, hl, :],
                       pT[sl2][KBs, (mi * 2 + 1) * 128:(mi * 2 + 2) * 128], False, True,
                       [("pT", sl2)] + vkeys, [pndk], track=(mi == 1), tp=(0, 0))
                if mp == 1:
                    av, inner = acc_perm(acc[hl], d, 512 * mg, 512)
                    pv_ = pnd[:, :] if inner is None else pnd[:, :].rearrange("p (a b) -> p a b", b=inner)
                    akey = ("acc", hl)
                    if g == 0:
                        cp("act", av, pv_, [pndk], [akey])
                    else:
                        tt("dve", av, av, pv_, ALU.add, [pndk, akey], [akey])
            if "a" in DBG.get("st", "qvaf"):
                LA = 2
                for ii in range(min(LA, len(its))):
                    att_front(ii)
                for ii in range(len(its)):
                    if ii + LA < len(its):
                        att_front(ii + LA)
                    att_back(ii)

            if g == len([1 for (hp_, g_) in units if hp_ == hp]) - 1 and "f" in DBG.get("st", "qvaf"):
                for j in range(8):
                    pg = P[j % 3]
                    pgk = "ps%d" % (j % 3)
                    for c in range(8):
                        mm(pg[:, :], wG[:, c, hp * 128:(hp + 1) * 128], hT_nat(c, 512 * j, 512), c == 0, c == 7,
                           ["wG", "hT"], [pgk], track=(c == 7))
                    HL = (0, 1)
                    for hl in HL:
                        act(sgh[hl][0:64, :], pg[hl * 64:(hl + 1) * 64, :], AF.Silu, [pgk], [("sgh", hl)])
                    for hl in HL:
                        act(rD[hl][0:64, :], acc[hl][64:128, 512 * j:512 * (j + 1)], AF.Ln, [("acc", hl)], [("rD", hl)])
                    for hl in HL:
                        act(rD[hl][0:64, :], rD[hl][0:64, :], AF.Exp, [("rD", hl)], [("rD", hl)], scale=-1.0)
                    for hl in HL:
                        tt("pool", t1[hl][0:64, :], acc[hl][0:64, 512 * j:512 * (j + 1)], rD[hl][0:64, :], ALU.mult,
                           [("acc", hl), ("rD", hl)], [("t1", hl)])
                    for hl in HL:
                        os_ = oacnt % 4
                        oacnt += 1
                        tt("dve", oaS[os_][0:64, :], t1[hl][0:64, :], sgh[hl][0:64, :], ALU.mult,
                           [("t1", hl), ("sgh", hl)], [("oaS", os_)])
                        k.dma("sp", oa_d[(2 * hp + hl) * 64:(2 * hp + hl + 1) * 64, 512 * j:512 * (j + 1)], oaS[os_][0:64, :],
                              reads=[("oaS", os_)], sem="d_oaS%d" % os_)
                k.barrier()
        k.barrier()
        A.reset(m0)

    if 3 in phases:
        m0 = A.mark()
        wH = [A.alloc([128, 8, 640], BF16) for _ in range(2)]
        vH = A.alloc([128, S], BF16)
        gbT = A.alloc([128, S], BF16)
        qtT = [A.alloc([128, S], BF16) for _ in range(2)]
        ktT = [A.alloc([128, S], BF16) for _ in range(2)]
        ktok = [A.alloc([128, S], BF16) for _ in range(2)]
        dec = [A.alloc([128, 32], F32) for _ in range(2)]
        Ubf = [A.alloc([128, S], BF16) for _ in range(2)]
        iT = Ubf[0]
        Wst = [[A.alloc([128, 128], F32) for _ in range(2)] for _ in range(2)]
        su = [A.alloc([128, 512], F32) for _ in range(2)]
        sL1 = [A.alloc([128, 512], F32) for _ in range(2)]
        sg_ = [A.alloc([128, 512], F32) for _ in range(2)]
        sb_ = [A.alloc([128, 512], F32) for _ in range(2)]
        sr = [A.alloc([128, 512], F32) for _ in range(2)]
        se = [A.alloc([128, 512], F32) for _ in range(2)]
        AT = [A.alloc([128, 256], BF16) for _ in range(2)]
        sqh = A.alloc([128, 512], BF16)
        lnh = A.alloc([128, 512], F32)
        on = lnh
        obS = [A.alloc([128, 512], BF16) for _ in range(2)]
        lbv2 = A.alloc([128, 16], F32)
        act(lbv2, lbd, AF.Sigmoid, ["lbd"], ["lbv2"])
        noml = A.alloc([128, 16], F32)
        ts("dve", noml, oml, -1.0, ALU.mult, ["oml"], ["noml"])
        triH = tri4[:, 1:3, :]
        NH = DBG.get("nh", 8)

        def load_wH(h_):
            k.dma("pool", flat(wH[h_ % 2]), wh_d[h_], writes=[("wH", h_ % 2)], sem="d_wH%d" % (h_ % 2))
        if DBG.get("pref", 1):
            load_wH(0)
        for h in range(NH):
            hs = h % 2
            w = wH[hs]
            if DBG.get("pref", 1):
                if h + 1 < NH:
                    load_wH(h + 1)
            else:
                load_wH(h)
            for j in range(8):
                cols = slice(512 * j, 512 * (j + 1))
                pb = j % 2
                for c in range(8):
                    mm(P[pb][:, :], w[:, c, 512:640], hT_nat(c, 512 * j, 512), c == 0, c == 7,
                       [("wH", hs), "hT"], ["ps%d" % pb], track=(c == 7))
                act(gbT[:, cols], P[pb][:, :], AF.Silu, ["ps%d" % pb], [("gbT", j)])
            for j in range(8):
                cols = slice(512 * j, 512 * (j + 1))
                pb = 2 + j % 2
                for c in range(8):
                    mm(P[pb][:, :], w[:, c, 384:512], hT_nat(c, 512 * j, 512), c == 0, c == 7,
                       [("wH", hs), "hT"], ["ps%d" % pb], track=(c == 7))
                cp("dve", iT[:, cols], P[pb][:, :], ["ps%d" % pb], [("iT", j)])
            for t0_ in range(0, 32, 8):
                for s_ in range(8):
                    t = t0_ + s_
                    tr(PT[:, s_ * 128:(s_ + 1) * 128], iT[:, t * 128:(t + 1) * 128], [("iT", t // 4)], ["PT"], track=(s_ == 7))
                cp("act" if (t0_ // 8) % 2 == 0 else "dve", vH[:, t0_ * 128:(t0_ + 8) * 128], PT[:, :], ["PT"], [("vH", t0_ // 8)])

            def gate_front(j):
                for (pi_, woff) in ((0, 0), (1, 128), (2, 256)):
                    pb = pi_ * 2 + j % 2
                    for c in range(8):
                        mm(P[pb][:, :], w[:, c, woff:woff + 128], hT_nat(c, 512 * j, 512), c == 0, c == 7,
                           [("wH", hs), "hT"], ["ps%d" % pb], track=(c == 7))
                cols = slice(512 * j, 512 * (j + 1))
                pqk = "ps%d" % (j % 2)
                pq = P[j % 2]
                DR = (0, 1)

                def K_(nm, dr):
                    return (nm, dr)
                src = [sb_[0], sr[1]]
                srck = [K_("b", 0), K_("r", 1)]
                dcol = [slice(127, 512, 128), slice(0, 512, 128)]
                for dr in DR:
                    pzb_ = (1 + dr) * 2 + j % 2
                    act(su[dr], P[pzb_][:, :], AF.Exp, ["ps%d" % pzb_], [K_("u", dr)])
                for dr in DR:
                    act(sL1[dr], su[dr], AF.Ln, [K_("u", dr)], [K_("L1", dr)], bias=1.0)
                for dr in DR:
                    act(su[dr], sL1[dr], AF.Exp, [K_("L1", dr)], [K_("u", dr)], scale=-1.0)
                for dr in DR:
                    act(sg_[dr], su[dr], AF.Ln, [K_("u", dr)], [K_("g", dr)], scale=noml[:, dr * 8 + h: dr * 8 + h + 1], bias=1.0)
                for dr in DR:
                    k.op("dve", lambda e, o=sb_[dr], g_=sg_[dr]: e.tensor_tensor_scan(out=o, data0=scanmask, data1=g_, initial=0.0,
                                                                                op0=ALU.mult, op1=ALU.add),
                         [K_("g", dr), "const"], [K_("b", dr)])
                tt("pool", sr[1], sg_[1], sb_[1], ALU.subtract, [K_("g", 1), K_("b", 1)], [K_("r", 1)])
                bbv = sb_[1].rearrange("p (c t) -> p c t", t=128)
                rrv = sr[1].rearrange("p (c t) -> p c t", t=128)
                tt("dve", rrv, rrv, bbv[:, :, 127:128].to_broadcast([128, 4, 128]), ALU.add, [K_("r", 1), K_("b", 1)], [K_("r", 1)])
                for dr in DR:
                    act(se[dr], src[dr], AF.Exp, [srck[dr]], [K_("e", dr)])
                for dr in DR:
                    act(sL1[dr], src[dr], AF.Exp, [srck[dr]], [K_("L1", dr)], scale=-1.0)
                for dr in DR:
                    tt("dve", qtT[dr][:, cols], pq[:, :], se[dr], ALU.mult, [pqk, K_("e", dr)], [("qtT", dr, j)])
                    cp("pool", dec[dr][:, 4 * j:4 * j + 4], se[dr][:, dcol[dr]], [K_("e", dr)], [("dec", dr)])
                for dr in DR:
                    stt(ktT[dr][:, cols], su[dr], oml[:, dr * 8 + h: dr * 8 + h + 1], sL1[dr], ALU.mult, ALU.mult,
                        [K_("u", dr), K_("L1", dr), "oml"], [("ktT", dr, j)])

            def gate_front_old(j):
                for (pi_, woff) in ((0, 0), (1, 128), (2, 256)):
                    pb = pi_ * 2 + j % 2
                    for c in range(8):
                        mm(P[pb][:, :], w[:, c, woff:woff + 128], hT_nat(c, 512 * j, 512), c == 0, c == 7,
                           [("wH", hs), "hT"], ["ps%d" % pb], track=(c == 7))
                cols = slice(512 * j, 512 * (j + 1))
                pqk = "ps%d" % (j % 2)
                pq = P[j % 2]
                for dr in range(2):
                    pzb_ = (1 + dr) * 2 + j % 2
                    pz = P[pzb_]
                    pzk = "ps%d" % pzb_
                    K_ = lambda nm: (nm, dr)
                    act(su[dr], pz[:, :], AF.Exp, [pzk], [K_("u")])
                    act(sL1[dr], su[dr], AF.Ln, [K_("u")], [K_("L1")], bias=1.0)
                    act(sg_[dr], su[dr], AF.Ln, [K_("u")], [K_("g")], bias=lbv2[:, dr * 8 + h: dr * 8 + h + 1])
                    tt("pool", sg_[dr], sg_[dr], sL1[dr], ALU.subtract, [K_("g"), K_("L1")], [K_("g")])
                    k.op("dve", lambda e, o=sb_[dr], g_=sg_[dr]: e.tensor_tensor_scan(out=o, data0=scanmask, data1=g_, initial=0.0,
                                                                                op0=ALU.mult, op1=ALU.add),
                         [K_("g"), "const"], [K_("b")])
                    if dr == 0:
                        src = sb_[dr]
                        srck = K_("b")
                        dcol = slice(127, 512, 128)
                    else:
                        tt("pool", sr[dr], sg_[dr], sb_[dr], ALU.subtract, [K_("g"), K_("b")], [K_("r")])
                        bbv = sb_[dr].rearrange("p (c t) -> p c t", t=128)
                        rrv = sr[dr].rearrange("p (c t) -> p c t", t=128)
                        tt("dve", rrv, rrv, bbv[:, :, 127:128].to_broadcast([128, 4, 128]), ALU.add, [K_("r"), K_("b")], [K_("r")])
                        src = sr[dr]
                        srck = K_("r")
                        dcol = slice(0, 512, 128)
                    act(se[dr], src, AF.Exp, [srck], [K_("e")])
                    tt("dve", qtT[dr][:, cols], pq[:, :], se[dr], ALU.mult, [pqk, K_("e")], [("qtT", dr, j)])
                    cp("pool", dec[dr][:, 4 * j:4 * j + 4], se[dr][:, dcol], [K_("e")], [("dec", dr)])
                    tt("pool", sL1[dr], sL1[dr], src, ALU.add, [K_("L1"), srck], [K_("L1")])
                    act(su[dr], sL1[dr], AF.Exp, [K_("L1")], [K_("u")], scale=-1.0)
                    ts("dve", ktT[dr][:, cols], su[dr], oml[:, dr * 8 + h: dr * 8 + h + 1], ALU.mult,
                       [K_("u"), "oml"], [("ktT", dr, j)])

            def gate_back(j):
                cols = slice(512 * j, 512 * (j + 1))
                for dr in range(2):
                    for cc in range(4):
                        ch = 4 * j + cc
                        tr(PT[:, (dr * 4 + cc) * 128:(dr * 4 + cc + 1) * 128], ktT[dr][:, ch * 128:(ch + 1) * 128],
                           [("ktT", dr, j)], ["PT"], track=(cc == 3 and dr == 1))
                ev = "act" if j % 2 == 0 else "dve"
                cp(ev, ktok[0][:, cols], PT[:, 0:512], ["PT"], [("ktok", 0, j)])
                cp(ev, ktok[1][:, cols], PT[:, 512:1024], ["PT"], [("ktok", 1, j)])
            gf = gate_front if DBG.get("wave", 1) else gate_front_old
            gf(0)
            for j in range(8):
                if j + 1 < 8:
                    gf(j + 1)
                gate_back(j)
            k.op("pool", lambda e: e.memset(Ubf[0][:, 0:128], 0.0), [("vH", 0)], [("U", 0, 0)])
            memset("pool", Ubf[1][:, 31 * 128:32 * 128], 0.0, [("U", 1, 31)])
            pslot = 0
            for step in range(32):
                for dr in range(2):
                    c = step if dr == 0 else 31 - step
                    pb = pslot % 4
                    pslot += 1
                    pP = P[pb][:, 0:128]
                    ppk = "ps%d" % pb
                    mm(pP, ktok[dr][:, c * 128:(c + 1) * 128], vH[:, c * 128:(c + 1) * 128], True, True,
                       [("ktok", dr, c // 4), ("vH", c // 8)], [ppk])
                    Wn = Wst[dr][step % 2]
                    Wo = Wst[dr][(step + 1) % 2]
                    if step == 0:
                        cp("dve", Wn, pP, [ppk], [("W", dr, step % 2)])
                    else:
                        cprev = c - 1 if dr == 0 else c + 1
                        dcl = dec[dr][:, cprev:cprev + 1]
                        if dr == 0:
                            k.op("act", lambda e, o=Ubf[dr][:, c * 128:(c + 1) * 128], i_=Wo, s_=dcl:
                                 e.activation(out=o, in_=i_, func=AF.Copy, scale=s_),
                                 [("W", dr, (step + 1) % 2), ("dec", dr)], [("U", dr, c)])
                        else:
                            ts("dve", Ubf[dr][:, c * 128:(c + 1) * 128], Wo, dcl, ALU.mult,
                               [("W", dr, (step + 1) % 2), ("dec", dr)], [("U", dr, c)])
                        if step < 31:
                            stt(Wn, Wo, dcl, pP, ALU.mult, ALU.add,
                                [("W", dr, (step + 1) % 2), ("dec", dr), ppk], [("W", dr, step % 2)])
            def out_front(c):
                j = c // 4
                ch = slice(c * 128, (c + 1) * 128)
                pA = P[2 + c % 2][:, 0:256]
                pAk = "ps%d" % (2 + c % 2)
                for dr in range(2):
                    mm(pA[:, dr * 128:(dr + 1) * 128], ktT[dr][:, ch], qtT[dr][:, ch], True, True,
                       [("ktT", dr, j), ("qtT", dr, j)], [pAk], track=(dr == 1))
                tt("dve", AT[c % 2].rearrange("p (a b) -> p a b", b=128), pA.rearrange("p (a b) -> p a b", b=128),
                   triH, ALU.mult, [pAk, "const"], [("AT", c % 2)])

            def out_back(c):
                j = c // 4
                cc = c % 4
                ch = slice(c * 128, (c + 1) * 128)
                pob = 4 + j % 2
                po = P[pob]
                pok = "ps%d" % pob
                oc = po[:, cc * 128:(cc + 1) * 128]
                mm(oc, vH[:, ch], AT[c % 2][:, 0:128], True, False, [("vH", c // 8), ("AT", c % 2)], [pok], track=False)
                mm(oc, vH[:, ch], AT[c % 2][:, 128:256], False, False, [("vH", c // 8), ("AT", c % 2)], [pok], track=False)
                mm(oc, Ubf[0][:, ch], qtT[0][:, ch], False, False, [("U", 0, c), ("qtT", 0, j)], [pok], track=False)
                mm(oc, Ubf[1][:, ch], qtT[1][:, ch], False, True, [("U", 1, c), ("qtT", 1, j)], [pok], track=True)
                if cc == 3:
                    cols = slice(512 * j, 512 * (j + 1))
                    act(sqh, po[:, :], AF.Square, [pok], ["sqh"])
                    mm(P[6][:, :], ones128, sqh, True, True, ["sqh", "const"], ["ps6"])
                    act(lnh, P[6][:, :], AF.Ln, ["ps6"], ["lnh"], scale=1.0 / 128, bias=EPS)
                    act(lnh, lnh, AF.Exp, ["lnh"], ["lnh"], scale=-0.5)
                    stt(on, po[:, :], hgw[:, 0:1], lnh, ALU.mult, ALU.mult, [pok, "lnh", "hgw"], ["lnh"])
                    tt("pool", obS[j % 2], on, gbT[:, cols], ALU.mult, ["lnh", ("gbT", j)], [("obS", j % 2)])
                    k.dma("sp", ob_d[h * 128:(h + 1) * 128, cols], obS[j % 2], reads=[("obS", j % 2)], sem="d_obS%d" % (j % 2))
            out_front(0)
            for c in range(32):
                if c + 1 < 32:
                    out_front(c + 1)
                out_back(c)
            k.barrier()
        k.barrier()
        A.reset(m0)

    if 4 in phases:
        m0 = A.mark()
        wZ = A.alloc([128, 8, 2048], BF16)
        wpa = A.alloc([128, 4, 1024], BF16)
        wpb = A.alloc([128, 8, 1024], BF16)
        wo = A.alloc([128, 8, 1024], BF16)
        oaj = [A.alloc([128, 4, 512], BF16) for _ in range(2)]
        obj = [A.alloc([128, 8, 512], BF16) for _ in range(2)]
        mg_ = A.alloc([128, 8, 512], BF16)
        sa = [A.alloc([128, 512], F32) for _ in range(2)]
        sb2 = [A.alloc([128, 512], F32) for _ in range(2)]
        ma = [A.alloc([128, 512], F32) for _ in range(2)]
        mb = [A.alloc([128, 512], F32) for _ in range(2)]
        xr = [A.alloc([128, D], F32) for _ in range(2)]
        for seg in range(4):
            k.dma("pool", flat(wZ[:, 2 * seg:2 * seg + 2, :]), wz_d[:, seg * 4096:(seg + 1) * 4096], writes=["wZ"], sem="d_wZ")
        k.dma("pool", flat(wpa), wpa_d, writes=["wpa"])
        for hh in range(2):
            k.dma("pool", flat(wpb[:, hh * 4:(hh + 1) * 4, :]), wpb_d[:, hh * 4096:(hh + 1) * 4096], writes=["wpb"], sem="d_wpb")
            k.dma("pool", flat(wo[:, hh * 4:(hh + 1) * 4, :]), wo_d[:, hh * 4096:(hh + 1) * 4096], writes=["wo"], sem="d_wo")
        oa_v = oa_d.rearrange("(c p) t -> p c t", p=128)
        ob_v = ob_d.rearrange("(c p) t -> p c t", p=128)

        def load_ab(j_):
            cols_ = slice(512 * j_, 512 * (j_ + 1))
            k.dma("sp", oaj[j_ % 2], oa_v[:, :, cols_], writes=[("oaj", j_ % 2)])
            k.dma("sp", obj[j_ % 2], ob_v[:, :, cols_], writes=[("obj", j_ % 2)])
        load_ab(0)
        xcnt = 0
        ocnt = 0
        for j in range(8):
            sl = j % 2
            if j + 1 < 8:
                load_ab(j + 1)
            for dc in range(8):
                dcs = slice(dc * 128, (dc + 1) * 128)
                d2 = dc % 2
                pya, pyb = P[d2], P[2 + d2]
                pyak, pybk = "ps%d" % d2, "ps%d" % (2 + d2)
                for c in range(8):
                    mm(P[4][:, :], wZ[:, c, dcs], hT_nat(c, 512 * j, 512), c == 0, c == 7, ["wZ", "hT"], ["ps4"], track=(c == 7))
                for c in range(8):
                    mm(P[5][:, :], wZ[:, c, 1024 + dc * 128:1024 + (dc + 1) * 128], hT_nat(c, 512 * j, 512), c == 0, c == 7,
                       ["wZ", "hT"], ["ps5"], track=(c == 7))
                for c in range(4):
                    mm(pya[:, :], wpa[:, c, dcs], oaj[sl][:, c, :], c == 0, c == 3, ["wpa", ("oaj", sl)], [pyak], track=(c == 3))
                for c in range(8):
                    mm(pyb[:, :], wpb[:, c, dcs], obj[sl][:, c, :], c == 0, c == 7, ["wpb", ("obj", sl)], [pybk], track=(c == 7))
                act(sa[d2], P[4][:, :], AF.Sigmoid, ["ps4"], [("sa", d2)])
                act(sb2[d2], P[5][:, :], AF.Sigmoid, ["ps5"], [("sb2", d2)])
                tt("dve", ma[d2], pya[:, :], sa[d2], ALU.mult, [pyak, ("sa", d2)], [("ma", d2)])
                tt("dve", mb[d2], pyb[:, :], sb2[d2], ALU.mult, [pybk, ("sb2", d2)], [("mb", d2)])
                tt("pool", mg_[:, dc, :], ma[d2], mb[d2], ALU.add, [("ma", d2), ("mb", d2)], [("mg", dc)])
            for tq in range(4):
                t = 4 * j + tq
                xs = xcnt % 2
                xcnt += 1
                k.dma("sp", xr[xs], x_d[t * 128:(t + 1) * 128, :], writes=[("xr", xs)])
                for half in range(2):
                    pb = 4 + ocnt % 3
                    ocnt += 1
                    for c in range(8):
                        mm(P[pb][:, :], mg_[:, c, tq * 128:(tq + 1) * 128], wo[:, c, half * 512:(half + 1) * 512],
                           c == 0, c == 7, [("mg", c), "wo"], ["ps%d" % pb], track=(c == 7))
                    tt("dve", xr[xs][:, half * 512:(half + 1) * 512], P[pb][:, :], xr[xs][:, half * 512:(half + 1) * 512],
                       ALU.add, ["ps%d" % pb, ("xr", xs)], [("xr", xs)])
                k.dma("sp", out_d[t * 128:(t + 1) * 128, :], xr[xs], reads=[("xr", xs)], sem="d_ot%d" % xs)
    k.finish("sp")
    print("instructions:", k.ninstr, "arena peak:", A.peak, {e: k.count[e] for e in ENGS})
    k.emit()
    k.close()
    es.close()
    return nc


def _bias_tiles(rel_bias):
    a = np.arange(128)[:, None]
    c = np.arange(128)[None, :]
    out = np.zeros((128, 3, 8, 2, 128), np.float32)
    for g, (_, d) in enumerate(GROUPS):
        for ab in range(2):
            rel = (a - 64 - c) if ab == 0 else (a + 64 - c)
            rel = np.clip(rel, -64, 64)
            idx = t5_bucket(rel * d)
            for h in range(8):
                out[:, g, h, ab, :] = rel_bias[idx, g * 8 + h]
    return out.reshape(128, -1)


_NC_CACHE = {}


def kernel(x, norm_w, w_in, q_norm_w, k_norm_w, rel_bias, lb_fwd, lb_bwd, hg_norm_w,
           w_proj_a, w_proj_b, w_out, _debug=False, _phases=(1, 2, 3, 4)):
    x = np.asarray(x, np.float32)
    B = x.shape[0]
    key = (_debug, tuple(_phases))
    if key not in _NC_CACHE:
        _NC_CACHE[key] = build_program(_debug, _phases)
    nc = _NC_CACHE[key]
    qkw = np.zeros((128, 6), np.float32)
    for g in range(3):
        qkw[:, 2 * g] = np.tile(np.asarray(q_norm_w, np.float32)[0, g], 2)
        qkw[:, 2 * g + 1] = np.tile(np.asarray(k_norm_w, np.float32)[0, g], 2)
    lbf = np.ascontiguousarray(np.asarray(lb_fwd, np.float32).reshape(2, 8, 128).transpose(2, 0, 1).reshape(128, 16))
    lbb = np.ascontiguousarray(np.asarray(lb_bwd, np.float32).reshape(2, 8, 128).transpose(2, 0, 1).reshape(128, 16))
    w_in0 = np.asarray(w_in, np.float32)[0]
    wv = np.ascontiguousarray(w_in0.reshape(8, 128, IN_COLS).transpose(1, 0, 2))
    ar = np.arange(128)
    w_att = np.empty((12, 128, 8, 384), np.float32)
    for hp in range(4):
        for g in range(3):
            cols = np.concatenate([g * 1536 + seg * 512 + hp * 128 + ar for seg in range(3)])
            w_att[hp * 3 + g] = wv[:, :, cols]
    w_h = np.empty((8, 128, 8, 640), np.float32)
    for h in range(8):
        cols = np.concatenate([5120 + seg * 1024 + h * 128 + ar for seg in range(5)])
        w_h[h] = wv[:, :, cols]

    def pmaj(wm):
        nchunk = wm.shape[0] // 128
        return np.ascontiguousarray(wm.reshape(nchunk, 128, wm.shape[1]).transpose(1, 0, 2).reshape(128, -1))
    common = {
        "w_att": w_att.reshape(12, 128, -1),
        "w_g": np.ascontiguousarray(wv[:, :, 4608:5120]).reshape(128, -1),
        "w_h": w_h.reshape(8, 128, -1),
        "w_z": np.ascontiguousarray(wv[:, :, 10240:12288]).reshape(128, -1),
        "norm_w": np.ascontiguousarray(np.asarray(norm_w, np.float32)[0:1]),
        "qkw": qkw,
        "biasT": _bias_tiles(np.asarray(rel_bias, np.float32)),
        "lbf": lbf, "lbb": lbb,
        "hgw": np.ascontiguousarray(np.asarray(hg_norm_w, np.float32)[0].reshape(128, 1)),
        "w_pa": pmaj(np.asarray(w_proj_a, np.float32)[0]),
        "w_pb": pmaj(np.asarray(w_proj_b, np.float32)[0]),
        "w_o": pmaj(np.asarray(w_out, np.float32)[0]),
    }
    in_maps = []
    for b in range(B):
        m = dict(common)
        m["x"] = np.ascontiguousarray(x[b])
        in_maps.append(m)
    res = run_bass_kernel_spmd(nc, in_maps, core_ids=list(range(B)))
    if _debug:
        return res.results
    return np.stack([np.asarray(r["out"], np.float32) for r in res.results], axis=0)
```

```python
import numpy as np
from contextlib import ExitStack
import concourse.bass as bass
import concourse.mybir as mybir
from concourse.bass_utils import run_bass_kernel_spmd

F32 = mybir.dt.float32
BF16 = mybir.dt.bfloat16
U8 = mybir.dt.uint8
AF = mybir.ActivationFunctionType
ALU = mybir.AluOpType

S = 4096
D = 1024
PAD = 1024
EPS = 1e-6
GROUPS = ((128, 1), (512, 4), (2048, 16))
NUM_BUCKETS = 32
REL_MAX_DISTANCE = 1024
IN_COLS = 12288
ENGS = ("pe", "act", "dve", "pool", "sp")
DBG = {}


class KB:
    def __init__(self, nc):
        self.nc = nc
        self.prog = {e: [] for e in ENGS}
        self.count = {}
        self.sems = {}
        self.seen = {e: {} for e in ENGS}
        self.snap = {}
        self.last_w = {}
        self.readers = {}
        self.pending = {e: ([], []) for e in ENGS}
        self._ctx = []
        self.ninstr = 0
        for e in ENGS:
            self._mksem(e)

    def _mksem(self, name):
        cm = self.nc.semaphore("s_" + name)
        h = cm.__enter__()
        self._ctx.append(cm)
        self.sems[name] = h
        self.count[name] = 0
        return h

    def close(self):
        for cm in reversed(self._ctx):
            cm.__exit__(None, None, None)

    def _deps(self, e, reads, writes):
        deps = {}

        def add(cv):
            if cv is None:
                return
            c, v = cv
            if deps.get(c, 0) < v:
                deps[c] = v
        for r in reads:
            add(self.last_w.get(r))
        for w in writes:
            add(self.last_w.get(w))
            for rd in self.readers.get(w, ()):
                add(rd)
        waits = []
        seen = self.seen[e]
        for c, v in deps.items():
            if c == e and e == "pe":
                continue
            if seen.get(c, 0) < v:
                waits.append((c, v))
                seen[c] = v
                sn = self.snap.get((c, v))
                if sn:
                    for c2, v2 in sn.items():
                        if c2 != e and seen.get(c2, 0) < v2:
                            seen[c2] = v2
        return waits

    def _commit(self, token, reads, writes):
        for r in reads:
            self.readers.setdefault(r, []).append(token)
        for w in writes:
            self.last_w[w] = token
            self.readers[w] = []

    def op(self, e, fn, reads=(), writes=(), track=True):
        reads = tuple(reads)
        writes = tuple(writes)
        waits = self._deps(e, reads, writes)
        self.ninstr += 1
        if track:
            self.count[e] += 1
            token = (e, self.count[e])
            pr, pw = self.pending[e]
            self._commit(token, list(reads) + pr, list(writes) + pw)
            self.pending[e] = ([], [])
            self.snap[token] = dict(self.seen[e])
            self.prog[e].append((waits, fn, (e, 1)))
        else:
            pr, pw = self.pending[e]
            pr.extend(reads)
            pw.extend(writes)
            self.prog[e].append((waits, fn, None))

    def dma(self, q, out, in_, reads=(), writes=(), sem=None):
        reads = tuple(reads)
        writes = tuple(writes)
        if sem is None:
            sem = "d_" + str(writes[0] if writes else reads[0])
        if sem not in self.sems:
            self._mksem(sem)
        waits = self._deps(q, reads, writes)
        self.count[sem] += 16
        token = (sem, self.count[sem])
        self._commit(token, reads, writes)
        self.snap[token] = dict(self.seen[q])
        self.ninstr += 1

        def fn(eng, out=out, in_=in_):
            return eng.dma_start(out=out, in_=in_)
        self.prog[q].append((waits, fn, (sem, 16)))
        return token

    def barrier(self):
        for e in ENGS:
            waits = []
            for c, v in self.count.items():
                if v > 0 and self.seen[e].get(c, 0) < v:
                    if c == e and e in ("pe", "sp"):
                        continue
                    waits.append((c, v))
                    self.seen[e][c] = v
            if waits:
                self.prog[e].append((waits, None, None))
        self.last_w = {}
        self.readers = {}

    def finish(self, e="sp"):
        waits = []
        for c, v in self.count.items():
            if v > 0 and self.seen[e].get(c, 0) < v and c != e:
                waits.append((c, v))
        self.prog[e].append((waits, None, None))

    def emit(self):
        nc = self.nc
        with nc.Block() as block:
            for e in ENGS:
                items = self.prog[e]
                if not items:
                    continue

                def body(eng, items=items):
                    for waits, fn, inc in items:
                        for c, v in waits:
                            eng.wait_ge(self.sems[c], v)
                        if fn is None:
                            continue
                        ins = fn(eng)
                        if inc is not None:
                            ins.then_inc(self.sems[inc[0]], inc[1])
                reg = {"pe": block.tensor, "act": block.scalar, "dve": block.vector,
                       "pool": block.gpsimd, "sp": block.sync}[e]
                reg(body)


class Arena:
    def __init__(self, ap, nbytes):
        self.ap = ap
        self.nbytes = nbytes
        self.off = 0
        self.peak = 0

    def alloc(self, shape, dt):
        n = int(np.prod(shape[1:]))
        nb = n * (4 if dt == F32 else 2)
        nb = (nb + 63) // 64 * 64
        assert self.off + nb <= self.nbytes, ("SBUF arena overflow", self.off, nb)
        a = self.ap[:, self.off:self.off + nb].bitcast(dt)[:, 0:n]
        self.off += nb
        self.peak = max(self.peak, self.off)
        if len(shape) == 3:
            a = a.rearrange("p (a b) -> p a b", b=shape[2])
        return a

    def mark(self):
        return self.off

    def reset(self, m):
        self.off = m


def t5_bucket(rel):
    half = NUM_BUCKETS // 2
    max_exact = half // 2
    n = np.abs(rel)
    large = max_exact + (np.log(np.maximum(n, 1) / max_exact)
                         / np.log(REL_MAX_DISTANCE / max_exact) * (half - max_exact)).astype(np.int32)
    large = np.minimum(large, half - 1)
    return np.where(rel > 0, half, 0) + np.where(n < max_exact, n, large)


def build_program(debug=False, phases=(1, 2, 3, 4)):
    nc = bass.Bass("TRN2", target_bir_lowering=False)
    x_d = nc.dram_tensor("x", [S, D], F32, kind="ExternalInput").ap()
    watt_d = nc.dram_tensor("w_att", [12, 128, 8 * 384], F32, kind="ExternalInput").ap()
    wg_d = nc.dram_tensor("w_g", [128, 8 * 512], F32, kind="ExternalInput").ap()
    wh_d = nc.dram_tensor("w_h", [8, 128, 8 * 640], F32, kind="ExternalInput").ap()
    wz_d = nc.dram_tensor("w_z", [128, 8 * 2048], F32, kind="ExternalInput").ap()
    normw_d = nc.dram_tensor("norm_w", [1, D], F32, kind="ExternalInput").ap()
    qkw_d = nc.dram_tensor("qkw", [128, 6], F32, kind="ExternalInput").ap()
    bias_d = nc.dram_tensor("biasT", [128, 3 * 8 * 2 * 128], F32, kind="ExternalInput").ap()
    lbf_d = nc.dram_tensor("lbf", [128, 16], F32, kind="ExternalInput").ap()
    lbb_d = nc.dram_tensor("lbb", [128, 16], F32, kind="ExternalInput").ap()
    hgw_d = nc.dram_tensor("hgw", [128, 1], F32, kind="ExternalInput").ap()
    wpa_d = nc.dram_tensor("w_pa", [128, 4 * D], F32, kind="ExternalInput").ap()
    wpb_d = nc.dram_tensor("w_pb", [128, 8 * D], F32, kind="ExternalInput").ap()
    wo_d = nc.dram_tensor("w_o", [128, 8 * D], F32, kind="ExternalInput").ap()
    out_d = nc.dram_tensor("out", [S, D], F32, kind="ExternalOutput").ap()
    skind = "ExternalOutput" if debug else "Internal"
    oa_d = nc.dram_tensor("oa_scr", [512, S], BF16, kind=skind).ap()
    ob_d = nc.dram_tensor("ob_scr", [D, S], BF16, kind=skind).ap()
    if debug:
        hT_dbg = nc.dram_tensor("hT_dbg", [128, 8 * S], BF16, kind="ExternalOutput").ap()

    def flat(ap3):
        return ap3.rearrange("p c n -> p (c n)")

    es = ExitStack()
    ARENA_BYTES = 209920
    arena_t = es.enter_context(nc.sbuf_tensor("arena", [128, ARENA_BYTES], U8))
    A = Arena(arena_t, ARENA_BYTES)
    P = [es.enter_context(nc.psum_tensor("ps%d" % i, [128, 512], F32)) for i in range(7)]
    PT = es.enter_context(nc.psum_tensor("pst", [128, 1024], BF16))
    k = KB(nc)

    def mm(out, lhsT, rhs, start, stop, reads, writes, track=True, tp=None):
        def fn(e):
            if tp is None:
                return e.matmul(out, lhsT=lhsT, rhs=rhs, start=start, stop=stop)
            return e.matmul(out, lhsT=lhsT, rhs=rhs, start=start, stop=stop, tile_position=tp)
        k.op("pe", fn, reads, writes, track)

    def tr(out, in_, reads, writes, track=True):
        k.op("pe", lambda e: e.transpose(out, in_, ident), tuple(reads) + ("const",), writes, track)

    def act(out, in_, func, reads, writes, scale=None, bias=None, accum=None):
        kw = {}
        if scale is not None:
            kw["scale"] = scale
        if bias is not None:
            kw["bias"] = bias
        if accum is not None:
            kw["accum_out"] = accum
        k.op("act", lambda e: e.activation(out=out, in_=in_, func=func, **kw), reads, writes)

    def tt(eng, out, in0, in1, op, reads, writes):
        k.op(eng, lambda e: e.tensor_tensor(out=out, in0=in0, in1=in1, op=op), reads, writes)

    def ts(eng, out, in0, s1, op0, reads, writes, s2=None, op1=None):
        if op1 is None:
            k.op(eng, lambda e: e.tensor_scalar(out=out, in0=in0, scalar1=s1, scalar2=None, op0=op0), reads, writes)
        else:
            k.op(eng, lambda e: e.tensor_scalar(out=out, in0=in0, scalar1=s1, scalar2=s2, op0=op0, op1=op1), reads, writes)

    def stt(out, in0, scalar, in1, op0, op1, reads, writes):
        k.op("dve", lambda e: e.scalar_tensor_tensor(out=out, in0=in0, scalar=scalar, in1=in1, op0=op0, op1=op1),
             reads, writes)

    def cp(eng, out, in_, reads, writes):
        if eng == "act":
            k.op("act", lambda e: e.copy(out=out, in_=in_), reads, writes)
        else:
            k.op(eng, lambda e: e.tensor_copy(out=out, in_=in_), reads, writes)

    def recip(out, in_, reads, writes):
        k.op("dve", lambda e: e.reciprocal(out=out, in_=in_), reads, writes)

    def memset(eng, ap, val, writes):
        k.op(eng, lambda e: e.memset(ap, val), (), writes)

    hTf = A.alloc([128, PAD + 8 * S + PAD], BF16)
    ident = A.alloc([128, 128], BF16)
    ones_bd = A.alloc([128, 128], BF16)
    ones128 = A.alloc([128, 128], BF16)
    tri4 = A.alloc([128, 3, 128], F32)
    scanmask = A.alloc([128, 512], F32)
    qkw = A.alloc([128, 6], F32)
    lbt = A.alloc([128, 32], F32)
    lbd = A.alloc([128, 16], F32)
    oml = A.alloc([128, 16], F32)
    hgw = A.alloc([128, 1], F32)
    persist_mark = A.mark()

    def hT_nat(c, t0, n):
        o = PAD + c * S + t0
        return hTf[:, o:o + n]

    def hT_perm(c, d, n0, cnt):
        L = S // d
        r0 = n0 // L
        l0 = n0 % L
        base = PAD + c * S
        if d == 1:
            return hTf[:, base + n0: base + n0 + cnt], None
        if cnt <= L - l0:
            st = base + l0 * d + r0
            return hTf[:, st: st + (cnt - 1) * d + 1: d], None
        assert l0 == 0 and cnt % L == 0
        nres = cnt // L
        st = base + r0
        v = hTf[:, st: st + L * d].rearrange("p (l r) -> p r l", r=d)[:, 0:nres, :]
        return v, L

    def hT_vtile(c, d, r, i):
        st = PAD + c * S + (128 * i - 64) * d + r
        if d == 1:
            return hTf[:, st: st + 128]
        return hTf[:, st: st + 127 * d + 1: d]

    def acc_perm(acc, d, n0, cnt):
        L = S // d
        r0 = n0 // L
        l0 = n0 % L
        if d == 1:
            return acc[:, n0:n0 + cnt], None
        if cnt <= L - l0:
            st = l0 * d + r0
            return acc[:, st: st + (cnt - 1) * d + 1: d], None
        assert l0 == 0 and cnt % L == 0
        nres = cnt // L
        v = acc.rearrange("p (l r) -> p r l", r=d)[:, r0:r0 + nres, :]
        return v, L

    memset("pool", hTf[:, 0:PAD], 0.0, ["hTpad"])
    memset("pool", hTf[:, PAD + 8 * S: PAD + 8 * S + PAD], 0.0, ["hTpad"])
    memset("pool", ident, 0.0, ["const"])
    k.op("pool", lambda e: e.affine_select(out=ident, in_=ident, pattern=[[-1, 128]], compare_op=ALU.not_equal,
                                           fill=1.0, base=0, channel_multiplier=1), ["const"], ["const"])
    memset("pool", ones_bd, 0.0, ["const"])
    memset("pool", ones_bd[0:64, 0:64], 1.0, ["const"])
    memset("pool", ones_bd[64:128, 64:128], 1.0, ["const"])
    memset("pool", ones128, 1.0, ["const"])
    memset("pool", tri4, 1.0, ["const"])
    for idx in (0, 2):
        k.op("pool", lambda e, idx=idx: e.affine_select(out=tri4[:, idx, :], in_=tri4[:, idx, :], pattern=[[-1, 128]],
                                                        compare_op=ALU.is_ge, fill=0.0, base=0, channel_multiplier=1),
             ["const"], ["const"])
    k.op("pool", lambda e: e.affine_select(out=tri4[:, 1, :], in_=tri4[:, 1, :], pattern=[[1, 128]],
                                           compare_op=ALU.is_ge, fill=0.0, base=0, channel_multiplier=-1),
         ["const"], ["const"])
    memset("pool", scanmask, 1.0, ["const"])
    memset("pool", scanmask.rearrange("p (c t) -> p c t", t=128)[:, :, 0:1], 0.0, ["const"])
    k.dma("sp", qkw, qkw_d, writes=["qkw"])
    k.dma("sp", lbt[:, 0:16], lbf_d, writes=["lbt"], sem="d_lbt")
    k.dma("sp", lbt[:, 16:32], lbb_d, writes=["lbt"], sem="d_lbt")
    k.dma("sp", hgw, hgw_d, writes=["hgw"])
    for g in range(3):
        ts("dve", qkw[:, 2 * g + 1:2 * g + 2], qkw[:, 2 * g + 1:2 * g + 2], 8.0, ALU.mult, ["qkw"], ["qkw"])
    lbv = lbt.rearrange("p (d l h) -> p d l h", d=2, l=2)
    for dd in range(2):
        tt("dve", lbd[:, dd * 8:(dd + 1) * 8], lbv[:, dd, 0, :], lbv[:, dd, 1, :], ALU.subtract, ["lbt"], ["lbd"])
    act(oml, lbd, AF.Sigmoid, ["lbd"], ["oml"], scale=-1.0)

    if 1 in phases:
        m0 = A.mark()
        normw_rep = A.alloc([128, D], F32)
        xts = [A.alloc([128, D], F32) for _ in range(2)]
        xn = [A.alloc([128, D], BF16) for _ in range(2)]
        junk = A.alloc([128, D], BF16)
        ssq = A.alloc([128, 32], F32)
        rstd = A.alloc([128, 32], F32)
        k.dma("sp", normw_rep, normw_d.partition_broadcast(128), writes=["normw"])
        memset("pool", ssq, 0.0, [("ssq", t) for t in range(32)])
        hT_ct = hTf[:, PAD:PAD + 8 * S].rearrange("p (c t) -> p c t", t=S)
        for t in range(32):
            sl = t % 2
            k.dma("sp" if sl == 0 else "act", xts[sl], x_d[t * 128:(t + 1) * 128, :], writes=[("xt", sl)])
            act(junk, xts[sl], AF.Square, [("xt", sl)], ["junk", ("ssq", t)], accum=ssq[:, t:t + 1])
            ts("dve", rstd[:, t:t + 1], ssq[:, t:t + 1], 1.0 / D, ALU.mult, [("ssq", t)], [("rstd", t)], s2=EPS, op1=ALU.add)
            act(rstd[:, t:t + 1], rstd[:, t:t + 1], AF.Sqrt, [("rstd", t)], [("rstd", t)])
            recip(rstd[:, t:t + 1], rstd[:, t:t + 1], [("rstd", t)], [("rstd", t)])
            stt(xn[sl], xts[sl], rstd[:, t:t + 1], normw_rep, ALU.mult, ALU.mult,
                [("xt", sl), ("rstd", t), "normw"], [("xn", sl)])
            for c in range(8):
                tr(PT[:, c * 128:(c + 1) * 128], xn[sl][:, c * 128:(c + 1) * 128], [("xn", sl)], ["PT"], track=(c == 7))
            cp("act" if t % 2 == 0 else "dve", hT_ct[:, :, t * 128:(t + 1) * 128],
               PT[:, :].rearrange("p (c t) -> p c t", t=128), ["PT"], [("hT", t)])
        k.barrier()
        A.reset(m0)
        if debug:
            k.dma("sp", hT_dbg, hTf[:, PAD:PAD + 8 * S], reads=["x"], sem="dbg")

    if 2 in phases:
        m0 = A.mark()
        wA = [A.alloc([128, 8, 384], BF16) for _ in range(2)]
        wG = A.alloc([128, 8, 512], BF16)
        qn = A.alloc([128, S], BF16)
        knp = A.alloc([128, 64 + S + 64], BF16)
        vTp = A.alloc([128, 64 + S + 64], BF16)
        vP = A.alloc([128, 33 * 256], BF16)
        acc = [A.alloc([128, S], F32) for _ in range(2)]
        bM = A.alloc([128, 512], F32)
        Mrep = A.alloc([128, 1024], F32)
        sq = [[A.alloc([128, 512], BF16) for _ in range(2)] for _ in range(2)]
        lnb = [A.alloc([128, 512], F32) for _ in range(2)]
        ee = [A.alloc([128, 512], F32) for _ in range(3)]
        pT = [A.alloc([128, 512], BF16) for _ in range(3)]
        sgh = [A.alloc([128, 512], F32) for _ in range(2)]
        rD = [A.alloc([128, 512], F32) for _ in range(2)]
        t1 = [A.alloc([128, 512], F32) for _ in range(2)]
        oaS = [A.alloc([128, 512], BF16) for _ in range(4)]
        zqk = [[rD[0], rD[1]], [t1[0], t1[1]]]
        vP4 = vP.rearrange("p (t h d) -> p t h d", h=2, d=128)
        memset("pool", knp[:, 0:64], 0.0, [("kn", -1)])
        memset("pool", knp[:, 64 + S:64 + S + 64], 0.0, [("kn", 8)])
        memset("pool", vTp[:, 0:64], 0.0, [("vT", -1)])
        memset("pool", vTp[:, 64 + S:64 + S + 64], 0.0, [("vT", 8)])
        memset("pool", vP4[:, :, :, 64:128], 1.0, ["vPones"])
        k.dma("pool", flat(wG), wg_d, writes=["wG"], sem="d_wG")
        maskAB = tri4[:, 0:2, :]
        units = [(hp, g) for hp in range(DBG.get("nhp", 4)) for g in range(DBG.get("ng", 3))]

        def load_unit_w(u):
            hp_, g_ = units[u]
            slot_ = u % 2
            k.dma("pool", flat(wA[slot_]), watt_d[hp_ * 3 + g_], writes=[("wA", slot_)], sem="d_wA%d" % slot_)
        load_unit_w(0)
        sccnt = 0
        oacnt = 0
        for u, (hp, g) in enumerate(units):
            win_, d = GROUPS[g]
            L = S // d
            TPS = L // 128
            slot = u % 2
            w = wA[slot]
            if u + 1 < len(units):
                load_unit_w(u + 1)
            bcol = (g * 8 + 2 * hp) * 256
            k.dma("sp", bM, bias_d[:, bcol:bcol + 512], writes=["bM"])
            act(bM, bM, AF.Exp, ["bM"], ["bM"])
            bMv = bM.rearrange("p (h a c) -> p h a c", h=2, a=2)
            Mv = Mrep.rearrange("p (h m a c) -> p h m a c", h=2, m=2, a=2)
            for hl in range(2):
                for mi in range(2):
                    tt("pool", Mv[:, hl, mi], bMv[:, hl], maskAB, ALU.mult, ["bM", "const"], ["Mrep"])

            nl = 512 // d

            def pview(ap_nat):
                if d == 1:
                    return ap_nat
                return ap_nat.rearrange("p (l r) -> p r l", r=d)

            def dview(buf, off, j):
                if d == 1:
                    return buf[:, off + 512 * j: off + 512 * (j + 1)]
                return buf[:, off:off + S].rearrange("p (r l) -> p r l", l=L)[:, :, nl * j: nl * (j + 1)]

            def wkeys(nm, j):
                return [(nm, j)] if d == 1 else [(nm, jj) for jj in range(8)]

            def proj_front(j):
                pqb, pkb, pvb = j % 2, 2 + j % 2, 4
                for (pb, woff) in ((pqb, 0), (pkb, 128), (pvb, 256)):
                    for c in range(8):
                        mm(P[pb][:, :], w[:, c, woff:woff + 128], hT_nat(c, 512 * j, 512), c == 0, c == 7,
                           [("wA", slot), "hT"], ["ps%d" % pb], track=(c == 7))
                act(sq[0][j % 2], P[pqb][:, :], AF.Square, ["ps%d" % pqb], [("sq", 0, j % 2)])
                cp("act", zqk[0][j % 2], P[pqb][:, :], ["ps%d" % pqb], [("zqk", 0, j % 2)])
                act(sq[1][j % 2], P[pkb][:, :], AF.Square, ["ps%d" % pkb], [("sq", 1, j % 2)])
                cp("act", zqk[1][j % 2], P[pkb][:, :], ["ps%d" % pkb], [("zqk", 1, j % 2)])
                cp("dve", dview(vTp, 64, j), pview(P[pvb][:, :]), ["ps%d" % pvb], wkeys("vT", j))

            def proj_back(j):
                pqb, pkb = j % 2, 2 + j % 2
                items = ((pqb, 2 * g, dview(qn, 0, j), wkeys("qn", j)),
                         (pkb, 2 * g + 1, dview(knp, 64, j), wkeys("kn", j)))
                for qi in range(2):
                    mm(P[5 + qi][:, :], ones_bd, sq[qi][j % 2], True, True, [("sq", qi, j % 2), "const"], ["ps%d" % (5 + qi)])
                for qi in range(2):
                    act(lnb[qi], P[5 + qi][:, :], AF.Ln, ["ps%d" % (5 + qi)], [("lnb", qi)], bias=64.0 * EPS)
                for qi in range(2):
                    act(lnb[qi], lnb[qi], AF.Exp, [("lnb", qi)], [("lnb", qi)], scale=-0.5)
                for qi, (pb, wcol, dst, dkeys) in enumerate(items):
                    stt(dst, pview(zqk[qi][j % 2]), qkw[:, wcol:wcol + 1], pview(lnb[qi]), ALU.mult, ALU.mult,
                        [("zqk", qi, j % 2), ("lnb", qi), "qkw"], dkeys)
            if "q" in DBG.get("st", "qvaf"):
                proj_front(0)
                for j in range(8):
                    if j + 1 < 8:
                        proj_front(j + 1)
                    proj_back(j)
                for t0_ in range(0, 33, 8):
                    nb = min(8, 33 - t0_)
                    for s_ in range(nb):
                        vt = t0_ + s_
                        lo = (128 * vt - 64) // 512 if vt > 0 else -1
                        hi = min((128 * vt + 63) // 512, 8)
                        tr(PT[:, s_ * 128:(s_ + 1) * 128], vTp[:, 128 * vt:128 * vt + 128],
                           [("vT", lo), ("vT", hi)], ["PT"], track=(s_ == nb - 1))
                    cp("act", vP4[:, t0_:t0_ + nb, :, 0:64],
                       PT[:, 0:nb * 128].rearrange("p (t h d) -> p t h d", h=2, d=64), ["PT"], [("vP", t0_ // 8)])

            its = [(mg, mp, hl) for mg in range(8) for mp in range(2) for hl in range(2)]
            state = {}

            def att_front(ii):
                nonlocal sccnt
                mg, mp, hl = its[ii]
                rows = slice(hl * 64, hl * 64 + 64)
                sb_ = sccnt % 3
                sl2 = sccnt % 3
                sccnt += 1
                state[ii] = (sb_, sl2)
                psc = P[sb_]
                psck = "ps%d" % sb_
                for mi in range(2):
                    m = mg * 4 + mp * 2 + mi
                    jq = m // 4
                    for ab in range(2):
                        kst = 128 * m + 128 * ab
                        kj = [("kn", (kst - 64) // 512 if kst >= 64 else -1), ("kn", min((kst + 127 - 64) // 512, 8))]
                        mm(psc[:, (mi * 2 + ab) * 128:(mi * 2 + ab + 1) * 128],
                           knp[rows, kst:kst + 128], qn[rows, 128 * m:128 * m + 128], True, True,
                           [("qn", jq)] + kj, [psck], track=(mi == 1 and ab == 1), tp=(hl * 64, 0))
                act(ee[sl2], psc[:, :], AF.Exp, [psck], [("ee", sl2)])
                tt("dve" if ii % 2 == 0 else "pool", pT[sl2], ee[sl2], Mrep[:, hl * 512:(hl + 1) * 512], ALU.mult,
                   [("ee", sl2), "Mrep"], [("pT", sl2)])

            def att_back(ii):
                mg, mp, hl = its[ii]
                sb_, sl2 = state[ii]
                pb = 3 + (mg % 2) * 2 + hl
                pnd = P[pb]
                pndk = "ps%d" % pb
                for mi in range(2):
                    m = mg * 4 + mp * 2 + mi
                    n0 = 128 * m
                    i = (n0 % L) // 128
                    first = (i == 0)
                    last = (i == TPS - 1)
                    vA = m
                    vB = m + 1
                    col = (mp * 2 + mi) * 128
                    KA = slice(64, 128) if first else slice(0, 128)
                    KBs = slice(0, 64) if last else slice(0, 128)
                    vkeys = [("vP", vA // 8), ("vP", vB // 8), "vPones"]
                    mm(pnd[:, col:col + 128], vP4[KA, vA, hl, :],
                       pT[sl2][KA, (mi * 2) * 128:(mi * 2 + 1) * 128], True, False,
                       [("pT", sl2)] + vkeys, [pndk], track=False, tp=(KA.start, 0))
                    mm(pnd[:, col:col + 128], vP4[KBs, vB, hl, :],
                       pT[sl2][KBs, (mi * 2 + 1) * 128:(mi * 2 + 2) * 128], False, True,
                       [("pT", sl2)] + vkeys, [pndk], track=(mi == 1), tp=(0, 0))
                if mp == 1:
                    av, inner = acc_perm(acc[hl], d, 512 * mg, 512)
                    pv_ = pnd[:, :] if inner is None else pnd[:, :].rearrange("p (a b) -> p a b", b=inner)
                    akey = ("acc", hl)
                    if g == 0:
                        cp("act", av, pv_, [pndk], [akey])
                    else:
                        tt("dve", av, av, pv_, ALU.add, [pndk, akey], [akey])
            if "a" in DBG.get("st", "qvaf"):
                LA = 2
                for ii in range(min(LA, len(its))):
                    att_front(ii)
                for ii in range(len(its)):
                    if ii + LA < len(its):
                        att_front(ii + LA)
                    att_back(ii)

            if g == len([1 for (hp_, g_) in units if hp_ == hp]) - 1 and "f" in DBG.get("st", "qvaf"):
                for j in range(8):
                    pg = P[j % 3]
                    pgk = "ps%d" % (j % 3)
                    for c in range(8):
                        mm(pg[:, :], wG[:, c, hp * 128:(hp + 1) * 128], hT_nat(c, 512 * j, 512), c == 0, c == 7,
                           ["wG", "hT"], [pgk], track=(c == 7))
                    HL = (0, 1)
                    for hl in HL:
                        act(sgh[hl][0:64, :], pg[hl * 64:(hl + 1) * 64, :], AF.Silu, [pgk], [("sgh", hl)])
                    for hl in HL:
                        act(rD[hl][0:64, :], acc[hl][64:128, 512 * j:512 * (j + 1)], AF.Ln, [("acc", hl)], [("rD", hl)])
                    for hl in HL:
                        act(rD[hl][0:64, :], rD[hl][0:64, :], AF.Exp, [("rD", hl)], [("rD", hl)], scale=-1.0)
                    for hl in HL:
                        tt("pool", t1[hl][0:64, :], acc[hl][0:64, 512 * j:512 * (j + 1)], rD[hl][0:64, :], ALU.mult,
                           [("acc", hl), ("rD", hl)], [("t1", hl)])
                    for hl in HL:
                        os_ = oacnt % 4
                        oacnt += 1
                        tt("dve", oaS[os_][0:64, :], t1[hl][0:64, :], sgh[hl][0:64, :], ALU.mult,
                           [("t1", hl), ("sgh", hl)], [("oaS", os_)])
                        k.dma("sp", oa_d[(2 * hp + hl) * 64:(2 * hp + hl + 1) * 64, 512 * j:512 * (j + 1)], oaS[os_][0:64, :],
                              reads=[("oaS", os_)], sem="d_oaS%d" % os_)
                k.barrier()
        k.barrier()
        A.reset(m0)

    if 3 in phases:
        m0 = A.mark()
        wH = [A.alloc([128, 8, 640], BF16) for _ in range(2)]
        vH = A.alloc([128, S], BF16)
        gbT = A.alloc([128, S], BF16)
        qtT = [A.alloc([128, S], BF16) for _ in range(2)]
        ktT = [A.alloc([128, S], BF16) for _ in range(2)]
        ktok = [A.alloc([128, S], BF16) for _ in range(2)]
        dec = [A.alloc([128, 32], F32) for _ in range(2)]
        Ubf = [A.alloc([128, S], BF16) for _ in range(2)]
        iT = Ubf[0]
        Wst = [[A.alloc([128, 128], F32) for _ in range(2)] for _ in range(2)]
        su = [A.alloc([128, 512], F32) for _ in range(2)]
        sL1 = [A.alloc([128, 512], F32) for _ in range(2)]
        sg_ = [A.alloc([128, 512], F32) for _ in range(2)]
        sb_ = [A.alloc([128, 512], F32) for _ in range(2)]
        sr = [None, A.alloc([128, 512], F32)]
        se = [A.alloc([128, 512], F32) for _ in range(2)]
        AT = [A.alloc([128, 256], BF16) for _ in range(3)]
        sqh = A.alloc([128, 512], BF16)
        lnh = A.alloc([128, 512], F32)
        on = lnh
        obS = [A.alloc([128, 512], BF16) for _ in range(2)]
        lbv2 = A.alloc([128, 16], F32)
        act(lbv2, lbd, AF.Sigmoid, ["lbd"], ["lbv2"])
        noml = A.alloc([128, 16], F32)
        ts("dve", noml, oml, -1.0, ALU.mult, ["oml"], ["noml"])
        triH = tri4[:, 1:3, :]
        NH = DBG.get("nh", 8)

        def load_wH(h_):
            k.dma("pool", flat(wH[h_ % 2]), wh_d[h_], writes=[("wH", h_ % 2)], sem="d_wH%d" % (h_ % 2))
        if DBG.get("pref", 1):
            load_wH(0)
        for h in range(NH):
            hs = h % 2
            w = wH[hs]
            if DBG.get("pref", 1):
                if h + 1 < NH:
                    load_wH(h + 1)
            else:
                load_wH(h)
            for j in range(8):
                cols = slice(512 * j, 512 * (j + 1))
                pb = j % 2
                for c in range(8):
                    mm(P[pb][:, :], w[:, c, 512:640], hT_nat(c, 512 * j, 512), c == 0, c == 7,
                       [("wH", hs), "hT"], ["ps%d" % pb], track=(c == 7))
                act(gbT[:, cols], P[pb][:, :], AF.Silu, ["ps%d" % pb], [("gbT", j)])
            for j in range(8):
                cols = slice(512 * j, 512 * (j + 1))
                pb = 2 + j % 2
                for c in range(8):
                    mm(P[pb][:, :], w[:, c, 384:512], hT_nat(c, 512 * j, 512), c == 0, c == 7,
                       [("wH", hs), "hT"], ["ps%d" % pb], track=(c == 7))
                cp("dve", iT[:, cols], P[pb][:, :], ["ps%d" % pb], [("iT", j)])
            for t0_ in range(0, 32, 8):
                for s_ in range(8):
                    t = t0_ + s_
                    tr(PT[:, s_ * 128:(s_ + 1) * 128], iT[:, t * 128:(t + 1) * 128], [("iT", t // 4)], ["PT"], track=(s_ == 7))
                cp("act" if (t0_ // 8) % 2 == 0 else "dve", vH[:, t0_ * 128:(t0_ + 8) * 128], PT[:, :], ["PT"], [("vH", t0_ // 8)])

            def gate_front(j):
                for (pi_, woff) in ((0, 0), (1, 128), (2, 256)):
                    pb = pi_ * 2 + j % 2
                    for c in range(8):
                        mm(P[pb][:, :], w[:, c, woff:woff + 128], hT_nat(c, 512 * j, 512), c == 0, c == 7,
                           [("wH", hs), "hT"], ["ps%d" % pb], track=(c == 7))
                cols = slice(512 * j, 512 * (j + 1))
                pqk = "ps%d" % (j % 2)
                pq = P[j % 2]
                DR = (0, 1)

                def K_(nm, dr):
                    return (nm, dr)
                src = [sb_[0], sr[1]]
                srck = [K_("b", 0), K_("r", 1)]
                dcol = [slice(127, 512, 128), slice(0, 512, 128)]
                for dr in DR:
                    pzb_ = (1 + dr) * 2 + j % 2
                    act(su[dr], P[pzb_][:, :], AF.Exp, ["ps%d" % pzb_], [K_("u", dr)])
                for dr in DR:
                    act(sL1[dr], su[dr], AF.Ln, [K_("u", dr)], [K_("L1", dr)], bias=1.0)
                for dr in DR:
                    act(su[dr], sL1[dr], AF.Exp, [K_("L1", dr)], [K_("u", dr)], scale=-1.0)
                for dr in DR:
                    act(sg_[dr], su[dr], AF.Ln, [K_("u", dr)], [K_("g", dr)], scale=noml[:, dr * 8 + h: dr * 8 + h + 1], bias=1.0)
                for dr in DR:
                    k.op("dve", lambda e, o=sb_[dr], g_=sg_[dr]: e.tensor_tensor_scan(out=o, data0=scanmask, data1=g_, initial=0.0,
                                                                                op0=ALU.mult, op1=ALU.add),
                         [K_("g", dr), "const"], [K_("b", dr)])
                tt("pool", sr[1], sg_[1], sb_[1], ALU.subtract, [K_("g", 1), K_("b", 1)], [K_("r", 1)])
                bbv = sb_[1].rearrange("p (c t) -> p c t", t=128)
                rrv = sr[1].rearrange("p (c t) -> p c t", t=128)
                tt("dve", rrv, rrv, bbv[:, :, 127:128].to_broadcast([128, 4, 128]), ALU.add, [K_("r", 1), K_("b", 1)], [K_("r", 1)])
                for dr in DR:
                    act(se[dr], src[dr], AF.Exp, [srck[dr]], [K_("e", dr)])
                for dr in DR:
                    act(sL1[dr], src[dr], AF.Exp, [srck[dr]], [K_("L1", dr)], scale=-1.0)
                for dr in DR:
                    tt("dve", qtT[dr][:, cols], pq[:, :], se[dr], ALU.mult, [pqk, K_("e", dr)], [("qtT", dr, j)])
                    cp("pool", dec[dr][:, 4 * j:4 * j + 4], se[dr][:, dcol[dr]], [K_("e", dr)], [("dec", dr)])
                for dr in DR:
                    stt(ktT[dr][:, cols], su[dr], oml[:, dr * 8 + h: dr * 8 + h + 1], sL1[dr], ALU.mult, ALU.mult,
                        [K_("u", dr), K_("L1", dr), "oml"], [("ktT", dr, j)])

            def gate_front_old(j):
                for (pi_, woff) in ((0, 0), (1, 128), (2, 256)):
                    pb = pi_ * 2 + j % 2
                    for c in range(8):
                        mm(P[pb][:, :], w[:, c, woff:woff + 128], hT_nat(c, 512 * j, 512), c == 0, c == 7,
                           [("wH", hs), "hT"], ["ps%d" % pb], track=(c == 7))
                cols = slice(512 * j, 512 * (j + 1))
                pqk = "ps%d" % (j % 2)
                pq = P[j % 2]
                for dr in range(2):
                    pzb_ = (1 + dr) * 2 + j % 2
                    pz = P[pzb_]
                    pzk = "ps%d" % pzb_
                    K_ = lambda nm: (nm, dr)
                    act(su[dr], pz[:, :], AF.Exp, [pzk], [K_("u")])
                    act(sL1[dr], su[dr], AF.Ln, [K_("u")], [K_("L1")], bias=1.0)
                    act(sg_[dr], su[dr], AF.Ln, [K_("u")], [K_("g")], bias=lbv2[:, dr * 8 + h: dr * 8 + h + 1])
                    tt("pool", sg_[dr], sg_[dr], sL1[dr], ALU.subtract, [K_("g"), K_("L1")], [K_("g")])
                    k.op("dve", lambda e, o=sb_[dr], g_=sg_[dr]: e.tensor_tensor_scan(out=o, data0=scanmask, data1=g_, initial=0.0,
                                                                                op0=ALU.mult, op1=ALU.add),
                         [K_("g"), "const"], [K_("b")])
                    if dr == 0:
                        src = sb_[dr]
                        srck = K_("b")
                        dcol = slice(127, 512, 128)
                    else:
                        tt("pool", sr[dr], sg_[dr], sb_[dr], ALU.subtract, [K_("g"), K_("b")], [K_("r")])
                        bbv = sb_[dr].rearrange("p (c t) -> p c t", t=128)
                        rrv = sr[dr].rearrange("p (c t) -> p c t", t=128)
                        tt("dve", rrv, rrv, bbv[:, :, 127:128].to_broadcast([128, 4, 128]), ALU.add, [K_("r"), K_("b")], [K_("r")])
                        src = sr[dr]
                        srck = K_("r")
                        dcol = slice(0, 512, 128)
                    act(se[dr], src, AF.Exp, [srck], [K_("e")])
                    tt("dve", qtT[dr][:, cols], pq[:, :], se[dr], ALU.mult, [pqk, K_("e")], [("qtT", dr, j)])
                    cp("pool", dec[dr][:, 4 * j:4 * j + 4], se[dr][:, dcol], [K_("e")], [("dec", dr)])
                    tt("pool", sL1[dr], sL1[dr], src, ALU.add, [K_("L1"), srck], [K_("L1")])
                    act(su[dr], sL1[dr], AF.Exp, [K_("L1")], [K_("u")], scale=-1.0)
                    ts("dve", ktT[dr][:, cols], su[dr], oml[:, dr * 8 + h: dr * 8 + h + 1], ALU.mult,
                       [K_("u"), "oml"], [("ktT", dr, j)])

            def gate_back(j):
                cols = slice(512 * j, 512 * (j + 1))
                for dr in range(2):
                    for cc in range(4):
                        ch = 4 * j + cc
                        tr(PT[:, (dr * 4 + cc) * 128:(dr * 4 + cc + 1) * 128], ktT[dr][:, ch * 128:(ch + 1) * 128],
                           [("ktT", dr, j)], ["PT"], track=(cc == 3 and dr == 1))
                ev = "act" if j % 2 == 0 else "dve"
                cp(ev, ktok[0][:, cols], PT[:, 0:512], ["PT"], [("ktok", 0, j)])
                cp(ev, ktok[1][:, cols], PT[:, 512:1024], ["PT"], [("ktok", 1, j)])
            gf = gate_front if DBG.get("wave", 1) else gate_front_old
            gf(0)
            for j in range(8):
                if j + 1 < 8:
                    gf(j + 1)
                gate_back(j)
            k.op("pool", lambda e: e.memset(Ubf[0][:, 0:128], 0.0), [("vH", 0)], [("U", 0, 0)])
            memset("pool", Ubf[1][:, 31 * 128:32 * 128], 0.0, [("U", 1, 31)])
            pslot = 0
            for step in range(32):
                for dr in range(2):
                    c = step if dr == 0 else 31 - step
                    pb = pslot % 4
                    pslot += 1
                    pP = P[pb][:, 0:128]
                    ppk = "ps%d" % pb
                    mm(pP, ktok[dr][:, c * 128:(c + 1) * 128], vH[:, c * 128:(c + 1) * 128], True, True,
                       [("ktok", dr, c // 4), ("vH", c // 8)], [ppk])
                    Wn = Wst[dr][step % 2]
                    Wo = Wst[dr][(step + 1) % 2]
                    if step == 0:
                        cp("dve", Wn, pP, [ppk], [("W", dr, step % 2)])
                    else:
                        cprev = c - 1 if dr == 0 else c + 1
                        dcl = dec[dr][:, cprev:cprev + 1]
                        if dr == 0:
                            k.op("act", lambda e, o=Ubf[dr][:, c * 128:(c + 1) * 128], i_=Wo, s_=dcl:
                                 e.activation(out=o, in_=i_, func=AF.Copy, scale=s_),
                                 [("W", dr, (step + 1) % 2), ("dec", dr)], [("U", dr, c)])
                        else:
                            ts(DBG.get("ueng", "dve"), Ubf[dr][:, c * 128:(c + 1) * 128], Wo, dcl, ALU.mult,
                               [("W", dr, (step + 1) % 2), ("dec", dr)], [("U", dr, c)])
                        if step < 31:
                            stt(Wn, Wo, dcl, pP, ALU.mult, ALU.add,
                                [("W", dr, (step + 1) % 2), ("dec", dr), ppk], [("W", dr, step % 2)])
            def out_front(c):
                j = c // 4
                ch = slice(c * 128, (c + 1) * 128)
                pA = P[1 + c % 3][:, 0:256]
                pAk = "ps%d" % (1 + c % 3)
                for dr in range(2):
                    mm(pA[:, dr * 128:(dr + 1) * 128], ktT[dr][:, ch], qtT[dr][:, ch], True, True,
                       [("ktT", dr, j), ("qtT", dr, j)], [pAk], track=(dr == 1))
                tt("dve", AT[c % 3].rearrange("p (a b) -> p a b", b=128), pA.rearrange("p (a b) -> p a b", b=128),
                   triH, ALU.mult, [pAk, "const"], [("AT", c % 3)])

            def out_back(c):
                j = c // 4
                cc = c % 4
                ch = slice(c * 128, (c + 1) * 128)
                pob = 4 + j % 2
                po = P[pob]
                pok = "ps%d" % pob
                oc = po[:, cc * 128:(cc + 1) * 128]
                mm(oc, vH[:, ch], AT[c % 3][:, 0:128], True, False, [("vH", c // 8), ("AT", c % 3)], [pok], track=False)
                mm(oc, vH[:, ch], AT[c % 3][:, 128:256], False, False, [("vH", c // 8), ("AT", c % 3)], [pok], track=False)
                mm(oc, Ubf[0][:, ch], qtT[0][:, ch], False, False, [("U", 0, c), ("qtT", 0, j)], [pok], track=False)
                mm(oc, Ubf[1][:, ch], qtT[1][:, ch], False, True, [("U", 1, c), ("qtT", 1, j)], [pok], track=True)
                if cc == 3:
                    cols = slice(512 * j, 512 * (j + 1))
                    act(sqh, po[:, :], AF.Square, [pok], ["sqh"])
                    mm(P[6][:, :], ones128, sqh, True, True, ["sqh", "const"], ["ps6"])
                    act(lnh, P[6][:, :], AF.Ln, ["ps6"], ["lnh"], scale=1.0 / 128, bias=EPS)
                    act(lnh, lnh, AF.Exp, ["lnh"], ["lnh"], scale=-0.5)
                    stt(on, po[:, :], hgw[:, 0:1], lnh, ALU.mult, ALU.mult, [pok, "lnh", "hgw"], ["lnh"])
                    tt("pool", obS[j % 2], on, gbT[:, cols], ALU.mult, ["lnh", ("gbT", j)], [("obS", j % 2)])
                    k.dma("sp", ob_d[h * 128:(h + 1) * 128, cols], obS[j % 2], reads=[("obS", j % 2)], sem="d_obS%d" % (j % 2))
            out_front(0)
            out_front(1)
            for c in range(32):
                if c + 2 < 32:
                    out_front(c + 2)
                out_back(c)
            k.barrier()
        k.barrier()
        A.reset(m0)

    if 4 in phases:
        m0 = A.mark()
        wZ = A.alloc([128, 8, 2048], BF16)
        wpa = A.alloc([128, 4, 1024], BF16)
        wpb = A.alloc([128, 8, 1024], BF16)
        wo = A.alloc([128, 8, 1024], BF16)
        oaj = [A.alloc([128, 4, 512], BF16) for _ in range(2)]
        obj = [A.alloc([128, 8, 512], BF16) for _ in range(2)]
        mg_ = A.alloc([128, 8, 512], BF16)
        sa = [A.alloc([128, 512], F32) for _ in range(2)]
        sb2 = [A.alloc([128, 512], F32) for _ in range(2)]
        ma = [A.alloc([128, 512], F32) for _ in range(2)]
        mb = [A.alloc([128, 512], F32) for _ in range(2)]
        xr = [A.alloc([128, D], F32) for _ in range(3)]
        for seg in range(4):
            k.dma("pool", flat(wZ[:, 2 * seg:2 * seg + 2, :]), wz_d[:, seg * 4096:(seg + 1) * 4096], writes=["wZ"], sem="d_wZ")
        k.dma("pool", flat(wpa), wpa_d, writes=["wpa"])
        for hh in range(2):
            k.dma("pool", flat(wpb[:, hh * 4:(hh + 1) * 4, :]), wpb_d[:, hh * 4096:(hh + 1) * 4096], writes=["wpb"], sem="d_wpb")
            k.dma("pool", flat(wo[:, hh * 4:(hh + 1) * 4, :]), wo_d[:, hh * 4096:(hh + 1) * 4096], writes=["wo"], sem="d_wo")
        oa_v = oa_d.rearrange("(c p) t -> p c t", p=128)
        ob_v = ob_d.rearrange("(c p) t -> p c t", p=128)

        def load_ab(j_):
            cols_ = slice(512 * j_, 512 * (j_ + 1))
            k.dma("sp", oaj[j_ % 2], oa_v[:, :, cols_], writes=[("oaj", j_ % 2)])
            k.dma("sp", obj[j_ % 2], ob_v[:, :, cols_], writes=[("obj", j_ % 2)])
        load_ab(0)
        xcnt = 0
        ocnt = 0
        for j in range(8):
            sl = j % 2
            if j + 1 < 8:
                load_ab(j + 1)
            for dc in range(8):
                dcs = slice(dc * 128, (dc + 1) * 128)
                d2 = dc % 2
                pya, pyb = P[d2], P[2 + d2]
                pyak, pybk = "ps%d" % d2, "ps%d" % (2 + d2)
                for c in range(8):
                    mm(P[4][:, :], wZ[:, c, dcs], hT_nat(c, 512 * j, 512), c == 0, c == 7, ["wZ", "hT"], ["ps4"], track=(c == 7))
                for c in range(8):
                    mm(P[5][:, :], wZ[:, c, 1024 + dc * 128:1024 + (dc + 1) * 128], hT_nat(c, 512 * j, 512), c == 0, c == 7,
                       ["wZ", "hT"], ["ps5"], track=(c == 7))
                for c in range(4):
                    mm(pya[:, :], wpa[:, c, dcs], oaj[sl][:, c, :], c == 0, c == 3, ["wpa", ("oaj", sl)], [pyak], track=(c == 3))
                for c in range(8):
                    mm(pyb[:, :], wpb[:, c, dcs], obj[sl][:, c, :], c == 0, c == 7, ["wpb", ("obj", sl)], [pybk], track=(c == 7))
                act(sa[d2], P[4][:, :], AF.Sigmoid, ["ps4"], [("sa", d2)])
                act(sb2[d2], P[5][:, :], AF.Sigmoid, ["ps5"], [("sb2", d2)])
                tt("dve", ma[d2], pya[:, :], sa[d2], ALU.mult, [pyak, ("sa", d2)], [("ma", d2)])
                tt("dve", mb[d2], pyb[:, :], sb2[d2], ALU.mult, [pybk, ("sb2", d2)], [("mb", d2)])
                tt("pool", mg_[:, dc, :], ma[d2], mb[d2], ALU.add, [("ma", d2), ("mb", d2)], [("mg", dc)])
            for tq in range(4):
                t = 4 * j + tq
                xs = xcnt % 3
                xcnt += 1
                k.dma("pool", xr[xs], x_d[t * 128:(t + 1) * 128, :], writes=[("xr", xs)])
                for half in range(2):
                    pb = 4 + ocnt % 3
                    ocnt += 1
                    for c in range(8):
                        mm(P[pb][:, :], mg_[:, c, tq * 128:(tq + 1) * 128], wo[:, c, half * 512:(half + 1) * 512],
                           c == 0, c == 7, [("mg", c), "wo"], ["ps%d" % pb], track=(c == 7))
                    tt("dve", xr[xs][:, half * 512:(half + 1) * 512], P[pb][:, :], xr[xs][:, half * 512:(half + 1) * 512],
                       ALU.add, ["ps%d" % pb, ("xr", xs)], [("xr", xs)])
                k.dma("sp", out_d[t * 128:(t + 1) * 128, :], xr[xs], reads=[("xr", xs)], sem="d_ot%d" % xs)
    k.finish("sp")
    print("instructions:", k.ninstr, "arena peak:", A.peak, {e: k.count[e] for e in ENGS})
    k.emit()
    k.close()
    es.close()
    return nc


def _bias_tiles(rel_bias):
    a = np.arange(128)[:, None]
    c = np.arange(128)[None, :]
    out = np.zeros((128, 3, 8, 2, 128), np.float32)
    for g, (_, d) in enumerate(GROUPS):
        for ab in range(2):
            rel = (a - 64 - c) if ab == 0 else (a + 64 - c)
            rel = np.clip(rel, -64, 64)
            idx = t5_bucket(rel * d)
            for h in range(8):
                out[:, g, h, ab, :] = rel_bias[idx, g * 8 + h]
    return out.reshape(128, -1)


_NC_CACHE = {}


def kernel(x, norm_w, w_in, q_norm_w, k_norm_w, rel_bias, lb_fwd, lb_bwd, hg_norm_w,
           w_proj_a, w_proj_b, w_out, _debug=False, _phases=(1, 2, 3, 4)):
    x = np.asarray(x, np.float32)
    B = x.shape[0]
    key = (_debug, tuple(_phases))
    if key not in _NC_CACHE:
        _NC_CACHE[key] = build_program(_debug, _phases)
    nc = _NC_CACHE[key]
    qkw = np.zeros((128, 6), np.float32)
    for g in range(3):
        qkw[:, 2 * g] = np.tile(np.asarray(q_norm_w, np.float32)[0, g], 2)
        qkw[:, 2 * g + 1] = np.tile(np.asarray(k_norm_w, np.float32)[0, g], 2)
    lbf = np.ascontiguousarray(np.asarray(lb_fwd, np.float32).reshape(2, 8, 128).transpose(2, 0, 1).reshape(128, 16))
    lbb = np.ascontiguousarray(np.asarray(lb_bwd, np.float32).reshape(2, 8, 128).transpose(2, 0, 1).reshape(128, 16))
    w_in0 = np.asarray(w_in, np.float32)[0]
    wv = np.ascontiguousarray(w_in0.reshape(8, 128, IN_COLS).transpose(1, 0, 2))
    ar = np.arange(128)
    w_att = np.empty((12, 128, 8, 384), np.float32)
    for hp in range(4):
        for g in range(3):
            cols = np.concatenate([g * 1536 + seg * 512 + hp * 128 + ar for seg in range(3)])
            w_att[hp * 3 + g] = wv[:, :, cols]
    w_h = np.empty((8, 128, 8, 640), np.float32)
    for h in range(8):
        cols = np.concatenate([5120 + seg * 1024 + h * 128 + ar for seg in range(5)])
        w_h[h] = wv[:, :, cols]

    def pmaj(wm):
        nchunk = wm.shape[0] // 128
        return np.ascontiguousarray(wm.reshape(nchunk, 128, wm.shape[1]).transpose(1, 0, 2).reshape(128, -1))
    common = {
        "w_att": w_att.reshape(12, 128, -1),
        "w_g": np.ascontiguousarray(wv[:, :, 4608:5120]).reshape(128, -1),
        "w_h": w_h.reshape(8, 128, -1),
        "w_z": np.ascontiguousarray(wv[:, :, 10240:12288]).reshape(128, -1),
        "norm_w": np.ascontiguousarray(np.asarray(norm_w, np.float32)[0:1]),
        "qkw": qkw,
        "biasT": _bias_tiles(np.asarray(rel_bias, np.float32)),
        "lbf": lbf, "lbb": lbb,
        "hgw": np.ascontiguousarray(np.asarray(hg_norm_w, np.float32)[0].reshape(128, 1)),
        "w_pa": pmaj(np.asarray(w_proj_a, np.float32)[0]),
        "w_pb": pmaj(np.asarray(w_proj_b, np.float32)[0]),
        "w_o": pmaj(np.asarray(w_out, np.float32)[0]),
    }
    in_maps = []
    for b in range(B):
        m = dict(common)
        m["x"] = np.ascontiguousarray(x[b])
        in_maps.append(m)
    res = run_bass_kernel_spmd(nc, in_maps, core_ids=list(range(B)))
    if _debug:
        return res.results
    return np.stack([np.asarray(r["out"], np.float32) for r in res.results], axis=0)
```

```python
import numpy as np
from contextlib import ExitStack
import concourse.bass as bass
import concourse.mybir as mybir
from concourse.bass_utils import run_bass_kernel_spmd

F32 = mybir.dt.float32
BF16 = mybir.dt.bfloat16
U8 = mybir.dt.uint8
AF = mybir.ActivationFunctionType
ALU = mybir.AluOpType

S = 4096
D = 1024
PAD = 1024
EPS = 1e-6
GROUPS = ((128, 1), (512, 4), (2048, 16))
NUM_BUCKETS = 32
REL_MAX_DISTANCE = 1024
IN_COLS = 12288
ENGS = ("pe", "act", "dve", "pool", "sp")
DBG = {}


class KB:
    def __init__(self, nc):
        self.nc = nc
        self.prog = {e: [] for e in ENGS}
        self.count = {}
        self.sems = {}
        self.seen = {e: {} for e in ENGS}
        self.snap = {}
        self.last_w = {}
        self.readers = {}
        self.pending = {e: ([], []) for e in ENGS}
        self._ctx = []
        self.ninstr = 0
        for e in ENGS:
            self._mksem(e)

    def _mksem(self, name):
        cm = self.nc.semaphore("s_" + name)
        h = cm.__enter__()
        self._ctx.append(cm)
        self.sems[name] = h
        self.count[name] = 0
        return h

    def close(self):
        for cm in reversed(self._ctx):
            cm.__exit__(None, None, None)

    def _deps(self, e, reads, writes):
        deps = {}

        def add(cv):
            if cv is None:
                return
            c, v = cv
            if deps.get(c, 0) < v:
                deps[c] = v
        for r in reads:
            add(self.last_w.get(r))
        for w in writes:
            add(self.last_w.get(w))
            for rd in self.readers.get(w, ()):
                add(rd)
        waits = []
        seen = self.seen[e]
        for c, v in deps.items():
            if c == e and e == "pe":
                continue
            if seen.get(c, 0) < v:
                waits.append((c, v))
                seen[c] = v
                sn = self.snap.get((c, v))
                if sn:
                    for c2, v2 in sn.items():
                        if c2 != e and seen.get(c2, 0) < v2:
                            seen[c2] = v2
        return waits

    def _commit(self, token, reads, writes):
        for r in reads:
            self.readers.setdefault(r, []).append(token)
        for w in writes:
            self.last_w[w] = token
            self.readers[w] = []

    def op(self, e, fn, reads=(), writes=(), track=True):
        reads = tuple(reads)
        writes = tuple(writes)
        waits = self._deps(e, reads, writes)
        self.ninstr += 1
        if track:
            self.count[e] += 1
            token = (e, self.count[e])
            pr, pw = self.pending[e]
            self._commit(token, list(reads) + pr, list(writes) + pw)
            self.pending[e] = ([], [])
            self.snap[token] = dict(self.seen[e])
            self.prog[e].append((waits, fn, (e, 1)))
        else:
            pr, pw = self.pending[e]
            pr.extend(reads)
            pw.extend(writes)
            self.prog[e].append((waits, fn, None))

    def dma(self, q, out, in_, reads=(), writes=(), sem=None):
        reads = tuple(reads)
        writes = tuple(writes)
        if sem is None:
            sem = "d_" + str(writes[0] if writes else reads[0])
        if sem not in self.sems:
            self._mksem(sem)
        waits = self._deps(q, reads, writes)
        self.count[sem] += 16
        token = (sem, self.count[sem])
        self._commit(token, reads, writes)
        self.snap[token] = dict(self.seen[q])
        self.ninstr += 1

        def fn(eng, out=out, in_=in_):
            return eng.dma_start(out=out, in_=in_)
        self.prog[q].append((waits, fn, (sem, 16)))
        return token

    def barrier(self):
        for e in ENGS:
            waits = []
            for c, v in self.count.items():
                if v > 0 and self.seen[e].get(c, 0) < v:
                    if c == e and e in ("pe", "sp"):
                        continue
                    waits.append((c, v))
                    self.seen[e][c] = v
            if waits:
                self.prog[e].append((waits, None, None))
        self.last_w = {}
        self.readers = {}

    def finish(self, e="sp"):
        waits = []
        for c, v in self.count.items():
            if v > 0 and self.seen[e].get(c, 0) < v and c != e:
                waits.append((c, v))
        self.prog[e].append((waits, None, None))

    def emit(self):
        nc = self.nc
        with nc.Block() as block:
            for e in ENGS:
                items = self.prog[e]
                if not items:
                    continue

                def body(eng, items=items):
                    for waits, fn, inc in items:
                        for c, v in waits:
                            eng.wait_ge(self.sems[c], v)
                        if fn is None:
                            continue
                        ins = fn(eng)
                        if inc is not None:
                            ins.then_inc(self.sems[inc[0]], inc[1])
                reg = {"pe": block.tensor, "act": block.scalar, "dve": block.vector,
                       "pool": block.gpsimd, "sp": block.sync}[e]
                reg(body)


class Arena:
    def __init__(self, ap, nbytes):
        self.ap = ap
        self.nbytes = nbytes
        self.off = 0
        self.peak = 0

    def alloc(self, shape, dt):
        n = int(np.prod(shape[1:]))
        nb = n * (4 if dt == F32 else 2)
        nb = (nb + 63) // 64 * 64
        assert self.off + nb <= self.nbytes, ("SBUF arena overflow", self.off, nb)
        a = self.ap[:, self.off:self.off + nb].bitcast(dt)[:, 0:n]
        self.off += nb
        self.peak = max(self.peak, self.off)
        if len(shape) == 3:
            a = a.rearrange("p (a b) -> p a b", b=shape[2])
        return a

    def mark(self):
        return self.off

    def reset(self, m):
        self.off = m


def t5_bucket(rel):
    half = NUM_BUCKETS // 2
    max_exact = half // 2
    n = np.abs(rel)
    large = max_exact + (np.log(np.maximum(n, 1) / max_exact)
                         / np.log(REL_MAX_DISTANCE / max_exact) * (half - max_exact)).astype(np.int32)
    large = np.minimum(large, half - 1)
    return np.where(rel > 0, half, 0) + np.where(n < max_exact, n, large)


def build_program(debug=False, phases=(1, 2, 3, 4)):
    nc = bass.Bass("TRN2", target_bir_lowering=False)
    x_d = nc.dram_tensor("x", [S, D], F32, kind="ExternalInput").ap()
    watt_d = nc.dram_tensor("w_att", [12, 128, 8 * 384], F32, kind="ExternalInput").ap()
    wg_d = nc.dram_tensor("w_g", [128, 8 * 512], F32, kind="ExternalInput").ap()
    wh_d = nc.dram_tensor("w_h", [8, 128, 8 * 640], F32, kind="ExternalInput").ap()
    wz_d = nc.dram_tensor("w_z", [128, 8 * 2048], F32, kind="ExternalInput").ap()
    normw_d = nc.dram_tensor("norm_w", [1, D], F32, kind="ExternalInput").ap()
    qkw_d = nc.dram_tensor("qkw", [128, 6], F32, kind="ExternalInput").ap()
    bias_d = nc.dram_tensor("biasT", [128, 3 * 8 * 2 * 128], F32, kind="ExternalInput").ap()
    lbf_d = nc.dram_tensor("lbf", [128, 16], F32, kind="ExternalInput").ap()
    lbb_d = nc.dram_tensor("lbb", [128, 16], F32, kind="ExternalInput").ap()
    hgw_d = nc.dram_tensor("hgw", [128, 1], F32, kind="ExternalInput").ap()
    wpa_d = nc.dram_tensor("w_pa", [128, 4 * D], F32, kind="ExternalInput").ap()
    wpb_d = nc.dram_tensor("w_pb", [128, 8 * D], F32, kind="ExternalInput").ap()
    wo_d = nc.dram_tensor("w_o", [128, 8 * D], F32, kind="ExternalInput").ap()
    out_d = nc.dram_tensor("out", [S, D], F32, kind="ExternalOutput").ap()
    skind = "ExternalOutput" if debug else "Internal"
    oa_d = nc.dram_tensor("oa_scr", [512, S], BF16, kind=skind).ap()
    ob_d = nc.dram_tensor("ob_scr", [D, S], BF16, kind=skind).ap()
    if debug:
        hT_dbg = nc.dram_tensor("hT_dbg", [128, 8 * S], BF16, kind="ExternalOutput").ap()

    def flat(ap3):
        return ap3.rearrange("p c n -> p (c n)")

    es = ExitStack()
    ARENA_BYTES = 209920
    arena_t = es.enter_context(nc.sbuf_tensor("arena", [128, ARENA_BYTES], U8))
    A = Arena(arena_t, ARENA_BYTES)
    P = [es.enter_context(nc.psum_tensor("ps%d" % i, [128, 512], F32)) for i in range(7)]
    PT = es.enter_context(nc.psum_tensor("pst", [128, 1024], BF16))
    k = KB(nc)

    def mm(out, lhsT, rhs, start, stop, reads, writes, track=True, tp=None):
        def fn(e):
            if tp is None:
                return e.matmul(out, lhsT=lhsT, rhs=rhs, start=start, stop=stop)
            return e.matmul(out, lhsT=lhsT, rhs=rhs, start=start, stop=stop, tile_position=tp)
        k.op("pe", fn, reads, writes, track)

    def tr(out, in_, reads, writes, track=True):
        k.op("pe", lambda e: e.transpose(out, in_, ident), tuple(reads) + ("const",), writes, track)

    def act(out, in_, func, reads, writes, scale=None, bias=None, accum=None):
        kw = {}
        if scale is not None:
            kw["scale"] = scale
        if bias is not None:
            kw["bias"] = bias
        if accum is not None:
            kw["accum_out"] = accum
        k.op("act", lambda e: e.activation(out=out, in_=in_, func=func, **kw), reads, writes)

    def tt(eng, out, in0, in1, op, reads, writes):
        k.op(eng, lambda e: e.tensor_tensor(out=out, in0=in0, in1=in1, op=op), reads, writes)

    def ts(eng, out, in0, s1, op0, reads, writes, s2=None, op1=None):
        if op1 is None:
            k.op(eng, lambda e: e.tensor_scalar(out=out, in0=in0, scalar1=s1, scalar2=None, op0=op0), reads, writes)
        else:
            k.op(eng, lambda e: e.tensor_scalar(out=out, in0=in0, scalar1=s1, scalar2=s2, op0=op0, op1=op1), reads, writes)

    def stt(out, in0, scalar, in1, op0, op1, reads, writes):
        k.op("dve", lambda e: e.scalar_tensor_tensor(out=out, in0=in0, scalar=scalar, in1=in1, op0=op0, op1=op1),
             reads, writes)

    def cp(eng, out, in_, reads, writes):
        if eng == "act":
            k.op("act", lambda e: e.copy(out=out, in_=in_), reads, writes)
        else:
            k.op(eng, lambda e: e.tensor_copy(out=out, in_=in_), reads, writes)

    def recip(out, in_, reads, writes):
        k.op("dve", lambda e: e.reciprocal(out=out, in_=in_), reads, writes)

    def memset(eng, ap, val, writes):
        k.op(eng, lambda e: e.memset(ap, val), (), writes)

    hTf = A.alloc([128, PAD + 8 * S + PAD], BF16)
    ident = A.alloc([128, 128], BF16)
    ones_bd = A.alloc([128, 128], BF16)
    ones128 = A.alloc([128, 128], BF16)
    tri4 = A.alloc([128, 3, 128], F32)
    scanmask = A.alloc([128, 512], F32)
    qkw = A.alloc([128, 6], F32)
    lbt = A.alloc([128, 32], F32)
    lbd = A.alloc([128, 16], F32)
    oml = A.alloc([128, 16], F32)
    hgw = A.alloc([128, 1], F32)
    persist_mark = A.mark()

    def hT_nat(c, t0, n):
        o = PAD + c * S + t0
        return hTf[:, o:o + n]

    def hT_perm(c, d, n0, cnt):
        L = S // d
        r0 = n0 // L
        l0 = n0 % L
        base = PAD + c * S
        if d == 1:
            return hTf[:, base + n0: base + n0 + cnt], None
        if cnt <= L - l0:
            st = base + l0 * d + r0
            return hTf[:, st: st + (cnt - 1) * d + 1: d], None
        assert l0 == 0 and cnt % L == 0
        nres = cnt // L
        st = base + r0
        v = hTf[:, st: st + L * d].rearrange("p (l r) -> p r l", r=d)[:, 0:nres, :]
        return v, L

    def hT_vtile(c, d, r, i):
        st = PAD + c * S + (128 * i - 64) * d + r
        if d == 1:
            return hTf[:, st: st + 128]
        return hTf[:, st: st + 127 * d + 1: d]

    def acc_perm(acc, d, n0, cnt):
        L = S // d
        r0 = n0 // L
        l0 = n0 % L
        if d == 1:
            return acc[:, n0:n0 + cnt], None
        if cnt <= L - l0:
            st = l0 * d + r0
            return acc[:, st: st + (cnt - 1) * d + 1: d], None
        assert l0 == 0 and cnt % L == 0
        nres = cnt // L
        v = acc.rearrange("p (l r) -> p r l", r=d)[:, r0:r0 + nres, :]
        return v, L

    memset("pool", hTf[:, 0:PAD], 0.0, ["hTpad"])
    memset("pool", hTf[:, PAD + 8 * S: PAD + 8 * S + PAD], 0.0, ["hTpad"])
    memset("pool", ident, 0.0, ["const"])
    k.op("pool", lambda e: e.affine_select(out=ident, in_=ident, pattern=[[-1, 128]], compare_op=ALU.not_equal,
                                           fill=1.0, base=0, channel_multiplier=1), ["const"], ["const"])
    memset("pool", ones_bd, 0.0, ["const"])
    memset("pool", ones_bd[0:64, 0:64], 1.0, ["const"])
    memset("pool", ones_bd[64:128, 64:128], 1.0, ["const"])
    memset("pool", ones128, 1.0, ["const"])
    memset("pool", tri4, 1.0, ["const"])
    for idx in (0, 2):
        k.op("pool", lambda e, idx=idx: e.affine_select(out=tri4[:, idx, :], in_=tri4[:, idx, :], pattern=[[-1, 128]],
                                                        compare_op=ALU.is_ge, fill=0.0, base=0, channel_multiplier=1),
             ["const"], ["const"])
    k.op("pool", lambda e: e.affine_select(out=tri4[:, 1, :], in_=tri4[:, 1, :], pattern=[[1, 128]],
                                           compare_op=ALU.is_ge, fill=0.0, base=0, channel_multiplier=-1),
         ["const"], ["const"])
    memset("pool", scanmask, 1.0, ["const"])
    memset("pool", scanmask.rearrange("p (c t) -> p c t", t=128)[:, :, 0:1], 0.0, ["const"])
    k.dma("sp", qkw, qkw_d, writes=["qkw"])
    k.dma("sp", lbt[:, 0:16], lbf_d, writes=["lbt"], sem="d_lbt")
    k.dma("sp", lbt[:, 16:32], lbb_d, writes=["lbt"], sem="d_lbt")
    k.dma("sp", hgw, hgw_d, writes=["hgw"])
    for g in range(3):
        ts("dve", qkw[:, 2 * g + 1:2 * g + 2], qkw[:, 2 * g + 1:2 * g + 2], 8.0, ALU.mult, ["qkw"], ["qkw"])
    lbv = lbt.rearrange("p (d l h) -> p d l h", d=2, l=2)
    for dd in range(2):
        tt("dve", lbd[:, dd * 8:(dd + 1) * 8], lbv[:, dd, 0, :], lbv[:, dd, 1, :], ALU.subtract, ["lbt"], ["lbd"])
    act(oml, lbd, AF.Sigmoid, ["lbd"], ["oml"], scale=-1.0)

    if 1 in phases:
        m0 = A.mark()
        normw_rep = A.alloc([128, D], F32)
        NXB = 4
        xts = [A.alloc([128, D], F32) for _ in range(NXB)]
        xn = [A.alloc([128, D], BF16) for _ in range(2)]
        junk = A.alloc([128, D], BF16)
        ssq = A.alloc([128, 32], F32)
        rstd = A.alloc([128, 32], F32)
        k.dma("sp", normw_rep, normw_d.partition_broadcast(128), writes=["normw"])
        memset("pool", ssq, 0.0, [("ssq", t) for t in range(32)])
        hT_ct = hTf[:, PAD:PAD + 8 * S].rearrange("p (c t) -> p c t", t=S)

        def p1_load(t):
            k.dma("sp", xts[t % NXB], x_d[t * 128:(t + 1) * 128, :], writes=[("xt", t % NXB)])

        def p1_stats(t):
            sl = t % NXB
            act(junk, xts[sl], AF.Square, [("xt", sl)], ["junk", ("ssq", t)], accum=ssq[:, t:t + 1])
            act(rstd[:, t:t + 1], ssq[:, t:t + 1], AF.Ln, [("ssq", t)], [("rstd", t)], scale=1.0 / D, bias=EPS)
            act(rstd[:, t:t + 1], rstd[:, t:t + 1], AF.Exp, [("rstd", t)], [("rstd", t)], scale=-0.5)
            stt(xn[t % 2], xts[sl], rstd[:, t:t + 1], normw_rep, ALU.mult, ALU.mult,
                [("xt", sl), ("rstd", t), "normw"], [("xn", t % 2)])

        def p1_back(t):
            for c in range(8):
                tr(PT[:, c * 128:(c + 1) * 128], xn[t % 2][:, c * 128:(c + 1) * 128], [("xn", t % 2)], ["PT"], track=(c == 7))
            cp("act" if t % 2 == 0 else "dve", hT_ct[:, :, t * 128:(t + 1) * 128],
               PT[:, :].rearrange("p (c t) -> p c t", t=128), ["PT"], [("hT", t)])
        for t in range(NXB - 1):
            p1_load(t)
        p1_stats(0)
        for t in range(32):
            if t + NXB - 1 < 32:
                p1_load(t + NXB - 1)
            if t + 1 < 32:
                p1_stats(t + 1)
            p1_back(t)
        k.barrier()
        A.reset(m0)
        if debug:
            k.dma("sp", hT_dbg, hTf[:, PAD:PAD + 8 * S], reads=["x"], sem="dbg")

    if 2 in phases:
        m0 = A.mark()
        wA = [A.alloc([128, 8, 384], BF16) for _ in range(2)]
        wG = A.alloc([128, 8, 512], BF16)
        qn = A.alloc([128, S], BF16)
        knp = A.alloc([128, 64 + S + 64], BF16)
        vTp = A.alloc([128, 64 + S + 64], BF16)
        vP = A.alloc([128, 33 * 256], BF16)
        acc = [A.alloc([128, S], F32) for _ in range(2)]
        bM = A.alloc([128, 512], F32)
        Mrep = A.alloc([128, 1024], F32)
        sq = [[A.alloc([128, 512], BF16) for _ in range(2)] for _ in range(2)]
        lnb = [A.alloc([128, 512], F32) for _ in range(2)]
        ee = [A.alloc([128, 512], F32) for _ in range(3)]
        pT = [A.alloc([128, 512], BF16) for _ in range(3)]
        sgh = [A.alloc([128, 512], F32) for _ in range(2)]
        rD = [A.alloc([128, 512], F32) for _ in range(2)]
        t1 = [A.alloc([128, 512], F32) for _ in range(2)]
        oaS = [A.alloc([128, 512], BF16) for _ in range(4)]
        zqk = [[rD[0], rD[1]], [t1[0], t1[1]]]
        vP4 = vP.rearrange("p (t h d) -> p t h d", h=2, d=128)
        memset("pool", knp[:, 0:64], 0.0, [("kn", -1)])
        memset("pool", knp[:, 64 + S:64 + S + 64], 0.0, [("kn", 8)])
        memset("pool", vTp[:, 0:64], 0.0, [("vT", -1)])
        memset("pool", vTp[:, 64 + S:64 + S + 64], 0.0, [("vT", 8)])
        memset("pool", vP4[:, :, :, 64:128], 1.0, ["vPones"])
        k.dma("pool", flat(wG), wg_d, writes=["wG"], sem="d_wG")
        maskAB = tri4[:, 0:2, :]
        units = [(hp, g) for hp in range(DBG.get("nhp", 4)) for g in range(DBG.get("ng", 3))]

        def load_unit_w(u):
            hp_, g_ = units[u]
            slot_ = u % 2
            k.dma("pool", flat(wA[slot_]), watt_d[hp_ * 3 + g_], writes=[("wA", slot_)], sem="d_wA%d" % slot_)
        load_unit_w(0)
        sccnt = 0
        oacnt = 0
        for u, (hp, g) in enumerate(units):
            win_, d = GROUPS[g]
            L = S // d
            TPS = L // 128
            slot = u % 2
            w = wA[slot]
            if u + 1 < len(units):
                load_unit_w(u + 1)
            bcol = (g * 8 + 2 * hp) * 256
            k.dma("sp", bM, bias_d[:, bcol:bcol + 512], writes=["bM"])
            act(bM, bM, AF.Exp, ["bM"], ["bM"])
            bMv = bM.rearrange("p (h a c) -> p h a c", h=2, a=2)
            Mv = Mrep.rearrange("p (h m a c) -> p h m a c", h=2, m=2, a=2)
            for hl in range(2):
                for mi in range(2):
                    tt("pool", Mv[:, hl, mi], bMv[:, hl], maskAB, ALU.mult, ["bM", "const"], ["Mrep"])

            nl = 512 // d

            def pview(ap_nat):
                if d == 1:
                    return ap_nat
                return ap_nat.rearrange("p (l r) -> p r l", r=d)

            def dview(buf, off, j):
                if d == 1:
                    return buf[:, off + 512 * j: off + 512 * (j + 1)]
                return buf[:, off:off + S].rearrange("p (r l) -> p r l", l=L)[:, :, nl * j: nl * (j + 1)]

            def wkeys(nm, j):
                return [(nm, j)] if d == 1 else [(nm, jj) for jj in range(8)]

            def proj_front(j):
                pqb, pkb, pvb = j % 2, 2 + j % 2, 4
                for (pb, woff) in ((pqb, 0), (pkb, 128), (pvb, 256)):
                    for c in range(8):
                        mm(P[pb][:, :], w[:, c, woff:woff + 128], hT_nat(c, 512 * j, 512), c == 0, c == 7,
                           [("wA", slot), "hT"], ["ps%d" % pb], track=(c == 7))
                act(sq[0][j % 2], P[pqb][:, :], AF.Square, ["ps%d" % pqb], [("sq", 0, j % 2)])
                cp("act", zqk[0][j % 2], P[pqb][:, :], ["ps%d" % pqb], [("zqk", 0, j % 2)])
                act(sq[1][j % 2], P[pkb][:, :], AF.Square, ["ps%d" % pkb], [("sq", 1, j % 2)])
                cp("act", zqk[1][j % 2], P[pkb][:, :], ["ps%d" % pkb], [("zqk", 1, j % 2)])
                cp("dve", dview(vTp, 64, j), pview(P[pvb][:, :]), ["ps%d" % pvb], wkeys("vT", j))

            def proj_back(j):
                pqb, pkb = j % 2, 2 + j % 2
                items = ((pqb, 2 * g, dview(qn, 0, j), wkeys("qn", j)),
                         (pkb, 2 * g + 1, dview(knp, 64, j), wkeys("kn", j)))
                for qi in range(2):
                    mm(P[5 + qi][:, :], ones_bd, sq[qi][j % 2], True, True, [("sq", qi, j % 2), "const"], ["ps%d" % (5 + qi)])
                for qi in range(2):
                    act(lnb[qi], P[5 + qi][:, :], AF.Ln, ["ps%d" % (5 + qi)], [("lnb", qi)], bias=64.0 * EPS)
                for qi in range(2):
                    act(lnb[qi], lnb[qi], AF.Exp, [("lnb", qi)], [("lnb", qi)], scale=-0.5)
                for qi, (pb, wcol, dst, dkeys) in enumerate(items):
                    stt(dst, pview(zqk[qi][j % 2]), qkw[:, wcol:wcol + 1], pview(lnb[qi]), ALU.mult, ALU.mult,
                        [("zqk", qi, j % 2), ("lnb", qi), "qkw"], dkeys)
            if "q" in DBG.get("st", "qvaf"):
                proj_front(0)
                for j in range(8):
                    if j + 1 < 8:
                        proj_front(j + 1)
                    proj_back(j)
                for t0_ in range(0, 33, 8):
                    nb = min(8, 33 - t0_)
                    for s_ in range(nb):
                        vt = t0_ + s_
                        lo = (128 * vt - 64) // 512 if vt > 0 else -1
                        hi = min((128 * vt + 63) // 512, 8)
                        tr(PT[:, s_ * 128:(s_ + 1) * 128], vTp[:, 128 * vt:128 * vt + 128],
                           [("vT", lo), ("vT", hi)], ["PT"], track=(s_ == nb - 1))
                    cp("act", vP4[:, t0_:t0_ + nb, :, 0:64],
                       PT[:, 0:nb * 128].rearrange("p (t h d) -> p t h d", h=2, d=64), ["PT"], [("vP", t0_ // 8)])

            its = [(mg, mp, hl) for mg in range(8) for mp in range(2) for hl in range(2)]
            state = {}

            def att_front(ii):
                nonlocal sccnt
                mg, mp, hl = its[ii]
                rows = slice(hl * 64, hl * 64 + 64)
                sb_ = sccnt % 3
                sl2 = sccnt % 3
                sccnt += 1
                state[ii] = (sb_, sl2)
                psc = P[sb_]
                psck = "ps%d" % sb_
                for mi in range(2):
                    m = mg * 4 + mp * 2 + mi
                    jq = m // 4
                    for ab in range(2):
                        kst = 128 * m + 128 * ab
                        kj = [("kn", (kst - 64) // 512 if kst >= 64 else -1), ("kn", min((kst + 127 - 64) // 512, 8))]
                        mm(psc[:, (mi * 2 + ab) * 128:(mi * 2 + ab + 1) * 128],
                           knp[rows, kst:kst + 128], qn[rows, 128 * m:128 * m + 128], True, True,
                           [("qn", jq)] + kj, [psck], track=(mi == 1 and ab == 1), tp=(hl * 64, 0))
                act(ee[sl2], psc[:, :], AF.Exp, [psck], [("ee", sl2)])
                tt("dve" if ii % 2 == 0 else "pool", pT[sl2], ee[sl2], Mrep[:, hl * 512:(hl + 1) * 512], ALU.mult,
                   [("ee", sl2), "Mrep"], [("pT", sl2)])

            def att_back(ii):
                mg, mp, hl = its[ii]
                sb_, sl2 = state[ii]
                pb = 3 + (mg % 2) * 2 + hl
                pnd = P[pb]
                pndk = "ps%d" % pb
                for mi in range(2):
                    m = mg * 4 + mp * 2 + mi
                    n0 = 128 * m
                    i = (n0 % L) // 128
                    first = (i == 0)
                    last = (i == TPS - 1)
                    vA = m
                    vB = m + 1
                    col = (mp * 2 + mi) * 128
                    KA = slice(64, 128) if first else slice(0, 128)
                    KBs = slice(0, 64) if last else slice(0, 128)
                    vkeys = [("vP", vA // 8), ("vP", vB // 8), "vPones"]
                    mm(pnd[:, col:col + 128], vP4[KA, vA, hl, :],
                       pT[sl2][KA, (mi * 2) * 128:(mi * 2 + 1) * 128], True, False,
                       [("pT", sl2)] + vkeys, [pndk], track=False, tp=(KA.start, 0))
                    mm(pnd[:, col:col + 128], vP4[KBs, vB, hl, :],
                       pT[sl2][KBs, (mi * 2 + 1) * 128:(mi * 2 + 2) * 128], False, True,
                       [("pT", sl2)] + vkeys, [pndk], track=(mi == 1), tp=(0, 0))
                if mp == 1:
                    av, inner = acc_perm(acc[hl], d, 512 * mg, 512)
                    pv_ = pnd[:, :] if inner is None else pnd[:, :].rearrange("p (a b) -> p a b", b=inner)
                    akey = ("acc", hl)
                    if g == 0:
                        cp("act", av, pv_, [pndk], [akey])
                    else:
                        tt("dve", av, av, pv_, ALU.add, [pndk, akey], [akey])
            if "a" in DBG.get("st", "qvaf"):
                LA = 2
                for ii in range(min(LA, len(its))):
                    att_front(ii)
                for ii in range(len(its)):
                    if ii + LA < len(its):
                        att_front(ii + LA)
                    att_back(ii)

            if g == len([1 for (hp_, g_) in units if hp_ == hp]) - 1 and "f" in DBG.get("st", "qvaf"):
                for j in range(8):
                    pg = P[j % 3]
                    pgk = "ps%d" % (j % 3)
                    for c in range(8):
                        mm(pg[:, :], wG[:, c, hp * 128:(hp + 1) * 128], hT_nat(c, 512 * j, 512), c == 0, c == 7,
                           ["wG", "hT"], [pgk], track=(c == 7))
                    HL = (0, 1)
                    for hl in HL:
                        act(sgh[hl][0:64, :], pg[hl * 64:(hl + 1) * 64, :], AF.Silu, [pgk], [("sgh", hl)])
                    for hl in HL:
                        act(rD[hl][0:64, :], acc[hl][64:128, 512 * j:512 * (j + 1)], AF.Ln, [("acc", hl)], [("rD", hl)])
                    for hl in HL:
                        act(rD[hl][0:64, :], rD[hl][0:64, :], AF.Exp, [("rD", hl)], [("rD", hl)], scale=-1.0)
                    for hl in HL:
                        tt("pool", t1[hl][0:64, :], acc[hl][0:64, 512 * j:512 * (j + 1)], rD[hl][0:64, :], ALU.mult,
                           [("acc", hl), ("rD", hl)], [("t1", hl)])
                    for hl in HL:
                        os_ = oacnt % 4
                        oacnt += 1
                        tt("dve", oaS[os_][0:64, :], t1[hl][0:64, :], sgh[hl][0:64, :], ALU.mult,
                           [("t1", hl), ("sgh", hl)], [("oaS", os_)])
                        k.dma("sp", oa_d[(2 * hp + hl) * 64:(2 * hp + hl + 1) * 64, 512 * j:512 * (j + 1)], oaS[os_][0:64, :],
                              reads=[("oaS", os_)], sem="d_oaS%d" % os_)
                k.barrier()
        k.barrier()
        A.reset(m0)

    if 3 in phases:
        m0 = A.mark()
        wH = [A.alloc([128, 8, 640], BF16) for _ in range(2)]
        vH = A.alloc([128, S], BF16)
        gbT = A.alloc([128, S], BF16)
        qtT = [A.alloc([128, S], BF16) for _ in range(2)]
        ktT = [A.alloc([128, S], BF16) for _ in range(2)]
        ktok = [A.alloc([128, S], BF16) for _ in range(2)]
        dec = [A.alloc([128, 32], F32) for _ in range(2)]
        Ubf = [A.alloc([128, S], BF16) for _ in range(2)]
        iT = Ubf[0]
        Wst = [[A.alloc([128, 128], F32) for _ in range(2)] for _ in range(2)]
        su = [A.alloc([128, 512], F32) for _ in range(2)]
        sL1 = [A.alloc([128, 512], F32) for _ in range(2)]
        sg_ = [A.alloc([128, 512], F32) for _ in range(2)]
        sb_ = [A.alloc([128, 512], F32) for _ in range(2)]
        sr = [None, A.alloc([128, 512], F32)]
        se = [A.alloc([128, 512], F32) for _ in range(2)]
        AT = [A.alloc([128, 256], BF16) for _ in range(3)]
        sqh = A.alloc([128, 512], BF16)
        lnh = A.alloc([128, 512], F32)
        on = lnh
        obS = [A.alloc([128, 512], BF16) for _ in range(2)]
        lbv2 = A.alloc([128, 16], F32)
        act(lbv2, lbd, AF.Sigmoid, ["lbd"], ["lbv2"])
        noml = A.alloc([128, 16], F32)
        ts("dve", noml, oml, -1.0, ALU.mult, ["oml"], ["noml"])
        triH = tri4[:, 1:3, :]
        NH = DBG.get("nh", 8)

        def load_wH(h_):
            k.dma("pool", flat(wH[h_ % 2]), wh_d[h_], writes=[("wH", h_ % 2)], sem="d_wH%d" % (h_ % 2))
        if DBG.get("pref", 1):
            load_wH(0)
        for h in range(NH):
            hs = h % 2
            w = wH[hs]
            if DBG.get("pref", 1):
                if h + 1 < NH:
                    load_wH(h + 1)
            else:
                load_wH(h)
            for j in range(8):
                cols = slice(512 * j, 512 * (j + 1))
                pb = j % 2
                for c in range(8):
                    mm(P[pb][:, :], w[:, c, 512:640], hT_nat(c, 512 * j, 512), c == 0, c == 7,
                       [("wH", hs), "hT"], ["ps%d" % pb], track=(c == 7))
                act(gbT[:, cols], P[pb][:, :], AF.Silu, ["ps%d" % pb], [("gbT", j)])
            for j in range(8):
                cols = slice(512 * j, 512 * (j + 1))
                pb = 2 + j % 2
                for c in range(8):
                    mm(P[pb][:, :], w[:, c, 384:512], hT_nat(c, 512 * j, 512), c == 0, c == 7,
                       [("wH", hs), "hT"], ["ps%d" % pb], track=(c == 7))
                cp("dve", iT[:, cols], P[pb][:, :], ["ps%d" % pb], [("iT", j)])
            for t0_ in range(0, 32, 8):
                for s_ in range(8):
                    t = t0_ + s_
                    tr(PT[:, s_ * 128:(s_ + 1) * 128], iT[:, t * 128:(t + 1) * 128], [("iT", t // 4)], ["PT"], track=(s_ == 7))
                cp("act" if (t0_ // 8) % 2 == 0 else "dve", vH[:, t0_ * 128:(t0_ + 8) * 128], PT[:, :], ["PT"], [("vH", t0_ // 8)])

            def gate_front(j):
                for (pi_, woff) in ((0, 0), (1, 128), (2, 256)):
                    pb = pi_ * 2 + j % 2
                    for c in range(8):
                        mm(P[pb][:, :], w[:, c, woff:woff + 128], hT_nat(c, 512 * j, 512), c == 0, c == 7,
                           [("wH", hs), "hT"], ["ps%d" % pb], track=(c == 7))
                cols = slice(512 * j, 512 * (j + 1))
                pqk = "ps%d" % (j % 2)
                pq = P[j % 2]
                DR = (0, 1)

                def K_(nm, dr):
                    return (nm, dr)
                src = [sb_[0], sr[1]]
                srck = [K_("b", 0), K_("r", 1)]
                dcol = [slice(127, 512, 128), slice(0, 512, 128)]
                for dr in DR:
                    pzb_ = (1 + dr) * 2 + j % 2
                    act(su[dr], P[pzb_][:, :], AF.Exp, ["ps%d" % pzb_], [K_("u", dr)])
                for dr in DR:
                    act(sL1[dr], su[dr], AF.Ln, [K_("u", dr)], [K_("L1", dr)], bias=1.0)
                for dr in DR:
                    act(su[dr], sL1[dr], AF.Exp, [K_("L1", dr)], [K_("u", dr)], scale=-1.0)
                for dr in DR:
                    act(sg_[dr], su[dr], AF.Ln, [K_("u", dr)], [K_("g", dr)], scale=noml[:, dr * 8 + h: dr * 8 + h + 1], bias=1.0)
                for dr in DR:
                    k.op("dve", lambda e, o=sb_[dr], g_=sg_[dr]: e.tensor_tensor_scan(out=o, data0=scanmask, data1=g_, initial=0.0,
                                                                                op0=ALU.mult, op1=ALU.add),
                         [K_("g", dr), "const"], [K_("b", dr)])
                tt("pool", sr[1], sg_[1], sb_[1], ALU.subtract, [K_("g", 1), K_("b", 1)], [K_("r", 1)])
                bbv = sb_[1].rearrange("p (c t) -> p c t", t=128)
                rrv = sr[1].rearrange("p (c t) -> p c t", t=128)
                tt("dve", rrv, rrv, bbv[:, :, 127:128].to_broadcast([128, 4, 128]), ALU.add, [K_("r", 1), K_("b", 1)], [K_("r", 1)])
                for dr in DR:
                    act(se[dr], src[dr], AF.Exp, [srck[dr]], [K_("e", dr)])
                for dr in DR:
                    act(sL1[dr], src[dr], AF.Exp, [srck[dr]], [K_("L1", dr)], scale=-1.0)
                for dr in DR:
                    tt("dve", qtT[dr][:, cols], pq[:, :], se[dr], ALU.mult, [pqk, K_("e", dr)], [("qtT", dr, j)])
                    cp("pool", dec[dr][:, 4 * j:4 * j + 4], se[dr][:, dcol[dr]], [K_("e", dr)], [("dec", dr)])
                for dr in DR:
                    stt(ktT[dr][:, cols], su[dr], oml[:, dr * 8 + h: dr * 8 + h + 1], sL1[dr], ALU.mult, ALU.mult,
                        [K_("u", dr), K_("L1", dr), "oml"], [("ktT", dr, j)])

            def gate_front_old(j):
                for (pi_, woff) in ((0, 0), (1, 128), (2, 256)):
                    pb = pi_ * 2 + j % 2
                    for c in range(8):
                        mm(P[pb][:, :], w[:, c, woff:woff + 128], hT_nat(c, 512 * j, 512), c == 0, c == 7,
                           [("wH", hs), "hT"], ["ps%d" % pb], track=(c == 7))
                cols = slice(512 * j, 512 * (j + 1))
                pqk = "ps%d" % (j % 2)
                pq = P[j % 2]
                for dr in range(2):
                    pzb_ = (1 + dr) * 2 + j % 2
                    pz = P[pzb_]
                    pzk = "ps%d" % pzb_
                    K_ = lambda nm: (nm, dr)
                    act(su[dr], pz[:, :], AF.Exp, [pzk], [K_("u")])
                    act(sL1[dr], su[dr], AF.Ln, [K_("u")], [K_("L1")], bias=1.0)
                    act(sg_[dr], su[dr], AF.Ln, [K_("u")], [K_("g")], bias=lbv2[:, dr * 8 + h: dr * 8 + h + 1])
                    tt("pool", sg_[dr], sg_[dr], sL1[dr], ALU.subtract, [K_("g"), K_("L1")], [K_("g")])
                    k.op("dve", lambda e, o=sb_[dr], g_=sg_[dr]: e.tensor_tensor_scan(out=o, data0=scanmask, data1=g_, initial=0.0,
                                                                                op0=ALU.mult, op1=ALU.add),
                         [K_("g"), "const"], [K_("b")])
                    if dr == 0:
                        src = sb_[dr]
                        srck = K_("b")
                        dcol = slice(127, 512, 128)
                    else:
                        tt("pool", sr[dr], sg_[dr], sb_[dr], ALU.subtract, [K_("g"), K_("b")], [K_("r")])
                        bbv = sb_[dr].rearrange("p (c t) -> p c t", t=128)
                        rrv = sr[dr].rearrange("p (c t) -> p c t", t=128)
                        tt("dve", rrv, rrv, bbv[:, :, 127:128].to_broadcast([128, 4, 128]), ALU.add, [K_("r"), K_("b")], [K_("r")])
                        src = sr[dr]
                        srck = K_("r")
                        dcol = slice(0, 512, 128)
                    act(se[dr], src, AF.Exp, [srck], [K_("e")])
                    tt("dve", qtT[dr][:, cols], pq[:, :], se[dr], ALU.mult, [pqk, K_("e")], [("qtT", dr, j)])
                    cp("pool", dec[dr][:, 4 * j:4 * j + 4], se[dr][:, dcol], [K_("e")], [("dec", dr)])
                    tt("pool", sL1[dr], sL1[dr], src, ALU.add, [K_("L1"), srck], [K_("L1")])
                    act(su[dr], sL1[dr], AF.Exp, [K_("L1")], [K_("u")], scale=-1.0)
                    ts("dve", ktT[dr][:, cols], su[dr], oml[:, dr * 8 + h: dr * 8 + h + 1], ALU.mult,
                       [K_("u"), "oml"], [("ktT", dr, j)])

            def gate_back(j):
                cols = slice(512 * j, 512 * (j + 1))
                for dr in range(2):
                    for cc in range(4):
                        ch = 4 * j + cc
                        tr(PT[:, (dr * 4 + cc) * 128:(dr * 4 + cc + 1) * 128], ktT[dr][:, ch * 128:(ch + 1) * 128],
                           [("ktT", dr, j)], ["PT"], track=(cc == 3 and dr == 1))
                ev = "act" if j % 2 == 0 else "dve"
                cp(ev, ktok[0][:, cols], PT[:, 0:512], ["PT"], [("ktok", 0, j)])
                cp(ev, ktok[1][:, cols], PT[:, 512:1024], ["PT"], [("ktok", 1, j)])
            gf = gate_front if DBG.get("wave", 1) else gate_front_old
            gf(0)
            for j in range(8):
                if j + 1 < 8:
                    gf(j + 1)
                gate_back(j)
            k.op("pool", lambda e: e.memset(Ubf[0][:, 0:128], 0.0), [("vH", 0)], [("U", 0, 0)])
            memset("pool", Ubf[1][:, 31 * 128:32 * 128], 0.0, [("U", 1, 31)])
            pslot = 0
            for step in range(32):
                for dr in range(2):
                    c = step if dr == 0 else 31 - step
                    pb = pslot % 4
                    pslot += 1
                    pP = P[pb][:, 0:128]
                    ppk = "ps%d" % pb
                    mm(pP, ktok[dr][:, c * 128:(c + 1) * 128], vH[:, c * 128:(c + 1) * 128], True, True,
                       [("ktok", dr, c // 4), ("vH", c // 8)], [ppk])
                    Wn = Wst[dr][step % 2]
                    Wo = Wst[dr][(step + 1) % 2]
                    if step == 0:
                        cp("dve", Wn, pP, [ppk], [("W", dr, step % 2)])
                    else:
                        cprev = c - 1 if dr == 0 else c + 1
                        dcl = dec[dr][:, cprev:cprev + 1]
                        if dr == 0:
                            k.op("act", lambda e, o=Ubf[dr][:, c * 128:(c + 1) * 128], i_=Wo, s_=dcl:
                                 e.activation(out=o, in_=i_, func=AF.Copy, scale=s_),
                                 [("W", dr, (step + 1) % 2), ("dec", dr)], [("U", dr, c)])
                        else:
                            ts(DBG.get("ueng", "dve"), Ubf[dr][:, c * 128:(c + 1) * 128], Wo, dcl, ALU.mult,
                               [("W", dr, (step + 1) % 2), ("dec", dr)], [("U", dr, c)])
                        if step < 31:
                            stt(Wn, Wo, dcl, pP, ALU.mult, ALU.add,
                                [("W", dr, (step + 1) % 2), ("dec", dr), ppk], [("W", dr, step % 2)])
            def out_front(c):
                j = c // 4
                ch = slice(c * 128, (c + 1) * 128)
                pA = P[1 + c % 3][:, 0:256]
                pAk = "ps%d" % (1 + c % 3)
                for dr in range(2):
                    mm(pA[:, dr * 128:(dr + 1) * 128], ktT[dr][:, ch], qtT[dr][:, ch], True, True,
                       [("ktT", dr, j), ("qtT", dr, j)], [pAk], track=(dr == 1))
                tt("dve", AT[c % 3].rearrange("p (a b) -> p a b", b=128), pA.rearrange("p (a b) -> p a b", b=128),
                   triH, ALU.mult, [pAk, "const"], [("AT", c % 3)])

            def out_back(c):
                j = c // 4
                cc = c % 4
                ch = slice(c * 128, (c + 1) * 128)
                pob = 4 + j % 2
                po = P[pob]
                pok = "ps%d" % pob
                oc = po[:, cc * 128:(cc + 1) * 128]
                mm(oc, vH[:, ch], AT[c % 3][:, 0:128], True, False, [("vH", c // 8), ("AT", c % 3)], [pok], track=False)
                mm(oc, vH[:, ch], AT[c % 3][:, 128:256], False, False, [("vH", c // 8), ("AT", c % 3)], [pok], track=False)
                mm(oc, Ubf[0][:, ch], qtT[0][:, ch], False, False, [("U", 0, c), ("qtT", 0, j)], [pok], track=False)
                mm(oc, Ubf[1][:, ch], qtT[1][:, ch], False, True, [("U", 1, c), ("qtT", 1, j)], [pok], track=True)
                if cc == 3:
                    cols = slice(512 * j, 512 * (j + 1))
                    act(sqh, po[:, :], AF.Square, [pok], ["sqh"])
                    mm(P[6][:, :], ones128, sqh, True, True, ["sqh", "const"], ["ps6"])
                    act(lnh, P[6][:, :], AF.Ln, ["ps6"], ["lnh"], scale=1.0 / 128, bias=EPS)
                    act(lnh, lnh, AF.Exp, ["lnh"], ["lnh"], scale=-0.5)
                    stt(on, po[:, :], hgw[:, 0:1], lnh, ALU.mult, ALU.mult, [pok, "lnh", "hgw"], ["lnh"])
                    tt("pool", obS[j % 2], on, gbT[:, cols], ALU.mult, ["lnh", ("gbT", j)], [("obS", j % 2)])
                    k.dma("sp", ob_d[h * 128:(h + 1) * 128, cols], obS[j % 2], reads=[("obS", j % 2)], sem="d_obS%d" % (j % 2))
            out_front(0)
            out_front(1)
            for c in range(32):
                if c + 2 < 32:
                    out_front(c + 2)
                out_back(c)
            k.barrier()
        k.barrier()
        A.reset(m0)

    if 4 in phases:
        m0 = A.mark()
        wZ = A.alloc([128, 8, 2048], BF16)
        wpa = A.alloc([128, 4, 1024], BF16)
        wpb = A.alloc([128, 8, 1024], BF16)
        wo = A.alloc([128, 8, 1024], BF16)
        oaj = [A.alloc([128, 4, 512], BF16) for _ in range(2)]
        obj = [A.alloc([128, 8, 512], BF16) for _ in range(2)]
        mg_ = A.alloc([128, 8, 512], BF16)
        sa = [A.alloc([128, 512], F32) for _ in range(2)]
        sb2 = [A.alloc([128, 512], F32) for _ in range(2)]
        ma = [A.alloc([128, 512], F32) for _ in range(2)]
        mb = [A.alloc([128, 512], F32) for _ in range(2)]
        xr = [A.alloc([128, D], F32) for _ in range(3)]
        for seg in range(4):
            k.dma("pool", flat(wZ[:, 2 * seg:2 * seg + 2, :]), wz_d[:, seg * 4096:(seg + 1) * 4096], writes=["wZ"], sem="d_wZ")
        k.dma("pool", flat(wpa), wpa_d, writes=["wpa"])
        for hh in range(2):
            k.dma("pool", flat(wpb[:, hh * 4:(hh + 1) * 4, :]), wpb_d[:, hh * 4096:(hh + 1) * 4096], writes=["wpb"], sem="d_wpb")
            k.dma("pool", flat(wo[:, hh * 4:(hh + 1) * 4, :]), wo_d[:, hh * 4096:(hh + 1) * 4096], writes=["wo"], sem="d_wo")
        oa_v = oa_d.rearrange("(c p) t -> p c t", p=128)
        ob_v = ob_d.rearrange("(c p) t -> p c t", p=128)

        def load_ab(j_):
            cols_ = slice(512 * j_, 512 * (j_ + 1))
            k.dma("sp", oaj[j_ % 2], oa_v[:, :, cols_], writes=[("oaj", j_ % 2)])
            k.dma("sp", obj[j_ % 2], ob_v[:, :, cols_], writes=[("obj", j_ % 2)])
        load_ab(0)
        xcnt = 0
        ocnt = 0
        for j in range(8):
            sl = j % 2
            if j + 1 < 8:
                load_ab(j + 1)
            for dc in range(8):
                dcs = slice(dc * 128, (dc + 1) * 128)
                d2 = dc % 2
                pya, pyb = P[d2], P[2 + d2]
                pyak, pybk = "ps%d" % d2, "ps%d" % (2 + d2)
                for c in range(8):
                    mm(P[4][:, :], wZ[:, c, dcs], hT_nat(c, 512 * j, 512), c == 0, c == 7, ["wZ", "hT"], ["ps4"], track=(c == 7))
                for c in range(8):
                    mm(P[5][:, :], wZ[:, c, 1024 + dc * 128:1024 + (dc + 1) * 128], hT_nat(c, 512 * j, 512), c == 0, c == 7,
                       ["wZ", "hT"], ["ps5"], track=(c == 7))
                for c in range(4):
                    mm(pya[:, :], wpa[:, c, dcs], oaj[sl][:, c, :], c == 0, c == 3, ["wpa", ("oaj", sl)], [pyak], track=(c == 3))
                for c in range(8):
                    mm(pyb[:, :], wpb[:, c, dcs], obj[sl][:, c, :], c == 0, c == 7, ["wpb", ("obj", sl)], [pybk], track=(c == 7))
                act(sa[d2], P[4][:, :], AF.Sigmoid, ["ps4"], [("sa", d2)])
                act(sb2[d2], P[5][:, :], AF.Sigmoid, ["ps5"], [("sb2", d2)])
                tt("dve", ma[d2], pya[:, :], sa[d2], ALU.mult, [pyak, ("sa", d2)], [("ma", d2)])
                tt("dve", mb[d2], pyb[:, :], sb2[d2], ALU.mult, [pybk, ("sb2", d2)], [("mb", d2)])
                tt("pool", mg_[:, dc, :], ma[d2], mb[d2], ALU.add, [("ma", d2), ("mb", d2)], [("mg", dc)])
            for tq in range(4):
                t = 4 * j + tq
                xs = xcnt % 3
                xcnt += 1
                k.dma("pool", xr[xs], x_d[t * 128:(t + 1) * 128, :], writes=[("xr", xs)])
                for half in range(2):
                    pb = 4 + ocnt % 3
                    ocnt += 1
                    for c in range(8):
                        mm(P[pb][:, :], mg_[:, c, tq * 128:(tq + 1) * 128], wo[:, c, half * 512:(half + 1) * 512],
                           c == 0, c == 7, [("mg", c), "wo"], ["ps%d" % pb], track=(c == 7))
                    tt("dve", xr[xs][:, half * 512:(half + 1) * 512], P[pb][:, :], xr[xs][:, half * 512:(half + 1) * 512],
                       ALU.add, ["ps%d" % pb, ("xr", xs)], [("xr", xs)])
                k.dma("sp", out_d[t * 128:(t + 1) * 128, :], xr[xs], reads=[("xr", xs)], sem="d_ot%d" % xs)
    k.finish("sp")
    print("instructions:", k.ninstr, "arena peak:", A.peak, {e: k.count[e] for e in ENGS})
    k.emit()
    k.close()
    es.close()
    return nc


def _bias_tiles(rel_bias):
    a = np.arange(128)[:, None]
    c = np.arange(128)[None, :]
    out = np.zeros((128, 3, 8, 2, 128), np.float32)
    for g, (_, d) in enumerate(GROUPS):
        for ab in range(2):
            rel = (a - 64 - c) if ab == 0 else (a + 64 - c)
            rel = np.clip(rel, -64, 64)
            idx = t5_bucket(rel * d)
            for h in range(8):
                out[:, g, h, ab, :] = rel_bias[idx, g * 8 + h]
    return out.reshape(128, -1)


_NC_CACHE = {}


def kernel(x, norm_w, w_in, q_norm_w, k_norm_w, rel_bias, lb_fwd, lb_bwd, hg_norm_w,
           w_proj_a, w_proj_b, w_out, _debug=False, _phases=(1, 2, 3, 4)):
    x = np.asarray(x, np.float32)
    B = x.shape[0]
    key = (_debug, tuple(_phases))
    if key not in _NC_CACHE:
        _NC_CACHE[key] = build_program(_debug, _phases)
    nc = _NC_CACHE[key]
    qkw = np.zeros((128, 6), np.float32)
    for g in range(3):
        qkw[:, 2 * g] = np.tile(np.asarray(q_norm_w, np.float32)[0, g], 2)
        qkw[:, 2 * g + 1] = np.tile(np.asarray(k_norm_w, np.float32)[0, g], 2)
    lbf = np.ascontiguousarray(np.asarray(lb_fwd, np.float32).reshape(2, 8, 128).transpose(2, 0, 1).reshape(128, 16))
    lbb = np.ascontiguousarray(np.asarray(lb_bwd, np.float32).reshape(2, 8, 128).transpose(2, 0, 1).reshape(128, 16))
    w_in0 = np.asarray(w_in, np.float32)[0]
    wv = np.ascontiguousarray(w_in0.reshape(8, 128, IN_COLS).transpose(1, 0, 2))
    ar = np.arange(128)
    w_att = np.empty((12, 128, 8, 384), np.float32)
    for hp in range(4):
        for g in range(3):
            cols = np.concatenate([g * 1536 + seg * 512 + hp * 128 + ar for seg in range(3)])
            w_att[hp * 3 + g] = wv[:, :, cols]
    w_h = np.empty((8, 128, 8, 640), np.float32)
    for h in range(8):
        cols = np.concatenate([5120 + seg * 1024 + h * 128 + ar for seg in range(5)])
        w_h[h] = wv[:, :, cols]

    def pmaj(wm):
        nchunk = wm.shape[0] // 128
        return np.ascontiguousarray(wm.reshape(nchunk, 128, wm.shape[1]).transpose(1, 0, 2).reshape(128, -1))
    common = {
        "w_att": w_att.reshape(12, 128, -1),
        "w_g": np.ascontiguousarray(wv[:, :, 4608:5120]).reshape(128, -1),
        "w_h": w_h.reshape(8, 128, -1),
        "w_z": np.ascontiguousarray(wv[:, :, 10240:12288]).reshape(128, -1),
        "norm_w": np.ascontiguousarray(np.asarray(norm_w, np.float32)[0:1]),
        "qkw": qkw,
        "biasT": _bias_tiles(np.asarray(rel_bias, np.float32)),
        "lbf": lbf, "lbb": lbb,
        "hgw": np.ascontiguousarray(np.asarray(hg_norm_w, np.float32)[0].reshape(128, 1)),
        "w_pa": pmaj(np.asarray(w_proj_a, np.float32)[0]),
        "w_pb": pmaj(np.asarray(w_proj_b, np.float32)[0]),
        "w_o": pmaj(np.asarray(w_out, np.float32)[0]),
    }
    in_maps = []
    for b in range(B):
        m = dict(common)
        m["x"] = np.ascontiguousarray(x[b])
        in_maps.append(m)
    res = run_bass_kernel_spmd(nc, in_maps, core_ids=list(range(B)))
    if _debug:
        return res.results
    return np.stack([np.asarray(r["out"], np.float32) for r in res.results], axis=0)
```
